# Optimizing a Trainium2 kernel written in Bass

```python
import jax, jax.numpy as jnp
from jax import lax
import numpy as np

D_MODEL = 2048
BATCH = 4
SEQ = 2048
DEPTH = 1
DEC_BATCH = 128
DEC_SEQ = 8
PAST_LEN = 16384
PAGE_SIZE = 128

D_MIX = D_MODEL
D_A = (3 * D_MIX) // 8
D_B = (3 * D_MIX) // 8
D_X = D_MIX - D_A - D_B
H_A = 4
HD_A = D_A // H_A
CHUNK_A = 128
H_B = 4
DV_B = D_B // H_B
DK_B = DV_B // 2
D_BK = H_B * DK_B
GATE_RANK = 16
GATE_TAU = 16.0
CHUNK_B = 64
H_X = 4
HD_X = D_X // H_X
N_MEM = 256
EPS = 1e-6
IN_WIDTHS = (D_A, D_A, D_A, D_BK, D_BK, D_B, GATE_RANK, D_B, D_X, D_X)
D_IN = sum(IN_WIDTHS)

kernel_name = "hybrid_chunkmlp_gla_memxattn_step"


def rmsnorm(x, w):
    xf = x.astype(jnp.float32)
    y = xf * lax.rsqrt(jnp.mean(xf * xf, axis=-1, keepdims=True) + EPS)
    return (y * w.astype(jnp.float32)).astype(x.dtype)


def split_points():
    return [int(c) for c in np.cumsum(IN_WIDTHS)[:-1]]


def spatial_gating(u, v, w_s, b_s):
    bn, t = u.shape[0], u.shape[1]
    ln = min(t, CHUNK_A)
    n = t // ln
    mask = jnp.tril(jnp.ones((ln, ln), dtype=bool))
    w = jnp.where(mask, w_s[:, :ln, :ln], 0)
    vc = v.reshape(bn, n, ln, H_A, HD_A)
    mixed = jnp.einsum('hij,bnjhd->bnihd', w, vc) + b_s[:, :ln].T[None, None, :, :, None]
    return u * mixed.reshape(bn, t, H_A, HD_A)


def gla(q, k, v, log_a, s0):
    f32 = jnp.float32
    bn, t, h, dk = q.shape
    dv = v.shape[-1]
    c = min(t, CHUNK_B)
    n = t // c
    qc = q.astype(f32).reshape(bn, n, c, h, dk) * (dk ** -0.5)
    kc = k.astype(f32).reshape(bn, n, c, h, dk)
    vc = v.astype(f32).reshape(bn, n, c, h, dv)
    cum = jnp.cumsum(log_a.astype(f32).reshape(bn, n, c, h, dk), axis=2)
    q_in = qc * jnp.exp(cum)
    k_in = kc * jnp.exp(-cum)
    k_out = kc * jnp.exp(cum[:, :, -1:] - cum)
    decay = jnp.exp(cum[:, :, -1])
    mask = jnp.tril(jnp.ones((c, c), dtype=bool))
    scores = jnp.where(mask, jnp.einsum('bnihd,bnjhd->bnhij', q_in, k_in), 0.0)
    o_intra = jnp.einsum('bnhij,bnjhe->bnihe', scores, vc)

    def step(s, inp):
        q_i, k_o, v_i, dec = inp
        o = jnp.einsum('bihd,bhde->bihe', q_i, s)
        s = dec[..., None] * s + jnp.einsum('bjhd,bjhe->bhde', k_o, v_i)
        return s, o

    xs = (jnp.moveaxis(q_in, 1, 0), jnp.moveaxis(k_out, 1, 0),
          jnp.moveaxis(vc, 1, 0), jnp.moveaxis(decay, 1, 0))
    s_fin, o_inter = lax.scan(step, s0.astype(f32), xs)
    o = o_intra + jnp.moveaxis(o_inter, 0, 1)
    return o.reshape(bn, t, h, dv), s_fin


def memory_kv(mem, mem_norm_w, w_mem_kv):
    m = rmsnorm(mem, mem_norm_w) @ w_mem_kv
    mk, mv = jnp.split(m, 2, axis=-1)
    bn = mem.shape[0]
    return mk.reshape(bn, N_MEM, H_X, HD_X), mv.reshape(bn, N_MEM, H_X, HD_X)


def memory_attention(q, mk, mv):
    s = jnp.einsum('bthd,bnhd->bhtn', q, mk).astype(jnp.float32) * (HD_X ** -0.5)
    p = jax.nn.softmax(s, axis=-1)
    return jnp.einsum('bhtn,bnhd->bthd', p.astype(mv.dtype), mv)


def mixer_layer(x, mk, mv, s0, norm_w, w_in, a_vnorm_w, a_ws, a_bs,
                b_wa, b_ba, b_onorm_w, w_out):
    bn, t, _ = x.shape
    h = rmsnorm(x, norm_w)
    proj = h @ w_in
    a_u, a_v, a_g, b_q, b_k, b_v, b_r, b_g, x_q, x_g = jnp.split(proj, split_points(), axis=-1)
    a_vn = rmsnorm(a_v, a_vnorm_w)
    a_o = spatial_gating(a_u.reshape(bn, t, H_A, HD_A), a_vn.reshape(bn, t, H_A, HD_A),
                         a_ws, a_bs).reshape(bn, t, D_A)
    log_a = jax.nn.log_sigmoid((b_r @ b_wa + b_ba).astype(jnp.float32)) / GATE_TAU
    o_b, s_new = gla(b_q.reshape(bn, t, H_B, DK_B), b_k.reshape(bn, t, H_B, DK_B),
                     b_v.reshape(bn, t, H_B, DV_B), log_a.reshape(bn, t, H_B, DK_B), s0)
    o_b = rmsnorm(o_b, b_onorm_w).astype(x.dtype).reshape(bn, t, D_B)
    o_x = memory_attention(x_q.reshape(bn, t, H_X, HD_X), mk, mv).reshape(bn, t, D_X)
    branches = jnp.concatenate([a_o * jax.nn.silu(a_g),
                                o_b * jax.nn.silu(b_g),
                                o_x * jax.nn.silu(x_g)], axis=-1)
    y = x + branches @ w_out
    return y, a_vn, s_new.astype(x.dtype)


def setup_inputs(seed: int = 0) -> dict:
    key = jax.random.key(seed)
    ks = jax.random.split(key, 20)
    f32 = jnp.float32
    nrm = lambda k, s: jax.random.normal(k, s, f32)
    return {
        "x_prompt": nrm(ks[0], (BATCH, SEQ, D_MODEL)),
        "x_sample": nrm(ks[1], (DEC_BATCH, DEC_SEQ, D_MODEL)),
        "mem_prompt": nrm(ks[2], (BATCH, N_MEM, D_MODEL)),
        "state_gla": nrm(ks[3], (DEPTH, DEC_BATCH, H_B, DK_B, DV_B)),
        "cache_mem_k": nrm(ks[4], (DEPTH, DEC_BATCH, N_MEM, H_X, HD_X)),
        "cache_mem_v": nrm(ks[5], (DEPTH, DEC_BATCH, N_MEM, H_X, HD_X)),
        "norm_w": 1.0 + 0.02 * nrm(ks[6], (DEPTH, D_MODEL)),
        "w_in": nrm(ks[7], (DEPTH, D_MODEL, D_IN)) * D_MODEL ** -0.5,
        "a_vnorm_w": 1.0 + 0.02 * nrm(ks[8], (DEPTH, D_A)),
        "a_ws": nrm(ks[9], (DEPTH, H_A, CHUNK_A, CHUNK_A)) * CHUNK_A ** -0.5,
        "a_bs": 0.02 * nrm(ks[10], (DEPTH, H_A, CHUNK_A)),
        "b_wa": nrm(ks[11], (DEPTH, GATE_RANK, D_BK)) * GATE_RANK ** -0.5,
        "b_ba": 0.02 * nrm(ks[12], (DEPTH, D_BK)),
        "b_onorm_w": 1.0 + 0.02 * nrm(ks[13], (DEPTH, DV_B)),
        "mem_norm_w": 1.0 + 0.02 * nrm(ks[14], (DEPTH, D_MODEL)),
        "w_mem_kv": nrm(ks[15], (DEPTH, D_MODEL, 2 * D_X)) * D_MODEL ** -0.5,
        "w_out": nrm(ks[16], (DEPTH, D_MIX, D_MODEL)) * D_MIX ** -0.5,
        "final_norm_w": 1.0 + 0.02 * nrm(ks[17], (D_MODEL,)),
    }


def reference(x_prompt, x_sample, mem_prompt, state_gla, cache_mem_k, cache_mem_v,
              norm_w, w_in, a_vnorm_w, a_ws, a_bs, b_wa, b_ba, b_onorm_w,
              mem_norm_w, w_mem_kv, w_out, final_norm_w):
    bp = x_prompt.shape[0]
    hp, hs = x_prompt, x_sample
    mem_k_p, mem_v_p, st_p, st_s, av_s = [], [], [], [], []
    for l in range(DEPTH):
        layer_w = (norm_w[l], w_in[l], a_vnorm_w[l], a_ws[l], a_bs[l],
                   b_wa[l], b_ba[l], b_onorm_w[l], w_out[l])
        mk, mv = memory_kv(mem_prompt, mem_norm_w[l], w_mem_kv[l])
        s0 = jnp.zeros((bp, H_B, DK_B, DV_B), dtype=jnp.float32)
        hp, _, sp = mixer_layer(hp, mk, mv, s0, *layer_w)
        hs, avs, ss = mixer_layer(hs, cache_mem_k[l], cache_mem_v[l], state_gla[l], *layer_w)
        mem_k_p.append(mk)
        mem_v_p.append(mv)
        st_p.append(sp)
        st_s.append(ss)
        av_s.append(avs)
    y_prompt = rmsnorm(hp, final_norm_w)
    y_sample = rmsnorm(hs, final_norm_w)
    return (y_prompt, y_sample, jnp.stack(mem_k_p), jnp.stack(mem_v_p),
            jnp.stack(st_p), jnp.stack(st_s), jnp.stack(av_s))
```

```python
import os
import numpy as np
from contextlib import ExitStack
import concourse.bass as bass
import concourse.mybir as mybir
from concourse.bass_utils import run_bass_kernel_spmd

F32 = mybir.dt.float32
BF16 = mybir.dt.bfloat16
AF = mybir.ActivationFunctionType
ALU = mybir.AluOpType

D = 2048
DIN = 5648
EPS = 1e-6
NPRE = 1024
NMAIN = 1152
C_AU, C_AV, C_AG = 0, 768, 1536
C_BQ, C_BK, C_BV, C_BR, C_BG = 2304, 2688, 3072, 3840, 3856
C_XQ, C_XG = 4624, 5136
WS = 384
NSLOT = 2


class KB:
    def __init__(self, nc, es):
        self.nc = nc
        self.es = es
        self.eng = {"pe": nc.tensor, "act": nc.scalar, "dve": nc.vector, "pool": nc.gpsimd, "sp": nc.sync}
        self.sems = {}
        self.cnt = {}
        for n in self.eng:
            self.sems["e_" + n] = es.enter_context(nc.semaphore("e_" + n))
            self.cnt["e_" + n] = 0
        self.seen = {n: {} for n in self.eng}
        self.lastw = {}
        self.readers = {}
        self.semkeys = {}
        self.groupbase = {}

    def _wait(self, E, ev):
        if ev is None:
            return
        sn, val = ev
        if sn == "e_" + E and E == "pe":
            return
        if sn.startswith("d_"):
            val = max(val, self.cnt[sn])
        if self.seen[E].get(sn, 0) >= val:
            return
        self.eng[E].wait_ge(self.sems[sn], val)
        self.seen[E][sn] = val

    def _deps(self, E, reads, writes, skip_waw_sem=None):
        for r in reads:
            self._wait(E, self.lastw.get(r))
            if len(r) == 2 and r[0] in "PT" and r[1].isdigit():
                for ev in self.readers.get(r, ()):
                    if ev[0] != "e_" + E:
                        self._wait(E, ev)
        for w in writes:
            lw = self.lastw.get(w)
            if lw is not None and skip_waw_sem is not None and lw[0] == skip_waw_sem:
                self._wait(E, self.groupbase.get(w))
            else:
                self._wait(E, lw)
                if skip_waw_sem is None and E in ("sp", "pool", "act"):
                    self.groupbase[w] = lw
            for ev in self.readers.get(w, ()):
                self._wait(E, ev)

    def _record(self, ev, reads, writes):
        for w in writes:
            self.lastw[w] = ev
            self.readers[w] = []
        for r in reads:
            self.readers.setdefault(r, []).append(ev)

    def op(self, E, fn, reads=(), writes=(), signal=True):
        self._deps(E, reads, writes)
        ins = fn(self.eng[E])
        sn = "e_" + E
        if signal:
            self.cnt[sn] += 1
            ins.then_inc(self.sems[sn], 1)
            ev = (sn, self.cnt[sn])
        else:
            ev = (sn, self.cnt[sn] + 1)
        self._record(ev, reads, writes)
        return ins

    def dma(self, Q, out, in_, reads=(), writes=(), semkey=None, part=False, **kw):
        key = semkey or (writes[0] if writes else reads[0])
        sn = "d_" + str(key)
        if sn not in self.sems:
            self.sems[sn] = self.es.enter_context(self.nc.semaphore(sn.replace("(", "_").replace(")", "_").replace(",", "_").replace(" ", "").replace("'", "")))
            self.cnt[sn] = 0
        self._deps(Q, reads, writes, skip_waw_sem=sn if part else None)
        ins = self.eng[Q].dma_start(out=out, in_=in_, **kw)
        self.cnt[sn] += 16
        ins.then_inc(self.sems[sn], 16)
        ev = (sn, self.cnt[sn])
        self._record(ev, reads, writes)
        self.semkeys.setdefault(sn, set()).update(writes)
        return ins

    def seal(self, semkey):
        sn = "d_" + str(semkey)
        for k in self.semkeys.get(sn, ()):
            if self.lastw.get(k, (None,))[0] == sn:
                self.lastw[k] = (sn, self.cnt[sn])

    def barrier(self):
        for E in self.eng:
            for sn, c in self.cnt.items():
                if c > 0:
                    self._wait(E, (sn, c))


def build_program(stop=None):
    nc = bass.Bass("TRN2", target_bir_lowering=False)
    dr = {}

    def din(name, shape):
        dr[name] = nc.dram_tensor(name, list(shape), F32, kind="ExternalInput").ap()

    def dout(name, shape):
        dr[name] = nc.dram_tensor(name, list(shape), F32, kind="ExternalOutput").ap()

    din("xp", [1024, D]); din("xpre", [1024, D]); din("xs", [128, D]); din("mem", [256, D])
    din("s0", [16, 4, 96, 192]); din("ck", [16, 256, 512]); din("cv", [16, 256, 512])
    din("norm_w", [1, D]); din("w_in", [D, DIN]); din("a_vnorm_w", [1, 768]); din("a_ws", [4, 128, 128])
    din("a_bs", [4, 128]); din("b_wa", [16, 384]); din("b_ba", [1, 384]); din("b_onorm_w", [1, 192])
    din("mem_norm_w", [1, D]); din("w_mem_kv", [D, 1024]); din("w_out", [D, D]); din("final_norm_w", [1, D])
    dout("yp", [1024, D]); dout("ys", [128, D]); dout("mk", [256, 512]); dout("mv", [256, 512])
    dout("sp", [4, 96, 192]); dout("ss", [16, 4, 96, 192]); dout("cvs", [128, 768])

    win_v = dr["w_in"].rearrange("(k p) c -> p k c", p=128)
    wkv_v = dr["w_mem_kv"].rearrange("(k p) c -> p k c", p=128)
    wout_v = dr["w_out"].rearrange("(k p) c -> p k c", p=128)

    with ExitStack() as es:
        K = KB(nc, es)

        uid = [0]

        def sb(stack, name, shape, dt):
            uid[0] += 1
            return stack.enter_context(nc.sbuf_tensor(f"{name}_{uid[0]}", list(shape), dt))

        P = [es.enter_context(nc.psum_tensor(f"P{i}", [128, 512], F32)) for i in range(6)]
        T = [es.enter_context(nc.psum_tensor(f"T{i}", [128, 1024], BF16)) for i in range(2)]
        PK = [f"P{i}" for i in range(6)]
        TK = [f"T{i}" for i in range(2)]

        ones_f = sb(es, "ones_f", [128, 512], F32)
        ident_f = sb(es, "ident_f", [128, 128], F32)
        ident = sb(es, "ident", [128, 128], BF16)
        maskT = sb(es, "maskT", [128, 128], F32)
        maskS = sb(es, "maskS", [128, 128], F32)
        ones_b = sb(es, "ones_b", [128, 128], BF16)
        blockmask = sb(es, "blockmask", [128, 16], F32)
        normw_col = sb(es, "normw_col", [128, 16], F32)
        memw_col = sb(es, "memw_col", [128, 16], F32)
        wo_b = sb(es, "wo_b", [128, 192], F32)
        abT = sb(es, "abT", [128, 4], F32)
        abTs = sb(es, "abTs", [128, 4], F32)
        WT = sb(es, "WT", [128, 4, 128], BF16)
        WTs = sb(es, "WTs", [128, 4, 128], BF16)
        bwa = sb(es, "bwa", [17, 384], F32)
        Sst = sb(es, "Sst", [96, 4, 192], F32)
        KT = sb(es, "KT", [128, 4, 256], BF16)
        Vb = sb(es, "Vb", [128, 2, 512], BF16)
        branchT = sb(es, "branchT", [128, 16, NMAIN], BF16)
        wsl = [sb(es, f"wsl{i}", [128, 16, WS], BF16) for i in range(NSLOT)]
        st = [sb(es, f"st{i}", [128, 8], F32) for i in range(4)]
        mhalf = sb(es, "mhalf", [128, 8], F32)

        K.op("pool", lambda e: e.memset(ones_f[:], 1.0), writes=["ones_f"])
        K.op("pool", lambda e: e.memset(ones_b[:], 1.0), writes=["ones_b"])
        K.op("pool", lambda e: e.memset(mhalf[:], -0.5), writes=["mhalf"])
        K.op("pool", lambda e: e.memset(Sst[:], 0.0), writes=["Sst0", "Sst1", "Sst2", "Sst3"])
        K.op("pool", lambda e: e.affine_select(out=ident_f[:], in_=ones_f[:, 0:128], pattern=[[-1, 128]],
                                               compare_op=ALU.is_equal, fill=0.0, base=0, channel_multiplier=1),
             reads=["ones_f"], writes=["ident_f"])
        K.op("pool", lambda e: e.tensor_copy(out=ident[:], in_=ident_f[:]), reads=["ident_f"], writes=["ident"])
        K.op("pool", lambda e: e.affine_select(out=maskT[:], in_=ones_f[:, 0:128], pattern=[[1, 128]],
                                               compare_op=ALU.is_ge, fill=0.0, base=0, channel_multiplier=-1),
             reads=["ones_f"], writes=["maskT"])
        K.op("pool", lambda e: e.affine_select(out=maskS[:], in_=maskT[:], pattern=[[-8, 16], [0, 8]],
                                               compare_op=ALU.is_ge, fill=0.0, base=0, channel_multiplier=1),
             reads=["maskT"], writes=["maskS"])
        K.op("pool", lambda e: e.affine_select(out=maskS[:], in_=maskS[:], pattern=[[8, 16], [0, 8]],
                                               compare_op=ALU.is_ge, fill=0.0, base=7, channel_multiplier=-1),
             reads=["maskS"], writes=["maskS"])
        K.op("pool", lambda e: e.affine_select(out=blockmask[:], in_=ones_f[:, 0:16], pattern=[[-8, 16]],
                                               compare_op=ALU.is_ge, fill=0.0, base=0, channel_multiplier=1),
             reads=["ones_f"], writes=["blockmask"])
        K.op("pool", lambda e: e.affine_select(out=blockmask[:], in_=blockmask[:], pattern=[[8, 16]],
                                               compare_op=ALU.is_ge, fill=0.0, base=7, channel_multiplier=-1),
             reads=["blockmask"], writes=["blockmask"])

        if stop == 'c1':
            K.barrier()
            return nc
        jobs = []
        wstate = {"issued": 0, "taken": 0, "done": 0}

        def wissue_upto(j):
            while wstate["issued"] <= min(j, len(jobs) - 1):
                i = wstate["issued"]
                view, c0, ncols = jobs[i]
                s = i % NSLOT
                K.dma("pool", wsl[s][:, :, 0:ncols], view[:, :, c0:c0 + ncols], writes=[f"wsl{s}"])
                wstate["issued"] += 1

        def wget(n=1, keep=0):
            j = wstate["taken"]
            wstate["done"] = j - keep
            wissue_upto(j + n - 1)
            wstate["taken"] += n
            res = [(wsl[(j + i) % NSLOT], f"wsl{(j + i) % NSLOT}") for i in range(n)]
            return res if n > 1 else res[0]

        def wdone():
            wstate["done"] = wstate["taken"]

        def wprefetch():
            wissue_upto(wstate["done"] + NSLOT - 1)

        def J(view, c0, n):
            jobs.append((view, c0, n))

        J(win_v, C_BR, 16); J(win_v, C_BK, 384); J(win_v, C_BV, 384); J(win_v, C_BV + 384, 384)
        J(wkv_v, 0, 256); J(wkv_v, 256, 256); J(wkv_v, 512, 256); J(wkv_v, 768, 256)
        J(win_v, C_BR, 16); J(win_v, C_BQ, 384); J(win_v, C_BK, 384); J(win_v, C_BV, 384); J(win_v, C_BV + 384, 384)
        J(win_v, C_BG, 384); J(win_v, C_BG + 384, 384)
        J(win_v, C_AV, 384); J(win_v, C_AV + 384, 384); J(win_v, C_AU, 384); J(win_v, C_AU + 384, 384)
        J(win_v, C_AG, 384); J(win_v, C_AG + 384, 384)
        J(win_v, C_XQ, 256); J(win_v, C_XQ + 256, 256); J(win_v, C_XG, 256); J(win_v, C_XG + 256, 256)

        def proj_tok(hT, hTk, tok0, wslot, wk, ncols, ps, psk):
            for k in range(16):
                K.op("pe", lambda e, k=k: e.matmul(ps[:, 0:ncols], lhsT=hT[:, k, tok0:tok0 + 128], rhs=wslot[:, k, 0:ncols],
                                                   start=(k == 0), stop=(k == 15)),
                     reads=[hTk, wk], writes=[psk], signal=(k == 15))

        def proj_feat(hT, hTk, tok0, ntok, wslot, wk, c0, M, ps, psk):
            for k in range(16):
                K.op("pe", lambda e, k=k: e.matmul(ps[0:M, 0:ntok], lhsT=wslot[:, k, c0:c0 + M], rhs=hT[:, k, tok0:tok0 + ntok],
                                                   start=(k == 0), stop=(k == 15)),
                     reads=[hTk, wk], writes=[psk], signal=(k == 15))

        def rstd_from_ss(stt, stk, ncol_in, n, eng_sum=True):
            if ncol_in == 2:
                K.op("dve", lambda e: e.tensor_tensor(out=stt[:, 4:5], in0=stt[:, 0:1], in1=stt[:, 1:2], op=ALU.add),
                     reads=[stk], writes=[stk])
                src = stt[:, 4:5]
            else:
                src = stt[:, 0:1]
            K.op("act", lambda e: e.activation(out=stt[:, 5:6], in_=src, func=AF.Sqrt, scale=1.0 / n, bias=EPS),
                 reads=[stk], writes=[stk])
            K.op("dve", lambda e: e.reciprocal(out=stt[:, 6:7], in_=stt[:, 5:6]), reads=[stk], writes=[stk])

        def norm_gen(stack_bufs, src_list, dstT, dstk, wcol, wcolk):
            xsl, xsb_, jnk = stack_bufs
            nx, nbf = len(xsl), len(xsb_)
            tl = [(src, t, col0 + t * 128) for (src, ntiles, col0) in src_list for t in range(ntiles)]
            n = len(tl)

            def load(i):
                src, t, c0 = tl[i]
                K.dma("sp", xsl[i % nx][:], src[t * 128:(t + 1) * 128, :], writes=[f"xsl{i % nx}"])

            def front(i):
                s, sx, sq_ = i % 2, i % nx, i % nbf
                xk, bk, sk = f"xsl{sx}", f"xsb{sq_}", f"st{s}"
                jo, jk = (jnk[:], "jnk") if jnk is not None else (xsb_[sq_][:], bk)
                K.op("act", lambda e: e.activation(out=jo, in_=xsl[sx][:], func=AF.Square, accum_out=st[s][:, 0:1]),
                     reads=[xk], writes=[jk, sk])
                rstd_from_ss(st[s], sk, 1, float(D))
                K.op("pool", lambda e: e.tensor_scalar(out=xsb_[sq_][:], in0=xsl[sx][:], scalar1=st[s][:, 6:7], scalar2=1.0,
                                                       op0=ALU.mult, op1=ALU.mult),
                     reads=[xk, sk], writes=[bk])

            def transposes(i):
                sq_ = i % nbf
                for k in range(16):
                    tb = k // 8
                    K.op("pe", lambda e, k=k, tb=tb: e.transpose(out=T[tb][:, (k % 8) * 128:(k % 8 + 1) * 128],
                                                                 in_=xsb_[sq_][:, k * 128:(k + 1) * 128], identity=ident[:]),
                         reads=[f"xsb{sq_}", "ident"], writes=[TK[tb]], signal=(k % 8 == 7))

            def back(i):
                src, t, c0 = tl[i]
                for tb in range(2):
                    K.op("dve", lambda e, tb=tb: e.tensor_tensor(
                        out=dstT[:, tb * 8:(tb + 1) * 8, c0:c0 + 128], in0=T[tb][:].rearrange("p (k i) -> p k i", k=8),
                        in1=wcol[:, tb * 8:(tb + 1) * 8].unsqueeze(2).broadcast_to([128, 8, 128]), op=ALU.mult),
                         reads=[TK[tb], wcolk], writes=[dstk])

            for i in range(min(nx - 1, n)):
                load(i)
            front(0)
            yield
            transposes(0)
            yield
            for i in range(n):
                if i + 1 < n:
                    front(i + 1)
                back(i)
                if i + nx - 1 < n:
                    load(i + nx - 1)
                yield
                if i + 1 < n:
                    transposes(i + 1)
                    yield

        def transpose_to_branch(srcb, srck, nchunks, kbase, col0, eng="dve", tb=0):
            for kk in range(nchunks):
                K.op("pe", lambda e, kk=kk: e.transpose(out=T[tb][:, kk * 128:(kk + 1) * 128], in_=srcb[:, kk * 128:(kk + 1) * 128],
                                                        identity=ident[:]),
                     reads=[srck, "ident"], writes=[TK[tb]], signal=(kk == nchunks - 1))
            K.op(eng, lambda e: (e.tensor_copy if eng != "act" else e.copy)(
                out=branchT[:, kbase:kbase + nchunks, col0:col0 + 128],
                in_=T[tb][:, 0:nchunks * 128].rearrange("p (k i) -> p k i", k=nchunks)),
                 reads=[TK[tb]], writes=[f"bT{k}_{col0 // 128}" for k in range(kbase, kbase + nchunks)])

        def bkeys(k0, nk, tok0, ntok):
            return [f"bT{k}_{t}" for k in range(k0, k0 + nk) for t in range(tok0 // 128, (tok0 + ntok + 127) // 128)]

        def gate_feat(hT, hTk, blocks, wslot, wk, nchunks, kbase, sg, mode):
            i = 0
            for (b0, nb) in blocks:
                for kk in range(nchunks):
                    pi = i % 2
                    proj_feat(hT, hTk, b0, nb, wslot, wk, kk * 128, 128, P[pi], PK[pi])
                    dst = branchT[:, kbase + kk, b0:b0 + nb]
                    bk = bkeys(kbase + kk, 1, b0, nb)
                    if mode == "silu":
                        K.op("act", lambda e, pi=pi: e.activation(out=sg[pi][:, 0:nb], in_=P[pi][:, 0:nb], func=AF.Silu),
                             reads=[PK[pi]], writes=[f"sg{pi}"])
                        K.op("dve", lambda e, pi=pi, dst=dst: e.tensor_tensor(out=dst, in0=dst, in1=sg[pi][:, 0:nb], op=ALU.mult),
                             reads=[f"sg{pi}"] + bk, writes=bk)
                    else:
                        K.op("dve", lambda e, pi=pi, dst=dst: e.tensor_tensor(out=dst, in0=P[pi][:, 0:nb], in1=dst, op=ALU.mult),
                             reads=[PK[pi]] + bk, writes=bk)
                    i += 1
                    yield

        def gla_phase(hT, hTk, NT, tiles, with_out, side=None):
            blocks = []
            ptoks = sum(128 for k, _ in tiles if k == "p")
            for b0 in range(0, ptoks, 512):
                blocks.append((b0, min(512, ptoks - b0), "p"))
            for k, t0 in tiles:
                if k == "s":
                    blocks.append((t0, 128, "s"))
            with ExitStack() as gs:
                dec = sb(gs, "dec", [96, 4, 24], F32)
                dtmp = sb(gs, "dtmp", [96, 4, 16], F32)
                kinT = sb(gs, "kinT", [96, 4, NT], BF16)
                koutT = sb(gs, "koutT", [96, 4, NT], BF16)
                qinT = sb(gs, "qinT", [96, 4, NT], BF16) if with_out else None
                has_s = any(k == "s" for k, _ in tiles)
                with ExitStack() as gs1:
                    G = sb(gs1, "G", [96, 4, NT + 1], F32)
                    rT = sb(gs1, "rT", [17, NT], BF16)
                    bwab = sb(gs1, "bwab", [17, 384], BF16)
                    etmp = [sb(gs1, f"etmp{i}", [96, 512], F32) for i in range(2)]
                    ltmp = [sb(gs1, f"ltmp{i}", [96, 512], F32) for i in range(2)]
                    cwt, cet = etmp, ltmp
                    ex1 = [sb(gs1, f"ex1_{i}", [96, 512], F32) for i in range(2)]
                    ex2 = [sb(gs1, f"ex2_{i}", [96, 512], F32) for i in range(2)]
                    gte = [sb(gs1, f"gte{i}", [96, 512], F32) for i in range(2)]
                    gtl = [sb(gs1, f"gtl{i}", [96, 512], F32) for i in range(2)]
                    K.op("pool", lambda e: e.memset(rT[:], 1.0), writes=["rTm"])
                    K.op("pool", lambda e: e.tensor_copy(out=bwab[:], in_=bwa[:]), reads=["bwa"], writes=["bwab"])
                    K.op("pool", lambda e: e.memset(G[:, :, 0:1], 0.0), writes=["Gi"])
                    NPT_ = sum(1 for k, _ in tiles if k == "p")
                    gkeys = ["Gi"] + [f"G{bi}" for bi in range(len(blocks))]

                    def gates_gen(rslot, rk):
                        i = 0
                        for bi, (b0, nb, kind) in enumerate(blocks):
                            pr = 4 + bi % 2
                            proj_feat(hT, hTk, b0, nb, rslot, rk, 0, 16, P[pr], PK[pr])
                            K.op("act", lambda e, pr=pr, b0=b0, nb=nb: e.copy(out=rT[0:16, b0:b0 + nb], in_=P[pr][0:16, 0:nb]),
                                 reads=[PK[pr], "rTm"], writes=[f"rT{bi}"])
                        for bi, (b0, nb, kind) in enumerate(blocks):
                            for h in range(4):
                                pi = 2 + i % 2
                                ti = i % 2
                                K.op("pe", lambda e, pi=pi, h=h, b0=b0, nb=nb: e.matmul(P[pi][0:96, 0:nb], lhsT=bwab[0:17, h * 96:(h + 1) * 96],
                                                                                       rhs=rT[0:17, b0:b0 + nb], start=True, stop=True),
                                     reads=["bwab", "rTm", f"rT{bi}"], writes=[PK[pi]])
                                K.op("act", lambda e, pi=pi, ti=ti, nb=nb: e.activation(out=gte[ti][:, 0:nb], in_=P[pi][0:96, 0:nb], func=AF.Exp,
                                                                                       scale=-1.0),
                                     reads=[PK[pi]], writes=[f"gte{ti}"])
                                K.op("act", lambda e, ti=ti, nb=nb: e.activation(out=gtl[ti][:, 0:nb], in_=gte[ti][:, 0:nb], func=AF.Ln, bias=1.0),
                                     reads=[f"gte{ti}"], writes=[f"gtl{ti}"])
                                K.op("dve", lambda e, ti=ti, h=h, b0=b0, nb=nb: e.tensor_tensor_scan(
                                    out=G[:, h, 1 + b0:1 + b0 + nb], data0=ones_f[0:96, 0:nb], data1=gtl[ti][:, 0:nb],
                                    initial=G[:, h, b0:b0 + 1], op0=ALU.mult, op1=ALU.subtract),
                                     reads=[f"gtl{ti}", "ones_f", gkeys[bi]], writes=[gkeys[bi + 1]])
                                i += 1
                            yield
                        if NPT_:
                            K.op("dve", lambda e: e.tensor_tensor(out=dtmp[:, :, 0:NPT_], in0=G[:, :, 128:128 * NPT_ + 1:128],
                                                                  in1=G[:, :, 0:128 * (NPT_ - 1) + 1:128], op=ALU.subtract),
                                 reads=gkeys, writes=["dtmp"])
                            K.op("act", lambda e: e.activation(out=dec[:, :, 0:NPT_], in_=dtmp[:, :, 0:NPT_], func=AF.Exp, scale=1.0 / 16),
                                 reads=["dtmp"], writes=["dec"])
                        for ci, (kind, t0) in enumerate(tiles):
                            if kind == "s":
                                K.op("dve", lambda e, t0=t0: e.tensor_tensor(out=dtmp[:, :, 0:16], in0=G[:, :, t0 + 8:t0 + 129:8],
                                                                             in1=G[:, :, t0:t0 + 121:8], op=ALU.subtract),
                                     reads=gkeys, writes=["dtmp"])
                                K.op("act", lambda e: e.activation(out=dec[:, :, 8:24], in_=dtmp[:, :, 0:16], func=AF.Exp, scale=1.0 / 16),
                                     reads=["dtmp"], writes=["dec"])
                        yield

                    def rel_exps(h, bi, b0, nb, kind, ti, want):
                        gk = [gkeys[bi], gkeys[bi + 1]]
                        if kind == "p":
                            m = nb // 128
                            gin = G[:, h, 1 + b0:1 + b0 + nb].rearrange("p (m i) -> p m i", m=m)
                            gst = G[:, h, b0:b0 + nb].rearrange("p (m i) -> p m i", m=m)[:, :, 0:1].broadcast_to([96, m, 128])
                            gen = G[:, h, b0 + 128:b0 + nb + 1:128].unsqueeze(2).broadcast_to([96, m, 128])
                            shp = "p (m i) -> p m i"
                            kw = dict(m=m)
                        else:
                            gin = G[:, h, 1 + b0:1 + b0 + 128].rearrange("p (m i) -> p m i", m=16)
                            gst = G[:, h, b0:b0 + 128].rearrange("p (m i) -> p m i", m=16)[:, :, 0:1].broadcast_to([96, 16, 8])
                            gen = G[:, h, b0 + 8:b0 + 129:8].unsqueeze(2).broadcast_to([96, 16, 8])
                            shp = "p (m i) -> p m i"
                            kw = dict(m=16)
                        K.op("dve", lambda e: e.tensor_tensor(out=cwt[ti][:, 0:nb].rearrange(shp, **kw), in0=gin, in1=gst, op=ALU.subtract),
                             reads=gk, writes=[f"etmp{ti}"])
                        if want == "q":
                            K.op("act", lambda e: e.activation(out=ex1[ti][:, 0:nb], in_=cwt[ti][:, 0:nb], func=AF.Exp, scale=1.0 / 16),
                                 reads=[f"etmp{ti}"], writes=[f"ex1_{ti}"])
                        else:
                            K.op("dve", lambda e: e.tensor_tensor(out=cet[ti][:, 0:nb].rearrange(shp, **kw), in0=gen, in1=gin, op=ALU.subtract),
                                 reads=gk, writes=[f"ltmp{ti}"])
                            K.op("act", lambda e: e.activation(out=ex1[ti][:, 0:nb], in_=cwt[ti][:, 0:nb], func=AF.Exp, scale=-1.0 / 16),
                                 reads=[f"etmp{ti}"], writes=[f"ex1_{ti}"])
                            K.op("act", lambda e: e.activation(out=ex2[ti][:, 0:nb], in_=cet[ti][:, 0:nb], func=AF.Exp, scale=1.0 / 16),
                                 reads=[f"ltmp{ti}"], writes=[f"ex2_{ti}"])

                    def qk_gen(which, wslot, wk):
                        i = 0
                        for bi, (b0, nb, kind) in enumerate(blocks):
                            for h in range(4):
                                pi = i % 2
                                ti = i % 2
                                proj_feat(hT, hTk, b0, nb, wslot, wk, h * 96, 96, P[pi], PK[pi])
                                rel_exps(h, bi, b0, nb, kind, ti, which)
                                if which == "q":
                                    K.op("dve", lambda e, pi=pi, ti=ti, h=h, b0=b0, nb=nb: e.scalar_tensor_tensor(
                                        out=qinT[:, h, b0:b0 + nb], in0=P[pi][0:96, 0:nb], scalar=96.0 ** -0.5, in1=ex1[ti][:, 0:nb],
                                        op0=ALU.mult, op1=ALU.mult),
                                         reads=[PK[pi], f"ex1_{ti}"], writes=["qinT"])
                                else:
                                    K.op("dve", lambda e, pi=pi, ti=ti, h=h, b0=b0, nb=nb: e.tensor_tensor(
                                        out=kinT[:, h, b0:b0 + nb], in0=P[pi][0:96, 0:nb], in1=ex1[ti][:, 0:nb], op=ALU.mult),
                                         reads=[PK[pi], f"ex1_{ti}"], writes=["kinT"])
                                    K.op("dve", lambda e, pi=pi, ti=ti, h=h, b0=b0, nb=nb: e.tensor_tensor(
                                        out=koutT[:, h, b0:b0 + nb], in0=P[pi][0:96, 0:nb], in1=ex2[ti][:, 0:nb], op=ALU.mult),
                                         reads=[PK[pi], f"ex2_{ti}"], writes=["koutT"])
                                i += 1
                                yield

                    def run1(*specs):
                        specs = [[g, n] for (g, n) in specs]
                        while specs:
                            for sp_ in list(specs):
                                g, n = sp_
                                for _ in range(n):
                                    try:
                                        next(g)
                                    except StopIteration:
                                        specs.remove(sp_)
                                        break

                    rslot, rk = wget()
                    wprefetch()
                    gg = gates_gen(rslot, rk)
                    next(gg)
                    w1slot, w1k = wget(keep=1)
                    if side is not None:
                        run1((qk_gen("q" if with_out else "k", w1slot, w1k), 4), (gg, 1), (side, 1))
                    else:
                        run1((qk_gen("q" if with_out else "k", w1slot, w1k), 4), (gg, 1))
                    wdone()
                    wprefetch()
                    if with_out:
                        w2slot, w2k = wget()
                        wprefetch()
                        run1((qk_gen("k", w2slot, w2k), 1))

                    K.barrier()

                with ExitStack() as gs2:
                    NTL = len(tiles)
                    NPT = sum(1 for k, _ in tiles if k == "p")
                    pidx = {}
                    for ci_, (k_, _) in enumerate(tiles):
                        if k_ == "p":
                            pidx[ci_] = len(pidx)
                    vb_all = [sb(gs2, f"vball{i}", [128, NTL, 384], BF16) for i in range(2)]
                    kob_all = [sb(gs2, f"koball{i}", [128, NTL, 192], BF16) for i in range(2)]
                    Sb_all = [sb(gs2, f"Sball{i}", [96, NPT, 2, 192], BF16) for i in range(2)]
                    if with_out:
                        sTb_all = [sb(gs2, f"sTball{i}", [128, NTL, 2, 128], BF16) for i in range(2)]
                        ob = [sb(gs2, f"ob{i}", [128, 384], BF16) for i in range(2)]
                        sg = [sb(gs2, f"sg{i}", [128, 512], BF16) for i in range(2)]
                    if has_s:
                        Qpad = sb(gs2, "Qpad", [96, 2, 2304], BF16)
                        S0f = [sb(gs2, f"S0f{i}", [96, 2, 192], F32) for i in range(4)]
                        S0b = [sb(gs2, f"S0b{i}", [96, 2, 192], BF16) for i in range(4)]
                        Vpad = [sb(gs2, f"Vpad{i}", [128, 384], BF16) for i in range(2)]

                    def state_step(hp, ci, pbank, pbk):
                        vk, kk = f"vb{hp}_{ci}", f"kob{hp}_{ci}"
                        pk_ = pidx[ci]
                        for hh in range(2):
                            K.op("pe", lambda e, hh=hh: e.matmul(pbank[0:96, hh * 192:(hh + 1) * 192],
                                                                lhsT=kob_all[hp][:, ci, hh * 96:(hh + 1) * 96],
                                                                rhs=vb_all[hp][:, ci, hh * 192:(hh + 1) * 192], start=True, stop=True),
                                 reads=[kk, vk], writes=[pbk], signal=(hh == 1))

                        def post():
                            for hh in range(2):
                                h = 2 * hp + hh
                                K.op("dve", lambda e, hh=hh, h=h: e.scalar_tensor_tensor(
                                    out=Sst[:, h, :], in0=Sst[:, h, :], scalar=dec[:, h, pk_:pk_ + 1], in1=pbank[0:96, hh * 192:(hh + 1) * 192],
                                    op0=ALU.mult, op1=ALU.add),
                                     reads=[f"Sst{h}", "dec", pbk], writes=[f"Sst{h}"])
                            if pk_ + 1 < NPT:
                                K.op("act", lambda e: e.copy(out=Sb_all[hp][:, pk_ + 1], in_=Sst[:, 2 * hp:2 * hp + 2, :]),
                                     reads=[f"Sst{2 * hp}", f"Sst{2 * hp + 1}"], writes=[f"Sb{hp}_{pk_ + 1}"])
                        return post

                    def stage1(hp, wslot, wk, do_state):
                        K.op("act", lambda e: e.copy(out=Sb_all[hp][:, 0], in_=Sst[:, 2 * hp:2 * hp + 2, :]),
                             reads=[f"Sst{2 * hp}", f"Sst{2 * hp + 1}"], writes=[f"Sb{hp}_0"])
                        for ci, (kind, t0) in enumerate(tiles):
                            pb = ci % 2
                            vk, kk, sk = f"vb{hp}_{ci}", f"kob{hp}_{ci}", f"sTb{hp}_{ci}"
                            proj_tok(hT, hTk, t0, wslot, wk, 384, P[pb], PK[pb])
                            for hh in range(2):
                                h = 2 * hp + hh
                                K.op("pe", lambda e, hh=hh, h=h, t0=t0: e.transpose(out=T[1][:, hh * 96:(hh + 1) * 96],
                                                                                    in_=koutT[:, h, t0:t0 + 128], identity=ident[0:96, 0:96]),
                                     reads=["koutT", "ident"], writes=[TK[1]], signal=(hh == 1))
                            if with_out:
                                for hh in range(2):
                                    h = 2 * hp + hh
                                    K.op("pe", lambda e, hh=hh, h=h, t0=t0: e.matmul(P[2][:, hh * 128:(hh + 1) * 128],
                                                                                    lhsT=kinT[:, h, t0:t0 + 128], rhs=qinT[:, h, t0:t0 + 128],
                                                                                    start=True, stop=True),
                                         reads=["kinT", "qinT"], writes=[PK[2]], signal=(hh == 1))
                            post = None
                            if do_state and ci >= 1 and tiles[ci - 1][0] == "p":
                                post = state_step(hp, ci - 1, P[3], PK[3])
                            K.op("act", lambda e, ci=ci, pb=pb: e.copy(out=vb_all[hp][:, ci, :], in_=P[pb][:, 0:384]), reads=[PK[pb]], writes=[vk])
                            K.op("dve", lambda e, ci=ci: e.tensor_copy(out=kob_all[hp][:, ci, :], in_=T[1][:, 0:192]), reads=[TK[1]], writes=[kk])
                            if with_out:
                                msk = maskT if kind == "p" else maskS
                                K.op("dve", lambda e, ci=ci, msk=msk: e.tensor_tensor(
                                    out=sTb_all[hp][:, ci], in0=P[2][:, 0:256].rearrange("p (h i) -> p h i", h=2),
                                    in1=msk[:].unsqueeze(1).broadcast_to([128, 2, 128]), op=ALU.mult),
                                     reads=[PK[2], "maskT", "maskS"], writes=[sk])
                            if post:
                                post()
                            yield
                        if do_state and tiles[NTL - 1][0] == "p":
                            state_step(hp, NTL - 1, P[3], PK[3])()
                        yield

                    def onorm_pre(hp, ci, ops_banks):
                        sti, stk = st[2 + ci % 2], f"st{2 + ci % 2}"
                        oi = ci % 2
                        for hh, (pb, pbk, off) in enumerate(ops_banks):
                            K.op("act", lambda e, pb=pb, off=off, hh=hh: e.activation(
                                out=ob[oi][:, hh * 192:(hh + 1) * 192], in_=pb[:, off:off + 192], func=AF.Square,
                                accum_out=sti[:, hh:hh + 1]),
                                 reads=[pbk], writes=[f"ob{oi}", stk])
                        K.op("pool", lambda e: e.tensor_scalar(out=sti[:, 2:4], in0=sti[:, 0:2], scalar1=1.0 / 192, scalar2=EPS,
                                                               op0=ALU.mult, op1=ALU.add),
                             reads=[stk], writes=[stk])
                        K.op("pool", lambda e: e.tensor_tensor(out=sti[:, 4:6], in0=sti[:, 2:4], in1=mhalf[:, 0:2], op=ALU.pow),
                             reads=[stk, "mhalf"], writes=[stk])
                        for hh, (pb, pbk, off) in enumerate(ops_banks):
                            K.op("dve", lambda e, pb=pb, off=off, hh=hh: e.scalar_tensor_tensor(
                                out=ob[oi][:, hh * 192:(hh + 1) * 192], in0=pb[:, off:off + 192], scalar=sti[:, 4 + hh:5 + hh],
                                in1=wo_b[:], op0=ALU.mult, op1=ALU.mult),
                                 reads=[pbk, stk, "wo_b"], writes=[f"ob{oi}"])

                    def onorm_post(hp, ci, t0):
                        transpose_to_branch(ob[ci % 2], f"ob{ci % 2}", 3, 6 + 3 * hp, t0, eng="act", tb=0)

                    def stage2(hp, prog):
                        prev = None
                        for ci, (kind, t0) in enumerate(tiles):
                            if kind != "p":
                                continue
                            pb = 4 + ci % 2
                            vk, sk = f"vb{hp}_{ci}", f"sTb{hp}_{ci}"
                            for hh in range(2):
                                h = 2 * hp + hh
                                K.op("pe", lambda e, hh=hh, ci=ci, pb=pb: e.matmul(P[pb][:, hh * 192:(hh + 1) * 192], lhsT=sTb_all[hp][:, ci, hh, :],
                                                                                  rhs=vb_all[hp][:, ci, hh * 192:(hh + 1) * 192], start=True, stop=False),
                                     reads=[sk, vk], writes=[PK[pb]], signal=False)
                                K.op("pe", lambda e, hh=hh, h=h, t0=t0, ci=ci, pb=pb: e.matmul(P[pb][:, hh * 192:(hh + 1) * 192],
                                                                                              lhsT=qinT[:, h, t0:t0 + 128], rhs=Sb_all[hp][:, pidx[ci], hh, :],
                                                                                              start=False, stop=True),
                                     reads=["qinT", f"Sb{hp}_{pidx[ci]}"], writes=[PK[pb]], signal=True)
                            post = state_step(hp, ci, P[3], PK[3])
                            if prev is not None:
                                onorm_post(hp, *prev)
                                prog.add(prev[1])
                            post()
                            onorm_pre(hp, ci, [(P[pb], PK[pb], 0), (P[pb], PK[pb], 192)])
                            prev = (ci, t0)
                            yield
                        onorm_post(hp, *prev)
                        prog.add(prev[1])
                        K.dma("sp", dr["sp"][2 * hp:2 * hp + 2].rearrange("h d e -> d h e"), Sst[:, 2 * hp:2 * hp + 2, :],
                              reads=[f"Sst{2 * hp}", f"Sst{2 * hp + 1}"], semkey="Sst")
                        yield

                    def stage_s(hp):
                        ci = [i for i, (k, _) in enumerate(tiles) if k == "s"][0]
                        t0 = tiles[ci][1]
                        vk, kk, sk = f"vb{hp}_{ci}", f"kob{hp}_{ci}", f"sTb{hp}_{ci}"

                        def load(s):
                            K.dma("sp", S0f[s % 4][:], dr["s0"][s, 2 * hp:2 * hp + 2].rearrange("h d e -> d h e"), writes=[f"S0f{s % 4}"])

                        def prep(s):
                            K.op("act", lambda e: e.copy(out=S0b[s % 4][:], in_=S0f[s % 4][:]), reads=[f"S0f{s % 4}"], writes=[f"S0b{s % 4}"])
                            K.op("dve", lambda e: e.tensor_scalar(out=Vpad[s % 2][:], in0=vb_all[hp][:, ci, :], scalar1=blockmask[:, s:s + 1],
                                                                  scalar2=None, op0=ALU.mult),
                                 reads=[vk, "blockmask"], writes=[f"Vpad{s % 2}"])

                        load(0)
                        load(1)
                        K.op("pool", lambda e: e.memset(Qpad[:], 0.0), writes=["Qpad"])
                        for hh in range(2):
                            h = 2 * hp + hh
                            K.op("pool", lambda e, hh=hh, h=h: e.tensor_copy(
                                out=Qpad[:, hh, :].rearrange("p (s c) -> p s c", c=144)[:, :, 0:8],
                                in_=qinT[:, h, t0:t0 + 128].rearrange("p (s i) -> p s i", i=8)),
                                 reads=["qinT"], writes=["Qpad"])
                        prep(0)
                        for hh in range(2):
                            K.op("pe", lambda e, hh=hh: e.matmul(P[4 + hh][:, 0:192], lhsT=sTb_all[hp][:, ci, hh, :],
                                                                rhs=vb_all[hp][:, ci, hh * 192:(hh + 1) * 192], start=True, stop=False),
                                 reads=[sk, vk], writes=[PK[4 + hh]], signal=False)
                        yield
                        T1f = T[1][:].bitcast(F32)
                        for s in range(16):
                            fk, bk_, pk = f"S0f{s % 4}", f"S0b{s % 4}", f"Vpad{s % 2}"
                            UB, UBk = (P[3], PK[3])
                            if s + 2 < 16:
                                load(s + 2)
                            for hh in range(2):
                                K.op("pe", lambda e, hh=hh, s=s: e.matmul(
                                    P[4 + hh][:, 0:192], lhsT=Qpad[:, hh, 136 * s:136 * s + 128], rhs=S0b[s % 4][:, hh, :],
                                    start=False, stop=(s == 15)),
                                     reads=["Qpad", bk_], writes=[PK[4 + hh]], signal=True)
                            for hh in range(2):
                                K.op("pe", lambda e, hh=hh, s=s: e.matmul(
                                    UB[0:96, hh * 192:(hh + 1) * 192], lhsT=kob_all[hp][:, ci, hh * 96:(hh + 1) * 96],
                                    rhs=Vpad[s % 2][:, hh * 192:(hh + 1) * 192], start=True, stop=True),
                                     reads=[kk, pk], writes=[UBk], signal=(hh == 1))
                            if s + 1 < 16:
                                prep(s + 1)
                            for hh in range(2):
                                h = 2 * hp + hh
                                K.op("dve", lambda e, hh=hh, h=h, s=s: e.scalar_tensor_tensor(
                                    out=S0f[s % 4][:, hh, :], in0=S0f[s % 4][:, hh, :], scalar=dec[:, h, 8 + s:9 + s],
                                    in1=UB[0:96, hh * 192:(hh + 1) * 192], op0=ALU.mult, op1=ALU.add),
                                     reads=[fk, "dec", UBk], writes=[fk])
                            K.dma("sp", dr["ss"][s, 2 * hp:2 * hp + 2].rearrange("h d e -> d h e"), S0f[s % 4][:], reads=[fk])
                            yield
                        onorm_pre(hp, ci, [(P[4], PK[4], 0), (P[5], PK[5], 0)])
                        yield
                        onorm_post(hp, ci, t0)
                        yield

                    def chain(*gens):
                        for g in gens:
                            yield from g

                    def run(*specs):
                        specs = [[g, n] for (g, n) in specs]
                        while specs:
                            for sp_ in list(specs):
                                g, n = sp_
                                for _ in range(n):
                                    try:
                                        next(g)
                                    except StopIteration:
                                        specs.remove(sp_)
                                        break

                    pblocks = [(b0, nb) for (b0, nb, kind) in blocks if kind == "p"]
                    sblocks = [(b0, nb) for (b0, nb, kind) in blocks if kind == "s"]
                    if not with_out:
                        for hp in range(2):
                            wslot, wk = wget()
                            wprefetch()
                            if side is not None:
                                run((stage1(hp, wslot, wk, True), 2), (side, 1))
                            else:
                                run((stage1(hp, wslot, wk, True), 1))
                    else:
                        for hp in range(2):
                            wslot, wk = wget()
                            wprefetch()
                            g1_ = stage1(hp, wslot, wk, False)
                            next(g1_)
                            run((g1_, 1), (stage_s(hp), 2))

                        def gate_with(hp, slot, k_):
                            prog = set()
                            s2 = stage2(hp, prog)
                            for (b0, nb) in sblocks + pblocks:
                                need = {t0 for (kd, t0) in tiles if kd == "p" and b0 <= t0 < b0 + nb}
                                while not need <= prog:
                                    next(s2)
                                for _ in gate_feat(hT, hTk, [(b0, nb)], slot, k_, 3, 6 + 3 * hp, sg, "silu"):
                                    try:
                                        next(s2)
                                    except StopIteration:
                                        pass
                            for _ in s2:
                                pass

                        for hp in range(2):
                            gslot, gk_ = wget()
                            wprefetch()
                            gate_with(hp, gslot, gk_)
                    K.barrier()

        with ExitStack() as g1:
            hT = sb(g1, "hT", [128, 16, NMAIN], BF16)

            def norm_scope(src_list, wcol, wcolk, dstT, dstk):
                with ExitStack() as ns:
                    xsl = [sb(ns, f"xsl{i}", [128, D], F32) for i in range(4)]
                    xsb_ = [sb(ns, f"xsb{i}", [128, D], BF16) for i in range(3)]
                    jnk = sb(ns, "jnk", [128, D], BF16)
                    for _ in norm_gen((xsl, xsb_, jnk), src_list, dstT, dstk, wcol, wcolk):
                        pass
                    K.barrier()

            wissue_upto(1)
            with ExitStack() as cs:
                Wf = sb(cs, "Wf", [128, 4, 128], F32)
                Wsf = sb(cs, "Wsf", [128, 4, 128], F32)
                Wmb = sb(cs, "Wmb", [128, 4, 128], BF16)
                Wsmb = sb(cs, "Wsmb", [128, 4, 128], BF16)
                K.op("pool", lambda e: e.memset(Wsf[:], 0.0), writes=["Wsf"])
                ck = dict(semkey="const")
                K.dma("sp", normw_col[:], dr["norm_w"].rearrange("o (k p) -> p (o k)", p=128), writes=["normw_col"],
                      allow_slow_non_contiguous=True, **ck)
                K.dma("sp", memw_col[:], dr["mem_norm_w"].rearrange("o (k p) -> p (o k)", p=128), writes=["memw_col"],
                      allow_slow_non_contiguous=True, **ck)
                if stop == 'c2':
                    K.barrier()
                    return nc
                K.dma("sp", wo_b[:], dr["b_onorm_w"].partition_broadcast(128), writes=["wo_b"], **ck)
                K.dma("sp", abT[:], dr["a_bs"].rearrange("h i -> i h"), writes=["abT"], allow_slow_non_contiguous=True, **ck)
                K.dma("sp", bwa[0:16, :], dr["b_wa"], writes=["bwa"], **ck)
                K.dma("sp", bwa[16:17, :], dr["b_ba"], writes=["bwa"], part=True, **ck)
                if stop == 'c3':
                    K.barrier()
                    return nc
                K.dma("sp", Wf[:], dr["a_ws"].rearrange("h i j -> i h j"), writes=["Wf"], **ck)
                for s in range(16):
                    K.dma(["sp", "act"][s % 2], abTs[8 * s:8 * s + 8, :], dr["a_bs"][:, 0:8].rearrange("h i -> i h"), writes=["abTs"],
                          part=(s > 0), allow_slow_non_contiguous=True, **ck)
                    K.dma(["act", "sp"][s % 2], Wsf[8 * s:8 * s + 8, :, 8 * s:8 * s + 8], dr["a_ws"][:, 0:8, 0:8].rearrange("h i j -> i h j"),
                          writes=["Wsf"], part=(s > 0), **ck)
                if stop == 'c4':
                    K.barrier()
                    return nc
                K.seal("const")
                for src, srck, dst, dstk in ((Wf, "Wf", Wmb, "Wmb"), (Wsf, "Wsf", Wsmb, "Wsmb")):
                    K.op("pool", lambda e, src=src: e.affine_select(out=src[:], in_=src[:], pattern=[[0, 4], [-1, 128]],
                                                                    compare_op=ALU.is_ge, fill=0.0, base=0, channel_multiplier=1),
                         reads=[srck], writes=[srck])
                    K.op("pool", lambda e, src=src, dst=dst: e.tensor_copy(out=dst[:], in_=src[:]), reads=[srck], writes=[dstk])
                if stop == 'c5':
                    K.barrier()
                    return nc
                for src, srck, dst, dstk, tb in ((Wmb, "Wmb", WT, "WT", 0), (Wsmb, "Wsmb", WTs, "WTs", 1)):
                    for h in range(4):
                        K.op("pe", lambda e, src=src, h=h, tb=tb: e.transpose(out=T[tb][:, h * 128:(h + 1) * 128], in_=src[:, h, :],
                                                                             identity=ident[:]),
                             reads=[srck, "ident"], writes=[TK[tb]], signal=(h == 3))
                    K.op("dve", lambda e, dst=dst, tb=tb: e.tensor_copy(out=dst[:].rearrange("p h i -> p (h i)"), in_=T[tb][:, 0:512]),
                         reads=[TK[tb]], writes=[dstk])
                norm_scope([(dr["xpre"], 8, 0)], normw_col, "normw_col", branchT, "hTp")

            if stop == 'prenorm':
                K.barrier()
                return nc
            with ExitStack() as sn:
                xsl_s = [sb(sn, f"xsl{i}", [128, D], F32) for i in range(3)]
                xsb_s = [sb(sn, f"xsb{i}", [128, D], BF16) for i in range(2)]
                side = norm_gen((xsl_s, xsb_s, None), [(dr["xp"], 8, 0), (dr["xs"], 1, 1024)], hT, "hT", normw_col, "normw_col")
                gla_phase(branchT, "hTp", NPRE, [("p", t * 128) for t in range(8)], with_out=False, side=side)
                for _ in side:
                    pass
                K.barrier()

            if stop == 'pregla':
                K.barrier()
                return nc

            with ExitStack() as ms:
                hmT = sb(ms, "hmT", [128, 16, 256], BF16)
                stg = [sb(ms, f"stg{i}", [128, 256], F32) for i in range(2)]
                norm_scope([(dr["mem"], 2, 0)], memw_col, "memw_col", hmT, "hmT")
                if stop == 'm1':
                    K.barrier()
                    return nc
                si = 0
                for j in range(4):
                    wslot, wk = wget()
                    wprefetch()
                    isK = j < 2
                    jj = j % 2
                    for t in range(2):
                        pi = (2 * j + t) % 2
                        for k in range(16):
                            K.op("pe", lambda e, k=k, t=t, pi=pi: e.matmul(P[pi][:, 0:256], lhsT=hmT[:, k, t * 128:(t + 1) * 128],
                                                                          rhs=wslot[:, k, 0:256], start=(k == 0), stop=(k == 15)),
                                 reads=["hmT", wk], writes=[PK[pi]], signal=(k == 15))
                        sgi = si % 2
                        si += 1
                        K.op("act", lambda e, pi=pi, sgi=sgi: e.copy(out=stg[sgi][:], in_=P[pi][:, 0:256]), reads=[PK[pi]],
                             writes=[f"stg{sgi}"])
                        if True:
                            K.dma("sp", dr["mk" if isK else "mv"][t * 128:(t + 1) * 128, jj * 256:(jj + 1) * 256], stg[sgi][:],
                                  reads=[f"stg{sgi}"])
                        if not isK:
                            K.op("dve", lambda e, pi=pi, t=t, jj=jj: e.tensor_copy(out=Vb[:, t, jj * 256:(jj + 1) * 256], in_=P[pi][:, 0:256]),
                                 reads=[PK[pi]], writes=["Vb"])
                    if stop == 'm2':
                        K.barrier()
                        return nc
                    if isK:
                        for hh in range(2):
                            pi = 2 + hh
                            proj_feat(hmT, "hmT", 0, 256, wslot, wk, hh * 128, 128, P[pi], PK[pi])
                            K.op("dve", lambda e, pi=pi, hh=hh, jj=jj: e.tensor_copy(out=KT[:, 2 * jj + hh, :], in_=P[pi][:, 0:256]),
                                 reads=[PK[pi]], writes=["KT"])
                    if stop == f"mj{j}":
                        K.barrier()
                        return nc
                K.barrier()

            if stop == 'memkv':
                K.barrier()
                return nc


            if stop == 'mainnorm':
                K.barrier()
                return nc
            main_tiles = [("s", 1024)] + [("p", t * 128) for t in range(8)]
            gla_phase(hT, "hT", NMAIN, main_tiles, with_out=True)

            if stop == 'maingla':
                K.barrier()
                return nc

            mblocks = [(0, 512), (512, 512), (1024, 128)]
            late = g1.enter_context(ExitStack())
            wout = sb(late, "wout", [128, 16, D], BF16)

            with ExitStack() as as_:
                wv_b = sb(as_, "wv_b", [128, 768], F32)
                vnb = [sb(as_, f"vnb{i}", [128, 768], BF16) for i in range(2)]
                vnf = sb(as_, "vnf", [128, 768], F32)
                mixb = [sb(as_, f"mixb{i}", [128, 768], BF16) for i in range(2)]
                sgA = [sb(as_, f"sg{i}", [128, 512], BF16) for i in range(2)]
                K.dma("sp", wv_b[:], dr["a_vnorm_w"].partition_broadcast(128), writes=["wv_b"])
                (w0, w0k), (w1, w1k) = wget(2)
                for q in range(4):
                    K.dma("pool", wout[:, :, q * 512:(q + 1) * 512], wout_v[:, :, q * 512:(q + 1) * 512], writes=[f"wout{q}"])
                NTm = len(main_tiles)

                def a_proj_pe(ci):
                    kind, t0 = main_tiles[ci]
                    pv = [0, 1] if ci % 2 == 0 else [4, 5]
                    proj_tok(hT, "hT", t0, w0, w0k, 384, P[pv[0]], PK[pv[0]])
                    proj_tok(hT, "hT", t0, w1, w1k, 384, P[pv[1]], PK[pv[1]])

                def a_proj_post(ci):
                    kind, t0 = main_tiles[ci]
                    pv = [0, 1] if ci % 2 == 0 else [4, 5]
                    vi = ci % 2
                    sti, stk = st[ci % 2], f"st{ci % 2}"
                    for half in range(2):
                        K.op("act", lambda e, half=half: e.activation(
                            out=sgA[vi][:, 0:384], in_=P[pv[half]][:, 0:384], func=AF.Square, accum_out=sti[:, half:half + 1]),
                             reads=[PK[pv[half]]], writes=[f"sg{vi}", stk])
                    rstd_from_ss(sti, stk, 2, 768.0)
                    for half in range(2):
                        if kind == "s":
                            K.op("dve", lambda e, half=half: e.scalar_tensor_tensor(
                                out=vnf[:, half * 384:(half + 1) * 384], in0=P[pv[half]][:, 0:384], scalar=sti[:, 6:7],
                                in1=wv_b[:, half * 384:(half + 1) * 384], op0=ALU.mult, op1=ALU.mult),
                                 reads=[PK[pv[half]], stk, "wv_b"], writes=["vnf"])
                            K.op("act", lambda e, half=half: e.copy(out=vnb[vi][:, half * 384:(half + 1) * 384],
                                                                    in_=vnf[:, half * 384:(half + 1) * 384]),
                                 reads=["vnf"], writes=[f"vnb{vi}"])
                        else:
                            K.op("dve", lambda e, half=half: e.scalar_tensor_tensor(
                                out=vnb[vi][:, half * 384:(half + 1) * 384], in0=P[pv[half]][:, 0:384], scalar=sti[:, 6:7],
                                in1=wv_b[:, half * 384:(half + 1) * 384], op0=ALU.mult, op1=ALU.mult),
                                 reads=[PK[pv[half]], stk, "wv_b"], writes=[f"vnb{vi}"])
                    if kind == "s":
                        K.dma("sp", dr["cvs"][:, :], vnf[:], reads=["vnf"])

                def a_mix_pe(ci):
                    kind, t0 = main_tiles[ci]
                    vi = ci % 2
                    Wm, Wmk = (WT, "WT") if kind == "p" else (WTs, "WTs")
                    for h in range(4):
                        pb = 2 + h // 2
                        K.op("pe", lambda e, h=h, pb=pb: e.matmul(P[pb][:, (h % 2) * 192:(h % 2 + 1) * 192], lhsT=Wm[:, h, :],
                                                                 rhs=vnb[vi][:, h * 192:(h + 1) * 192], start=True, stop=True),
                             reads=[Wmk, f"vnb{vi}"], writes=[PK[pb]], signal=(h % 2 == 1))

                def a_mix_post(ci):
                    kind, t0 = main_tiles[ci]
                    vi = ci % 2
                    ab = abT if kind == "p" else abTs
                    for h in range(4):
                        pb = 2 + h // 2
                        K.op("dve", lambda e, h=h, pb=pb: e.tensor_scalar(
                            out=mixb[vi][:, h * 192:(h + 1) * 192], in0=P[pb][:, (h % 2) * 192:(h % 2 + 1) * 192],
                            scalar1=ab[:, h:h + 1], scalar2=None, op0=ALU.add),
                             reads=[PK[pb], "abT", "abTs"], writes=[f"mixb{vi}"])

                def a_tr(ci):
                    kind, t0 = main_tiles[ci]
                    transpose_to_branch(mixb[ci % 2], f"mixb{ci % 2}", 6, 0, t0, eng="act", tb=ci % 2)

                for step in range(NTm + 2):
                    if step < NTm:
                        a_proj_pe(step)
                        a_proj_post(step)
                    if 0 <= step - 1 < NTm:
                        a_mix_pe(step - 1)
                        a_mix_post(step - 1)
                    if 0 <= step - 2 < NTm:
                        a_tr(step - 2)
                wdone()
                wprefetch()
                for j in range(2):
                    wslot, wk = wget()
                    wprefetch()
                    for _ in gate_feat(hT, "hT", mblocks, wslot, wk, 3, 3 * j, sgA, "mul"):
                        pass
                for j in range(2):
                    wslot, wk = wget()
                    wprefetch()
                    for _ in gate_feat(hT, "hT", mblocks, wslot, wk, 3, 3 * j, sgA, "silu"):
                        pass
                K.barrier()

            with ExitStack() as xs_:
                qxT = sb(xs_, "qxT", [128, 4, NMAIN], BF16)
                pT = [sb(xs_, f"pT{i}", [128, 2, 512], BF16) for i in range(2)]
                rinv = sb(xs_, "rinv", [128, 512], F32)
                rprod = sb(xs_, "rprod", [128, 512], BF16)
                sgX = [sb(xs_, f"sg{i}", [128, 512], BF16) for i in range(2)]
                ckb = [sb(xs_, f"ckb{i}", [128, 2, 512], BF16) for i in range(2)]
                cvb = [sb(xs_, f"cvb{i}", [128, 2, 512], BF16) for i in range(3)]
                KTs = [sb(xs_, f"KTs{i}", [128, 4, 256], BF16) for i in range(2)]
                pTs = pT[0]
                scale = 128.0 ** -0.5
                t0 = 1024
                qslots = wget(2)
                qcnt = [0]

                def q_step(j, b0, nb, hh):
                    h = 2 * j + hh
                    pi = qcnt[0] % 2
                    qcnt[0] += 1
                    proj_feat(hT, "hT", b0, nb, qslots[j][0], qslots[j][1], hh * 128, 128, P[pi], PK[pi])
                    K.op("act", lambda e: e.copy(out=qxT[:, h, b0:b0 + nb], in_=P[pi][:, 0:nb]),
                         reads=[PK[pi]], writes=[f"qxT{h}_{b0}"])

                def q_prompt_gen():
                    for j in range(2):
                        for (b0, nb) in mblocks[:2]:
                            for hh in range(2):
                                q_step(j, b0, nb, hh)
                                yield

                its = [(b0, nb, h) for (b0, nb) in mblocks[:2] for h in range(4)]

                def x_scores(i):
                    b0, nb, h = its[i]
                    for c in range(2):
                        K.op("pe", lambda e, c=c: e.matmul(P[4 + c][:, 0:nb], lhsT=KT[:, h, c * 128:(c + 1) * 128],
                                                          rhs=qxT[:, h, b0:b0 + nb], start=True, stop=True),
                             reads=["KT", f"qxT{h}_{b0}"], writes=[PK[4 + c]])

                def x_exp(i):
                    b0, nb, h = its[i]
                    pi = i % 2
                    for c in range(2):
                        K.op("act", lambda e, c=c: e.activation(out=pT[pi][:, c, 0:nb], in_=P[4 + c][:, 0:nb], func=AF.Exp, scale=scale),
                             reads=[PK[4 + c]], writes=[f"pT{pi}"])

                def x_pv(i):
                    b0, nb, h = its[i]
                    pi = i % 2
                    for c in range(2):
                        K.op("pe", lambda e, c=c: e.matmul(P[2][:, 0:nb], lhsT=Vb[:, c, h * 128:(h + 1) * 128],
                                                          rhs=pT[pi][:, c, 0:nb], start=(c == 0), stop=(c == 1)),
                             reads=["Vb", f"pT{pi}"], writes=[PK[2]], signal=(c == 1))
                    for c in range(2):
                        K.op("pe", lambda e, c=c: e.matmul(P[3][:, 0:nb], lhsT=ones_b[:], rhs=pT[pi][:, c, 0:nb],
                                                          start=(c == 0), stop=(c == 1)),
                             reads=["ones_b", f"pT{pi}"], writes=[PK[3]], signal=(c == 1))

                def x_fin(i):
                    b0, nb, h = its[i]
                    K.op("dve", lambda e: e.reciprocal(out=rinv[:, 0:nb], in_=P[3][:, 0:nb]), reads=[PK[3]], writes=["rinv"])
                    K.op("dve", lambda e: e.tensor_tensor(out=branchT[:, 12 + h, b0:b0 + nb], in0=P[2][:, 0:nb], in1=rinv[:, 0:nb],
                                                          op=ALU.mult),
                         reads=[PK[2], "rinv"], writes=bkeys(12 + h, 1, b0, nb))

                xp_prog = set()

                def xp_gen():
                    for step in range(len(its) + 1):
                        if step < len(its):
                            x_scores(step)
                        if step >= 1:
                            x_pv(step - 1)
                        if step < len(its):
                            x_exp(step)
                        if step >= 1:
                            x_fin(step - 1)
                            xp_prog.add((its[step - 1][0], its[step - 1][2]))
                        yield

                def xs_load(s):
                    K.dma("pool", ckb[s % 2][:], dr["ck"][s].rearrange("(c p) f -> p c f", p=128), writes=[f"ckb{s % 2}"])
                    K.dma("pool", cvb[s % 3][:], dr["cv"][s].rearrange("(c p) f -> p c f", p=128), writes=[f"cvb{s % 3}"])

                def xs_tr(s):
                    si_ = s % 2
                    for h in range(4):
                        for c in range(2):
                            K.op("pe", lambda e, h=h, c=c: e.transpose(out=T[si_][:, (h * 2 + c) * 128:(h * 2 + c + 1) * 128],
                                                                       in_=ckb[s % 2][:, c, h * 128:(h + 1) * 128], identity=ident[:]),
                                 reads=[f"ckb{s % 2}", "ident"], writes=[TK[si_]], signal=(h == 3 and c == 1))

                def xs_tr_post(s):
                    si_ = s % 2
                    K.op("dve", lambda e: e.tensor_copy(out=KTs[si_][:].rearrange("p h n -> p (h n)"), in_=T[si_][:, 0:1024]),
                         reads=[TK[si_]], writes=[f"KTs{si_}"])

                def xs_scores(s):
                    si_ = s % 2
                    for h in range(4):
                        for c in range(2):
                            K.op("pe", lambda e, h=h, c=c: e.matmul(
                                P[4 + si_][:, c * 32 + h * 8:c * 32 + h * 8 + 8], lhsT=KTs[si_][:, h, c * 128:(c + 1) * 128],
                                rhs=qxT[:, h, t0 + 8 * s:t0 + 8 * s + 8], start=True, stop=True),
                                 reads=[f"KTs{si_}", f"qxT{h}_{t0}"], writes=[PK[4 + si_]], signal=(h == 3 and c == 1))

                def xs_exp(s):
                    si_ = s % 2
                    K.op("act", lambda e: e.activation(out=pTs[:, :, s * 32:(s + 1) * 32],
                                                       in_=P[4 + si_][:, 0:64].rearrange("p (c x) -> p c x", c=2), func=AF.Exp, scale=scale),
                         reads=[PK[4 + si_]], writes=[f"pTs{s}", "pT0"])

                def xs_pv(s):
                    for h in range(4):
                        for c in range(2):
                            K.op("pe", lambda e, h=h, c=c: e.matmul(
                                P[2][:, s * 32 + h * 8:s * 32 + h * 8 + 8], lhsT=cvb[s % 3][:, c, h * 128:(h + 1) * 128],
                                rhs=pTs[:, c, s * 32 + h * 8:s * 32 + h * 8 + 8], start=(c == 0), stop=(c == 1)),
                                 reads=[f"cvb{s % 3}", f"pTs{s}"], writes=[PK[2]], signal=(h == 3 and c == 1))

                def xs_gen():
                    xs_load(0)
                    for step in range(16 + 2):
                        if step < 16:
                            xs_tr(step)
                        if 0 <= step - 1 < 16:
                            xs_scores(step - 1)
                        if 0 <= step - 2 < 16:
                            xs_pv(step - 2)
                        if step + 1 < 16:
                            xs_load(step + 1)
                        if step < 16:
                            xs_tr_post(step)
                        if 0 <= step - 1 < 16:
                            xs_exp(step - 1)
                        yield
                    for c in range(2):
                        K.op("pe", lambda e, c=c: e.matmul(P[3][:, 0:512], lhsT=ones_b[:], rhs=pTs[:, c, :], start=(c == 0), stop=(c == 1)),
                             reads=["ones_b"] + [f"pTs{s}" for s in range(16)], writes=[PK[3]], signal=(c == 1))
                    K.op("dve", lambda e: e.reciprocal(out=rinv[:], in_=P[3][:, 0:512]), reads=[PK[3]], writes=["rinv"])
                    K.op("dve", lambda e: e.tensor_tensor(out=rprod[:], in0=P[2][:, 0:512], in1=rinv[:], op=ALU.mult),
                         reads=[PK[2], "rinv"], writes=["rprod"])
                    for h in range(4):
                        K.op("dve", lambda e, h=h: e.tensor_copy(
                            out=branchT[:, 12 + h, t0:t0 + 128].rearrange("p (s i) -> p s i", i=8),
                            in_=rprod[:].rearrange("p (s h i) -> p s h i", s=16, h=4)[:, :, h, :]),
                             reads=["rprod"], writes=bkeys(12 + h, 1, t0, 128))
                    yield

                def runx(*specs):
                    specs = [[g, n] for (g, n) in specs]
                    while specs:
                        for sp_ in list(specs):
                            g, n = sp_
                            for _ in range(n):
                                try:
                                    next(g)
                                except StopIteration:
                                    specs.remove(sp_)
                                    break

                for j in range(2):
                    for hh in range(2):
                        q_step(j, t0, 128, hh)
                runx((q_prompt_gen(), 1), (xs_gen(), 2))
                wdone()

                xp = xp_gen()

                def gate_with_xp(j, slot, k_):
                    for (b0, nb) in [mblocks[2]] + mblocks[:2]:
                        need = {(b0, 2 * j + hh) for hh in range(2)} if b0 < 1024 else set()
                        while not need <= xp_prog:
                            next(xp)
                        for _ in gate_feat(hT, "hT", [(b0, nb)], slot, k_, 2, 12 + 2 * j, sgX, "silu"):
                            try:
                                next(xp)
                            except StopIteration:
                                pass

                for j in range(2):
                    wslot, wk = wget()
                    wprefetch()
                    gate_with_xp(j, wslot, wk)
                for _ in xp:
                    pass
                K.barrier()

            with ExitStack() as os_:
                wf_b = sb(os_, "wf_b", [128, D], F32)
                xsl = [sb(os_, "xsl0", [128, D], F32)] * 2
                ysl = [sb(os_, f"ysl{i}", [128, D], F32) for i in range(2)]
                K.dma("sp", wf_b[:], dr["final_norm_w"].partition_broadcast(128), writes=["wf_b"])
                for t in range(9):
                    s = t % 2
                    src = dr["xp"][t * 128:(t + 1) * 128, :] if t < 8 else dr["xs"][:, :]
                    dst = dr["yp"][t * 128:(t + 1) * 128, :] if t < 8 else dr["ys"][:, :]
                    xk, yk, sk = "xsl0", f"ysl{s}", f"st{s}"
                    K.dma("sp", xsl[s][:], src, writes=[xk])
                    for q in range(4):
                        pi = (t * 4 + q) % 6
                        for k in range(16):
                            K.op("pe", lambda e, k=k, q=q, pi=pi: e.matmul(P[pi][:, 0:512], lhsT=branchT[:, k, t * 128:(t + 1) * 128],
                                                                          rhs=wout[:, k, q * 512:(q + 1) * 512], start=(k == 0), stop=(k == 15)),
                                 reads=[f"bT{k}_{t}", f"wout{q}"], writes=[PK[pi]], signal=(k == 15))
                        K.op("dve", lambda e, q=q, pi=pi, s=s: e.tensor_tensor(out=ysl[s][:, q * 512:(q + 1) * 512], in0=P[pi][:, 0:512],
                                                                              in1=xsl[s][:, q * 512:(q + 1) * 512], op=ALU.add),
                             reads=[PK[pi], xk], writes=[yk])
                    K.op("act", lambda e, s=s: e.activation(out=xsl[s][:], in_=ysl[s][:], func=AF.Square, accum_out=st[s][:, 0:1]),
                         reads=[yk], writes=[xk, sk])
                    rstd_from_ss(st[s], sk, 1, float(D))
                    K.op("dve", lambda e, s=s: e.scalar_tensor_tensor(out=ysl[s][:], in0=ysl[s][:], scalar=st[s][:, 6:7], in1=wf_b[:],
                                                                      op0=ALU.mult, op1=ALU.mult),
                         reads=[yk, sk, "wf_b"], writes=[yk])
                    K.dma("sp", dst, ysl[s][:], reads=[yk])
                K.barrier()
    return nc


_NC_CACHE = {}


def kernel(x_prompt, x_sample, mem_prompt, state_gla, cache_mem_k, cache_mem_v,
           norm_w, w_in, a_vnorm_w, a_ws, a_bs, b_wa, b_ba, b_onorm_w,
           mem_norm_w, w_mem_kv, w_out, final_norm_w):
    f = lambda a: np.ascontiguousarray(np.asarray(a, dtype=np.float32))
    x_prompt, x_sample, mem_prompt = f(x_prompt), f(x_sample), f(mem_prompt)
    state_gla, cache_mem_k, cache_mem_v = f(state_gla), f(cache_mem_k), f(cache_mem_v)
    shared = {
        "norm_w": f(norm_w).reshape(1, D), "w_in": f(w_in).reshape(D, DIN), "a_vnorm_w": f(a_vnorm_w).reshape(1, 768),
        "a_ws": f(a_ws).reshape(4, 128, 128), "a_bs": f(a_bs).reshape(4, 128), "b_wa": f(b_wa).reshape(16, 384),
        "b_ba": f(b_ba).reshape(1, 384), "b_onorm_w": f(b_onorm_w).reshape(1, 192), "mem_norm_w": f(mem_norm_w).reshape(1, D),
        "w_mem_kv": f(w_mem_kv).reshape(D, 1024), "w_out": f(w_out).reshape(D, D), "final_norm_w": f(final_norm_w).reshape(1, D),
    }
    zeros_pre = np.zeros((1024, D), np.float32)
    in_maps = []
    for c in range(8):
        b, half = c // 2, c % 2
        m = dict(shared)
        m["xp"] = np.ascontiguousarray(x_prompt[b, half * 1024:(half + 1) * 1024])
        m["xpre"] = np.ascontiguousarray(x_prompt[b, 0:1024]) if half == 1 else zeros_pre
        m["xs"] = np.ascontiguousarray(x_sample[16 * c:16 * c + 16].reshape(128, D))
        m["mem"] = np.ascontiguousarray(mem_prompt[b])
        m["s0"] = np.ascontiguousarray(state_gla[0, 16 * c:16 * c + 16])
        m["ck"] = np.ascontiguousarray(cache_mem_k[0, 16 * c:16 * c + 16].reshape(16, 256, 512))
        m["cv"] = np.ascontiguousarray(cache_mem_v[0, 16 * c:16 * c + 16].reshape(16, 256, 512))
        in_maps.append(m)
    if "nc" not in _NC_CACHE:
        _NC_CACHE["nc"] = build_program()
    res = run_bass_kernel_spmd(_NC_CACHE["nc"], in_maps, core_ids=list(range(8)))
    R = res.results
    y_prompt = np.zeros((4, 2048, D), np.float32)
    y_sample = np.zeros((128, 8, D), np.float32)
    mem_k = np.zeros((1, 4, 256, 4, 128), np.float32)
    mem_v = np.zeros((1, 4, 256, 4, 128), np.float32)
    st_p = np.zeros((1, 4, 4, 96, 192), np.float32)
    st_s = np.zeros((1, 128, 4, 96, 192), np.float32)
    cv_s = np.zeros((1, 128, 8, 768), np.float32)
    for c in range(8):
        b, half = c // 2, c % 2
        r = R[c]
        y_prompt[b, half * 1024:(half + 1) * 1024] = r["yp"]
        y_sample[16 * c:16 * c + 16] = r["ys"].reshape(16, 8, D)
        if half == 0:
            mem_k[0, b] = r["mk"].reshape(256, 4, 128)
            mem_v[0, b] = r["mv"].reshape(256, 4, 128)
        else:
            st_p[0, b] = r["sp"]
        st_s[0, 16 * c:16 * c + 16] = r["ss"]
        cv_s[0, 16 * c:16 * c + 16] = r["cvs"].reshape(16, 8, 768)
    return (y_prompt, y_sample, mem_k, mem_v, st_p, st_s, cv_s)
```

```python
import os
import numpy as np
from contextlib import ExitStack
import concourse.bass as bass
import concourse.mybir as mybir
from concourse.bass_utils import run_bass_kernel_spmd

F32 = mybir.dt.float32
BF16 = mybir.dt.bfloat16
AF = mybir.ActivationFunctionType
ALU = mybir.AluOpType

D = 2048
DIN = 5648
EPS = 1e-6
NPRE = 1024
NMAIN = 1152
C_AU, C_AV, C_AG = 0, 768, 1536
C_BQ, C_BK, C_BV, C_BR, C_BG = 2304, 2688, 3072, 3840, 3856
C_XQ, C_XG = 4624, 5136
WS = 384
NSLOT = 2


class KB:
    def __init__(self, nc, es):
        self.nc = nc
        self.es = es
        self.eng = {"pe": nc.tensor, "act": nc.scalar, "dve": nc.vector, "pool": nc.gpsimd, "sp": nc.sync}
        self.sems = {}
        self.cnt = {}
        for n in self.eng:
            self.sems["e_" + n] = es.enter_context(nc.semaphore("e_" + n))
            self.cnt["e_" + n] = 0
        self.seen = {n: {} for n in self.eng}
        self.lastw = {}
        self.readers = {}
        self.semkeys = {}
        self.groupbase = {}

    def _wait(self, E, ev):
        if ev is None:
            return
        sn, val = ev
        if sn == "e_" + E and E == "pe":
            return
        if sn.startswith("d_"):
            val = max(val, self.cnt[sn])
        if self.seen[E].get(sn, 0) >= val:
            return
        self.eng[E].wait_ge(self.sems[sn], val)
        self.seen[E][sn] = val

    def _deps(self, E, reads, writes, skip_waw_sem=None):
        for r in reads:
            self._wait(E, self.lastw.get(r))
            if len(r) == 2 and r[0] in "PT" and r[1].isdigit():
                for ev in self.readers.get(r, ()):
                    if ev[0] != "e_" + E:
                        self._wait(E, ev)
        for w in writes:
            lw = self.lastw.get(w)
            if lw is not None and skip_waw_sem is not None and lw[0] == skip_waw_sem:
                self._wait(E, self.groupbase.get(w))
            else:
                self._wait(E, lw)
                if skip_waw_sem is None and E in ("sp", "pool", "act"):
                    self.groupbase[w] = lw
            for ev in self.readers.get(w, ()):
                self._wait(E, ev)

    def _record(self, ev, reads, writes):
        for w in writes:
            self.lastw[w] = ev
            self.readers[w] = []
        for r in reads:
            self.readers.setdefault(r, []).append(ev)

    def op(self, E, fn, reads=(), writes=(), signal=True):
        self._deps(E, reads, writes)
        ins = fn(self.eng[E])
        sn = "e_" + E
        if signal:
            self.cnt[sn] += 1
            ins.then_inc(self.sems[sn], 1)
            ev = (sn, self.cnt[sn])
        else:
            ev = (sn, self.cnt[sn] + 1)
        self._record(ev, reads, writes)
        return ins

    def dma(self, Q, out, in_, reads=(), writes=(), semkey=None, part=False, **kw):
        key = semkey or (writes[0] if writes else reads[0])
        sn = "d_" + str(key)
        if sn not in self.sems:
            self.sems[sn] = self.es.enter_context(self.nc.semaphore(sn.replace("(", "_").replace(")", "_").replace(",", "_").replace(" ", "").replace("'", "")))
            self.cnt[sn] = 0
        self._deps(Q, reads, writes, skip_waw_sem=sn if part else None)
        ins = self.eng[Q].dma_start(out=out, in_=in_, **kw)
        self.cnt[sn] += 16
        ins.then_inc(self.sems[sn], 16)
        ev = (sn, self.cnt[sn])
        self._record(ev, reads, writes)
        self.semkeys.setdefault(sn, set()).update(writes)
        return ins

    def seal(self, semkey):
        sn = "d_" + str(semkey)
        for k in self.semkeys.get(sn, ()):
            if self.lastw.get(k, (None,))[0] == sn:
                self.lastw[k] = (sn, self.cnt[sn])

    def barrier(self):
        for E in self.eng:
            for sn, c in self.cnt.items():
                if c > 0:
                    self._wait(E, (sn, c))


def build_program(stop=None):
    nc = bass.Bass("TRN2", target_bir_lowering=False)
    dr = {}

    def din(name, shape):
        dr[name] = nc.dram_tensor(name, list(shape), F32, kind="ExternalInput").ap()

    def dout(name, shape):
        dr[name] = nc.dram_tensor(name, list(shape), F32, kind="ExternalOutput").ap()

    din("xp", [1024, D]); din("xpre", [1024, D]); din("xs", [128, D]); din("mem", [256, D])
    din("s0", [16, 4, 96, 192]); din("ck", [16, 256, 512]); din("cv", [16, 256, 512])
    din("norm_w", [1, D]); din("w_in", [D, DIN]); din("a_vnorm_w", [1, 768]); din("a_ws", [4, 128, 128])
    din("a_bs", [4, 128]); din("b_wa", [16, 384]); din("b_ba", [1, 384]); din("b_onorm_w", [1, 192])
    din("mem_norm_w", [1, D]); din("w_mem_kv", [D, 1024]); din("w_out", [D, D]); din("final_norm_w", [1, D])
    dout("yp", [1024, D]); dout("ys", [128, D]); dout("mk", [256, 512]); dout("mv", [256, 512])
    dout("sp", [4, 96, 192]); dout("ss", [16, 4, 96, 192]); dout("cvs", [128, 768])

    win_v = dr["w_in"].rearrange("(k p) c -> p k c", p=128)
    wkv_v = dr["w_mem_kv"].rearrange("(k p) c -> p k c", p=128)
    wout_v = dr["w_out"].rearrange("(k p) c -> p k c", p=128)

    with ExitStack() as es:
        K = KB(nc, es)

        uid = [0]

        def sb(stack, name, shape, dt):
            uid[0] += 1
            return stack.enter_context(nc.sbuf_tensor(f"{name}_{uid[0]}", list(shape), dt))

        P = [es.enter_context(nc.psum_tensor(f"P{i}", [128, 512], F32)) for i in range(6)]
        T = [es.enter_context(nc.psum_tensor(f"T{i}", [128, 1024], BF16)) for i in range(2)]
        PK = [f"P{i}" for i in range(6)]
        TK = [f"T{i}" for i in range(2)]

        ones_f = sb(es, "ones_f", [128, 512], F32)
        ident_f = sb(es, "ident_f", [128, 128], F32)
        ident = sb(es, "ident", [128, 128], BF16)
        maskT = sb(es, "maskT", [128, 128], F32)
        maskS = sb(es, "maskS", [128, 128], F32)
        ones_b = sb(es, "ones_b", [128, 128], BF16)
        blockmask = sb(es, "blockmask", [128, 16], F32)
        normw_col = sb(es, "normw_col", [128, 16], F32)
        memw_col = sb(es, "memw_col", [128, 16], F32)
        wo_b = sb(es, "wo_b", [128, 192], F32)
        abT = sb(es, "abT", [128, 4], F32)
        abTs = sb(es, "abTs", [128, 4], F32)
        WT = sb(es, "WT", [128, 4, 128], BF16)
        WTs = sb(es, "WTs", [128, 4, 128], BF16)
        bwa = sb(es, "bwa", [17, 384], F32)
        Sst = sb(es, "Sst", [96, 4, 192], F32)
        KT = sb(es, "KT", [128, 4, 256], BF16)
        Vb = sb(es, "Vb", [128, 2, 512], BF16)
        branchT = sb(es, "branchT", [128, 16, NMAIN], BF16)
        wsl = [sb(es, f"wsl{i}", [128, 16, WS], BF16) for i in range(NSLOT)]
        st = [sb(es, f"st{i}", [128, 8], F32) for i in range(4)]
        mhalf = sb(es, "mhalf", [128, 8], F32)

        K.op("pool", lambda e: e.memset(ones_f[:], 1.0), writes=["ones_f"])
        K.op("pool", lambda e: e.memset(ones_b[:], 1.0), writes=["ones_b"])
        K.op("pool", lambda e: e.memset(mhalf[:], -0.5), writes=["mhalf"])
        K.op("pool", lambda e: e.memset(Sst[:], 0.0), writes=["Sst0", "Sst1", "Sst2", "Sst3"])
        K.op("pool", lambda e: e.affine_select(out=ident_f[:], in_=ones_f[:, 0:128], pattern=[[-1, 128]],
                                               compare_op=ALU.is_equal, fill=0.0, base=0, channel_multiplier=1),
             reads=["ones_f"], writes=["ident_f"])
        K.op("pool", lambda e: e.tensor_copy(out=ident[:], in_=ident_f[:]), reads=["ident_f"], writes=["ident"])
        K.op("pool", lambda e: e.affine_select(out=maskT[:], in_=ones_f[:, 0:128], pattern=[[1, 128]],
                                               compare_op=ALU.is_ge, fill=0.0, base=0, channel_multiplier=-1),
             reads=["ones_f"], writes=["maskT"])
        K.op("pool", lambda e: e.affine_select(out=maskS[:], in_=maskT[:], pattern=[[-8, 16], [0, 8]],
                                               compare_op=ALU.is_ge, fill=0.0, base=0, channel_multiplier=1),
             reads=["maskT"], writes=["maskS"])
        K.op("pool", lambda e: e.affine_select(out=maskS[:], in_=maskS[:], pattern=[[8, 16], [0, 8]],
                                               compare_op=ALU.is_ge, fill=0.0, base=7, channel_multiplier=-1),
             reads=["maskS"], writes=["maskS"])
        K.op("pool", lambda e: e.affine_select(out=blockmask[:], in_=ones_f[:, 0:16], pattern=[[-8, 16]],
                                               compare_op=ALU.is_ge, fill=0.0, base=0, channel_multiplier=1),
             reads=["ones_f"], writes=["blockmask"])
        K.op("pool", lambda e: e.affine_select(out=blockmask[:], in_=blockmask[:], pattern=[[8, 16]],
                                               compare_op=ALU.is_ge, fill=0.0, base=7, channel_multiplier=-1),
             reads=["blockmask"], writes=["blockmask"])

        if stop == 'c1':
            K.barrier()
            return nc
        jobs = []
        wstate = {"issued": 0, "taken": 0, "done": 0}

        def wissue_upto(j):
            while wstate["issued"] <= min(j, len(jobs) - 1):
                i = wstate["issued"]
                view, c0, ncols = jobs[i]
                s = i % NSLOT
                K.dma("pool", wsl[s][:, :, 0:ncols], view[:, :, c0:c0 + ncols], writes=[f"wsl{s}"])
                wstate["issued"] += 1

        def wget(n=1, keep=0):
            j = wstate["taken"]
            wstate["done"] = j - keep
            wissue_upto(j + n - 1)
            wstate["taken"] += n
            res = [(wsl[(j + i) % NSLOT], f"wsl{(j + i) % NSLOT}") for i in range(n)]
            return res if n > 1 else res[0]

        def wdone():
            wstate["done"] = wstate["taken"]

        def wprefetch():
            wissue_upto(wstate["done"] + NSLOT - 1)

        def J(view, c0, n):
            jobs.append((view, c0, n))

        J(win_v, C_BR, 16); J(win_v, C_BK, 384); J(win_v, C_BV, 384); J(win_v, C_BV + 384, 384)
        J(wkv_v, 0, 256); J(wkv_v, 256, 256); J(wkv_v, 512, 256); J(wkv_v, 768, 256)
        J(win_v, C_BR, 16); J(win_v, C_BQ, 384); J(win_v, C_BK, 384); J(win_v, C_BV, 384); J(win_v, C_BV + 384, 384)
        J(win_v, C_BG, 384); J(win_v, C_BG + 384, 384)
        J(win_v, C_AV, 384); J(win_v, C_AV + 384, 384); J(win_v, C_AU, 384); J(win_v, C_AU + 384, 384)
        J(win_v, C_AG, 384); J(win_v, C_AG + 384, 384)
        J(win_v, C_XQ, 256); J(win_v, C_XQ + 256, 256); J(win_v, C_XG, 256); J(win_v, C_XG + 256, 256)

        def proj_tok(hT, hTk, tok0, wslot, wk, ncols, ps, psk):
            for k in range(16):
                K.op("pe", lambda e, k=k: e.matmul(ps[:, 0:ncols], lhsT=hT[:, k, tok0:tok0 + 128], rhs=wslot[:, k, 0:ncols],
                                                   start=(k == 0), stop=(k == 15)),
                     reads=[hTk, wk], writes=[psk], signal=(k == 15))

        def proj_feat(hT, hTk, tok0, ntok, wslot, wk, c0, M, ps, psk):
            for k in range(16):
                K.op("pe", lambda e, k=k: e.matmul(ps[0:M, 0:ntok], lhsT=wslot[:, k, c0:c0 + M], rhs=hT[:, k, tok0:tok0 + ntok],
                                                   start=(k == 0), stop=(k == 15)),
                     reads=[hTk, wk], writes=[psk], signal=(k == 15))

        def rstd_from_ss(stt, stk, ncol_in, n, eng_sum=True):
            if ncol_in == 2:
                K.op("dve", lambda e: e.tensor_tensor(out=stt[:, 4:5], in0=stt[:, 0:1], in1=stt[:, 1:2], op=ALU.add),
                     reads=[stk], writes=[stk])
                src = stt[:, 4:5]
            else:
                src = stt[:, 0:1]
            K.op("act", lambda e: e.activation(out=stt[:, 5:6], in_=src, func=AF.Sqrt, scale=1.0 / n, bias=EPS),
                 reads=[stk], writes=[stk])
            K.op("dve", lambda e: e.reciprocal(out=stt[:, 6:7], in_=stt[:, 5:6]), reads=[stk], writes=[stk])

        def norm_gen(stack_bufs, src_list, dstT, dstk, wcol, wcolk):
            xsl, xsb_, jnk = stack_bufs
            nx, nbf = len(xsl), len(xsb_)
            tl = [(src, t, col0 + t * 128) for (src, ntiles, col0) in src_list for t in range(ntiles)]
            n = len(tl)

            def load(i):
                src, t, c0 = tl[i]
                K.dma("sp", xsl[i % nx][:], src[t * 128:(t + 1) * 128, :], writes=[f"xsl{i % nx}"])

            def front(i):
                s, sx, sq_ = i % 2, i % nx, i % nbf
                xk, bk, sk = f"xsl{sx}", f"xsb{sq_}", f"st{s}"
                jo, jk = (jnk[:], "jnk") if jnk is not None else (xsb_[sq_][:], bk)
                K.op("act", lambda e: e.activation(out=jo, in_=xsl[sx][:], func=AF.Square, accum_out=st[s][:, 0:1]),
                     reads=[xk], writes=[jk, sk])
                rstd_from_ss(st[s], sk, 1, float(D))
                K.op("pool", lambda e: e.tensor_scalar(out=xsb_[sq_][:], in0=xsl[sx][:], scalar1=st[s][:, 6:7], scalar2=1.0,
                                                       op0=ALU.mult, op1=ALU.mult),
                     reads=[xk, sk], writes=[bk])

            def transposes(i):
                sq_ = i % nbf
                for k in range(16):
                    tb = k // 8
                    K.op("pe", lambda e, k=k, tb=tb: e.transpose(out=T[tb][:, (k % 8) * 128:(k % 8 + 1) * 128],
                                                                 in_=xsb_[sq_][:, k * 128:(k + 1) * 128], identity=ident[:]),
                         reads=[f"xsb{sq_}", "ident"], writes=[TK[tb]], signal=(k % 8 == 7))

            def back(i):
                src, t, c0 = tl[i]
                for tb in range(2):
                    K.op("dve", lambda e, tb=tb: e.tensor_tensor(
                        out=dstT[:, tb * 8:(tb + 1) * 8, c0:c0 + 128], in0=T[tb][:].rearrange("p (k i) -> p k i", k=8),
                        in1=wcol[:, tb * 8:(tb + 1) * 8].unsqueeze(2).broadcast_to([128, 8, 128]), op=ALU.mult),
                         reads=[TK[tb], wcolk], writes=[dstk])

            for i in range(min(nx - 1, n)):
                load(i)
            front(0)
            yield
            transposes(0)
            yield
            for i in range(n):
                if i + 1 < n:
                    front(i + 1)
                back(i)
                if i + nx - 1 < n:
                    load(i + nx - 1)
                yield
                if i + 1 < n:
                    transposes(i + 1)
                    yield

        def transpose_to_branch(srcb, srck, nchunks, kbase, col0, eng="dve", tb=0):
            for kk in range(nchunks):
                K.op("pe", lambda e, kk=kk: e.transpose(out=T[tb][:, kk * 128:(kk + 1) * 128], in_=srcb[:, kk * 128:(kk + 1) * 128],
                                                        identity=ident[:]),
                     reads=[srck, "ident"], writes=[TK[tb]], signal=(kk == nchunks - 1))
            K.op(eng, lambda e: (e.tensor_copy if eng != "act" else e.copy)(
                out=branchT[:, kbase:kbase + nchunks, col0:col0 + 128],
                in_=T[tb][:, 0:nchunks * 128].rearrange("p (k i) -> p k i", k=nchunks)),
                 reads=[TK[tb]], writes=[f"bT{k}_{col0 // 128}" for k in range(kbase, kbase + nchunks)])

        def bkeys(k0, nk, tok0, ntok):
            return [f"bT{k}_{t}" for k in range(k0, k0 + nk) for t in range(tok0 // 128, (tok0 + ntok + 127) // 128)]

        def gate_feat(hT, hTk, blocks, wslot, wk, nchunks, kbase, sg, mode, nbank=2):
            i = 0
            for (b0, nb) in blocks:
                for kk in range(nchunks):
                    pi = i % nbank
                    proj_feat(hT, hTk, b0, nb, wslot, wk, kk * 128, 128, P[pi], PK[pi])
                    dst = branchT[:, kbase + kk, b0:b0 + nb]
                    bk = bkeys(kbase + kk, 1, b0, nb)
                    if mode == "silu":
                        K.op("act", lambda e, pi=pi: e.activation(out=sg[pi][:, 0:nb], in_=P[pi][:, 0:nb], func=AF.Silu),
                             reads=[PK[pi]], writes=[f"sg{pi}"])
                        K.op("dve", lambda e, pi=pi, dst=dst: e.tensor_tensor(out=dst, in0=dst, in1=sg[pi][:, 0:nb], op=ALU.mult),
                             reads=[f"sg{pi}"] + bk, writes=bk)
                    else:
                        K.op("dve", lambda e, pi=pi, dst=dst: e.tensor_tensor(out=dst, in0=P[pi][:, 0:nb], in1=dst, op=ALU.mult),
                             reads=[PK[pi]] + bk, writes=bk)
                    i += 1
                    yield

        def gla_phase(hT, hTk, NT, tiles, with_out, side=None):
            blocks = []
            ptoks = sum(128 for k, _ in tiles if k == "p")
            for b0 in range(0, ptoks, 512):
                blocks.append((b0, min(512, ptoks - b0), "p"))
            for k, t0 in tiles:
                if k == "s":
                    blocks.append((t0, 128, "s"))
            with ExitStack() as gs:
                dec = sb(gs, "dec", [96, 4, 24], F32)
                dtmp = sb(gs, "dtmp", [96, 4, 16], F32)
                kinT = sb(gs, "kinT", [96, 4, NT], BF16)
                koutT = sb(gs, "koutT", [96, 4, NT], BF16)
                qinT = sb(gs, "qinT", [96, 4, NT], BF16) if with_out else None
                has_s = any(k == "s" for k, _ in tiles)
                with ExitStack() as gs1:
                    G = sb(gs1, "G", [96, 4, NT + 1], F32)
                    rT = sb(gs1, "rT", [17, NT], F32)
                    etmp = [sb(gs1, f"etmp{i}", [96, 512], F32) for i in range(2)]
                    ltmp = [sb(gs1, f"ltmp{i}", [96, 512], F32) for i in range(2)]
                    cwt, cet = etmp, ltmp
                    ex1 = [sb(gs1, f"ex1_{i}", [96, 512], F32) for i in range(2)]
                    ex2 = [sb(gs1, f"ex2_{i}", [96, 512], F32) for i in range(2)]
                    gte = [sb(gs1, f"gte{i}", [96, 512], F32) for i in range(2)]
                    gtl = [sb(gs1, f"gtl{i}", [96, 512], F32) for i in range(2)]
                    K.op("pool", lambda e: e.memset(rT[:], 1.0), writes=["rTm"])
                    K.op("pool", lambda e: e.memset(G[:, :, 0:1], 0.0), writes=["Gi"])
                    NPT_ = sum(1 for k, _ in tiles if k == "p")
                    gkeys = ["Gi"] + [f"G{bi}" for bi in range(len(blocks))]

                    def gates_gen(rslot, rk):
                        i = 0
                        for bi, (b0, nb, kind) in enumerate(blocks):
                            pr = 4 + bi % 2
                            proj_feat(hT, hTk, b0, nb, rslot, rk, 0, 16, P[pr], PK[pr])
                            K.op("act", lambda e, pr=pr, b0=b0, nb=nb: e.copy(out=rT[0:16, b0:b0 + nb], in_=P[pr][0:16, 0:nb]),
                                 reads=[PK[pr], "rTm"], writes=[f"rT{bi}"])
                        for bi, (b0, nb, kind) in enumerate(blocks):
                            for h in range(4):
                                pi = 2 + i % 2
                                ti = i % 2
                                K.op("pe", lambda e, pi=pi, h=h, b0=b0, nb=nb: e.matmul(P[pi][0:96, 0:nb], lhsT=bwa[0:17, h * 96:(h + 1) * 96],
                                                                                       rhs=rT[0:17, b0:b0 + nb], start=True, stop=True),
                                     reads=["bwa", "rTm", f"rT{bi}"], writes=[PK[pi]])
                                K.op("act", lambda e, pi=pi, ti=ti, nb=nb: e.activation(out=gte[ti][:, 0:nb], in_=P[pi][0:96, 0:nb], func=AF.Exp,
                                                                                       scale=-1.0),
                                     reads=[PK[pi]], writes=[f"gte{ti}"])
                                K.op("act", lambda e, ti=ti, nb=nb: e.activation(out=gtl[ti][:, 0:nb], in_=gte[ti][:, 0:nb], func=AF.Ln, bias=1.0),
                                     reads=[f"gte{ti}"], writes=[f"gtl{ti}"])
                                K.op("dve", lambda e, ti=ti, h=h, b0=b0, nb=nb: e.tensor_tensor_scan(
                                    out=G[:, h, 1 + b0:1 + b0 + nb], data0=ones_f[0:96, 0:nb], data1=gtl[ti][:, 0:nb],
                                    initial=G[:, h, b0:b0 + 1], op0=ALU.mult, op1=ALU.subtract),
                                     reads=[f"gtl{ti}", "ones_f", gkeys[bi]], writes=[gkeys[bi + 1]])
                                i += 1
                            yield
                        if NPT_:
                            K.op("dve", lambda e: e.tensor_tensor(out=dtmp[:, :, 0:NPT_], in0=G[:, :, 128:128 * NPT_ + 1:128],
                                                                  in1=G[:, :, 0:128 * (NPT_ - 1) + 1:128], op=ALU.subtract),
                                 reads=gkeys, writes=["dtmp"])
                            K.op("act", lambda e: e.activation(out=dec[:, :, 0:NPT_], in_=dtmp[:, :, 0:NPT_], func=AF.Exp, scale=1.0 / 16),
                                 reads=["dtmp"], writes=["dec"])
                        for ci, (kind, t0) in enumerate(tiles):
                            if kind == "s":
                                K.op("dve", lambda e, t0=t0: e.tensor_tensor(out=dtmp[:, :, 0:16], in0=G[:, :, t0 + 8:t0 + 129:8],
                                                                             in1=G[:, :, t0:t0 + 121:8], op=ALU.subtract),
                                     reads=gkeys, writes=["dtmp"])
                                K.op("act", lambda e: e.activation(out=dec[:, :, 8:24], in_=dtmp[:, :, 0:16], func=AF.Exp, scale=1.0 / 16),
                                     reads=["dtmp"], writes=["dec"])
                        yield

                    def rel_exps(h, bi, b0, nb, kind, ti, want):
                        gk = [gkeys[bi], gkeys[bi + 1]]
                        if kind == "p":
                            m = nb // 128
                            gin = G[:, h, 1 + b0:1 + b0 + nb].rearrange("p (m i) -> p m i", m=m)
                            gst = G[:, h, b0:b0 + nb].rearrange("p (m i) -> p m i", m=m)[:, :, 0:1].broadcast_to([96, m, 128])
                            gen = G[:, h, b0 + 128:b0 + nb + 1:128].unsqueeze(2).broadcast_to([96, m, 128])
                            shp = "p (m i) -> p m i"
                            kw = dict(m=m)
                        else:
                            gin = G[:, h, 1 + b0:1 + b0 + 128].rearrange("p (m i) -> p m i", m=16)
                            gst = G[:, h, b0:b0 + 128].rearrange("p (m i) -> p m i", m=16)[:, :, 0:1].broadcast_to([96, 16, 8])
                            gen = G[:, h, b0 + 8:b0 + 129:8].unsqueeze(2).broadcast_to([96, 16, 8])
                            shp = "p (m i) -> p m i"
                            kw = dict(m=16)
                        K.op("dve", lambda e: e.tensor_tensor(out=cwt[ti][:, 0:nb].rearrange(shp, **kw), in0=gin, in1=gst, op=ALU.subtract),
                             reads=gk, writes=[f"etmp{ti}"])
                        if want == "q":
                            K.op("act", lambda e: e.activation(out=ex1[ti][:, 0:nb], in_=cwt[ti][:, 0:nb], func=AF.Exp, scale=1.0 / 16),
                                 reads=[f"etmp{ti}"], writes=[f"ex1_{ti}"])
                        else:
                            K.op("dve", lambda e: e.tensor_tensor(out=cet[ti][:, 0:nb].rearrange(shp, **kw), in0=gen, in1=gin, op=ALU.subtract),
                                 reads=gk, writes=[f"ltmp{ti}"])
                            K.op("act", lambda e: e.activation(out=ex1[ti][:, 0:nb], in_=cwt[ti][:, 0:nb], func=AF.Exp, scale=-1.0 / 16),
                                 reads=[f"etmp{ti}"], writes=[f"ex1_{ti}"])
                            K.op("act", lambda e: e.activation(out=ex2[ti][:, 0:nb], in_=cet[ti][:, 0:nb], func=AF.Exp, scale=1.0 / 16),
                                 reads=[f"ltmp{ti}"], writes=[f"ex2_{ti}"])

                    def qk_gen(which, wslot, wk):
                        i = 0
                        for bi, (b0, nb, kind) in enumerate(blocks):
                            for h in range(4):
                                pi = i % 2
                                ti = i % 2
                                proj_feat(hT, hTk, b0, nb, wslot, wk, h * 96, 96, P[pi], PK[pi])
                                rel_exps(h, bi, b0, nb, kind, ti, which)
                                if which == "q":
                                    K.op("dve", lambda e, pi=pi, ti=ti, h=h, b0=b0, nb=nb: e.scalar_tensor_tensor(
                                        out=qinT[:, h, b0:b0 + nb], in0=P[pi][0:96, 0:nb], scalar=96.0 ** -0.5, in1=ex1[ti][:, 0:nb],
                                        op0=ALU.mult, op1=ALU.mult),
                                         reads=[PK[pi], f"ex1_{ti}"], writes=["qinT"])
                                else:
                                    K.op("dve", lambda e, pi=pi, ti=ti, h=h, b0=b0, nb=nb: e.tensor_tensor(
                                        out=kinT[:, h, b0:b0 + nb], in0=P[pi][0:96, 0:nb], in1=ex1[ti][:, 0:nb], op=ALU.mult),
                                         reads=[PK[pi], f"ex1_{ti}"], writes=["kinT"])
                                    K.op("dve", lambda e, pi=pi, ti=ti, h=h, b0=b0, nb=nb: e.tensor_tensor(
                                        out=koutT[:, h, b0:b0 + nb], in0=P[pi][0:96, 0:nb], in1=ex2[ti][:, 0:nb], op=ALU.mult),
                                         reads=[PK[pi], f"ex2_{ti}"], writes=["koutT"])
                                i += 1
                                yield

                    def run1(*specs):
                        specs = [[g, n] for (g, n) in specs]
                        while specs:
                            for sp_ in list(specs):
                                g, n = sp_
                                for _ in range(n):
                                    try:
                                        next(g)
                                    except StopIteration:
                                        specs.remove(sp_)
                                        break

                    rslot, rk = wget()
                    wprefetch()
                    gg = gates_gen(rslot, rk)
                    next(gg)
                    w1slot, w1k = wget(keep=1)
                    if side is not None:
                        run1((qk_gen("q" if with_out else "k", w1slot, w1k), 4), (gg, 1), (side, 1))
                    else:
                        run1((qk_gen("q" if with_out else "k", w1slot, w1k), 4), (gg, 1))
                    wdone()
                    wprefetch()
                    if with_out:
                        w2slot, w2k = wget()
                        wprefetch()
                        run1((qk_gen("k", w2slot, w2k), 1))

                    K.barrier()

                with ExitStack() as gs2:
                    NTL = len(tiles)
                    NPT = sum(1 for k, _ in tiles if k == "p")
                    pidx = {}
                    for ci_, (k_, _) in enumerate(tiles):
                        if k_ == "p":
                            pidx[ci_] = len(pidx)
                    vb_all = [sb(gs2, f"vball{i}", [128, NTL, 384], BF16) for i in range(2)]
                    kob_all = [sb(gs2, f"koball{i}", [128, NTL, 192], BF16) for i in range(2)]
                    Sb_all = [sb(gs2, f"Sball{i}", [96, NPT, 2, 192], BF16) for i in range(2)]
                    if with_out:
                        sTb_all = [sb(gs2, f"sTball{i}", [128, NTL, 2, 128], BF16) for i in range(2)]
                        ob = [sb(gs2, f"ob{i}", [128, 384], BF16) for i in range(2)]
                        sg = [sb(gs2, f"sg{i}", [128, 512], BF16) for i in range(2)]
                    if has_s:
                        Qpad = sb(gs2, "Qpad", [96, 2, 2304], BF16)
                        S0f = [sb(gs2, f"S0f{i}", [96, 2, 192], F32) for i in range(4)]
                        S0b = [sb(gs2, f"S0b{i}", [96, 2, 192], BF16) for i in range(4)]
                        Vpad = [sb(gs2, f"Vpad{i}", [128, 384], BF16) for i in range(2)]

                    def state_step(hp, ci, pbank, pbk):
                        vk, kk = f"vb{hp}_{ci}", f"kob{hp}_{ci}"
                        pk_ = pidx[ci]
                        for hh in range(2):
                            K.op("pe", lambda e, hh=hh: e.matmul(pbank[0:96, hh * 192:(hh + 1) * 192],
                                                                lhsT=kob_all[hp][:, ci, hh * 96:(hh + 1) * 96],
                                                                rhs=vb_all[hp][:, ci, hh * 192:(hh + 1) * 192], start=True, stop=True),
                                 reads=[kk, vk], writes=[pbk], signal=(hh == 1))

                        def post():
                            for hh in range(2):
                                h = 2 * hp + hh
                                K.op("dve", lambda e, hh=hh, h=h: e.scalar_tensor_tensor(
                                    out=Sst[:, h, :], in0=Sst[:, h, :], scalar=dec[:, h, pk_:pk_ + 1], in1=pbank[0:96, hh * 192:(hh + 1) * 192],
                                    op0=ALU.mult, op1=ALU.add),
                                     reads=[f"Sst{h}", "dec", pbk], writes=[f"Sst{h}"])
                            if pk_ + 1 < NPT:
                                K.op("act", lambda e: e.copy(out=Sb_all[hp][:, pk_ + 1], in_=Sst[:, 2 * hp:2 * hp + 2, :]),
                                     reads=[f"Sst{2 * hp}", f"Sst{2 * hp + 1}"], writes=[f"Sb{hp}_{pk_ + 1}"])
                        return post

                    def stage1(hp, wslot, wk, do_state):
                        K.op("act", lambda e: e.copy(out=Sb_all[hp][:, 0], in_=Sst[:, 2 * hp:2 * hp + 2, :]),
                             reads=[f"Sst{2 * hp}", f"Sst{2 * hp + 1}"], writes=[f"Sb{hp}_0"])
                        for ci, (kind, t0) in enumerate(tiles):
                            pb = ci % 2
                            vk, kk, sk = f"vb{hp}_{ci}", f"kob{hp}_{ci}", f"sTb{hp}_{ci}"
                            proj_tok(hT, hTk, t0, wslot, wk, 384, P[pb], PK[pb])
                            for hh in range(2):
                                h = 2 * hp + hh
                                K.op("pe", lambda e, hh=hh, h=h, t0=t0: e.transpose(out=T[1][:, hh * 96:(hh + 1) * 96],
                                                                                    in_=koutT[:, h, t0:t0 + 128], identity=ident[0:96, 0:96]),
                                     reads=["koutT", "ident"], writes=[TK[1]], signal=(hh == 1))
                            if with_out:
                                for hh in range(2):
                                    h = 2 * hp + hh
                                    K.op("pe", lambda e, hh=hh, h=h, t0=t0: e.matmul(P[2][:, hh * 128:(hh + 1) * 128],
                                                                                    lhsT=kinT[:, h, t0:t0 + 128], rhs=qinT[:, h, t0:t0 + 128],
                                                                                    start=True, stop=True),
                                         reads=["kinT", "qinT"], writes=[PK[2]], signal=(hh == 1))
                            post = None
                            if do_state and ci >= 1 and tiles[ci - 1][0] == "p":
                                post = state_step(hp, ci - 1, P[3], PK[3])
                            K.op("act", lambda e, ci=ci, pb=pb: e.copy(out=vb_all[hp][:, ci, :], in_=P[pb][:, 0:384]), reads=[PK[pb]], writes=[vk])
                            K.op("dve", lambda e, ci=ci: e.tensor_copy(out=kob_all[hp][:, ci, :], in_=T[1][:, 0:192]), reads=[TK[1]], writes=[kk])
                            if with_out:
                                msk = maskT if kind == "p" else maskS
                                K.op("dve", lambda e, ci=ci, msk=msk: e.tensor_tensor(
                                    out=sTb_all[hp][:, ci], in0=P[2][:, 0:256].rearrange("p (h i) -> p h i", h=2),
                                    in1=msk[:].unsqueeze(1).broadcast_to([128, 2, 128]), op=ALU.mult),
                                     reads=[PK[2], "maskT", "maskS"], writes=[sk])
                            if post:
                                post()
                            yield
                        if do_state and tiles[NTL - 1][0] == "p":
                            state_step(hp, NTL - 1, P[3], PK[3])()
                        yield

                    def onorm_pre(hp, ci, ops_banks):
                        sti, stk = st[2 + ci % 2], f"st{2 + ci % 2}"
                        oi = ci % 2
                        for hh, (pb, pbk, off) in enumerate(ops_banks):
                            K.op("act", lambda e, pb=pb, off=off, hh=hh: e.activation(
                                out=ob[oi][:, hh * 192:(hh + 1) * 192], in_=pb[:, off:off + 192], func=AF.Square,
                                accum_out=sti[:, hh:hh + 1]),
                                 reads=[pbk], writes=[f"ob{oi}", stk])
                        K.op("pool", lambda e: e.tensor_scalar(out=sti[:, 2:4], in0=sti[:, 0:2], scalar1=1.0 / 192, scalar2=EPS,
                                                               op0=ALU.mult, op1=ALU.add),
                             reads=[stk], writes=[stk])
                        K.op("pool", lambda e: e.tensor_tensor(out=sti[:, 4:6], in0=sti[:, 2:4], in1=mhalf[:, 0:2], op=ALU.pow),
                             reads=[stk, "mhalf"], writes=[stk])
                        for hh, (pb, pbk, off) in enumerate(ops_banks):
                            K.op("dve", lambda e, pb=pb, off=off, hh=hh: e.scalar_tensor_tensor(
                                out=ob[oi][:, hh * 192:(hh + 1) * 192], in0=pb[:, off:off + 192], scalar=sti[:, 4 + hh:5 + hh],
                                in1=wo_b[:], op0=ALU.mult, op1=ALU.mult),
                                 reads=[pbk, stk, "wo_b"], writes=[f"ob{oi}"])

                    def onorm_post(hp, ci, t0):
                        transpose_to_branch(ob[ci % 2], f"ob{ci % 2}", 3, 6 + 3 * hp, t0, eng="act", tb=0)

                    def stage2(hp, prog):
                        prev = None
                        for ci, (kind, t0) in enumerate(tiles):
                            if kind != "p":
                                continue
                            pb = 4 + ci % 2
                            vk, sk = f"vb{hp}_{ci}", f"sTb{hp}_{ci}"
                            for hh in range(2):
                                h = 2 * hp + hh
                                K.op("pe", lambda e, hh=hh, ci=ci, pb=pb: e.matmul(P[pb][:, hh * 192:(hh + 1) * 192], lhsT=sTb_all[hp][:, ci, hh, :],
                                                                                  rhs=vb_all[hp][:, ci, hh * 192:(hh + 1) * 192], start=True, stop=False),
                                     reads=[sk, vk], writes=[PK[pb]], signal=False)
                                K.op("pe", lambda e, hh=hh, h=h, t0=t0, ci=ci, pb=pb: e.matmul(P[pb][:, hh * 192:(hh + 1) * 192],
                                                                                              lhsT=qinT[:, h, t0:t0 + 128], rhs=Sb_all[hp][:, pidx[ci], hh, :],
                                                                                              start=False, stop=True),
                                     reads=["qinT", f"Sb{hp}_{pidx[ci]}"], writes=[PK[pb]], signal=True)
                            post = state_step(hp, ci, P[3], PK[3])
                            if prev is not None:
                                onorm_post(hp, *prev)
                                prog.add(prev[1])
                            post()
                            onorm_pre(hp, ci, [(P[pb], PK[pb], 0), (P[pb], PK[pb], 192)])
                            prev = (ci, t0)
                            yield
                        onorm_post(hp, *prev)
                        prog.add(prev[1])
                        K.dma("sp", dr["sp"][2 * hp:2 * hp + 2].rearrange("h d e -> d h e"), Sst[:, 2 * hp:2 * hp + 2, :],
                              reads=[f"Sst{2 * hp}", f"Sst{2 * hp + 1}"], semkey="Sst")
                        yield

                    def stage_s(hp):
                        ci = [i for i, (k, _) in enumerate(tiles) if k == "s"][0]
                        t0 = tiles[ci][1]
                        vk, kk, sk = f"vb{hp}_{ci}", f"kob{hp}_{ci}", f"sTb{hp}_{ci}"

                        def load(s):
                            K.dma("sp", S0f[s % 4][:], dr["s0"][s, 2 * hp:2 * hp + 2].rearrange("h d e -> d h e"), writes=[f"S0f{s % 4}"])

                        def prep(s):
                            K.op("act", lambda e: e.copy(out=S0b[s % 4][:], in_=S0f[s % 4][:]), reads=[f"S0f{s % 4}"], writes=[f"S0b{s % 4}"])
                            K.op("dve", lambda e: e.tensor_scalar(out=Vpad[s % 2][:], in0=vb_all[hp][:, ci, :], scalar1=blockmask[:, s:s + 1],
                                                                  scalar2=None, op0=ALU.mult),
                                 reads=[vk, "blockmask"], writes=[f"Vpad{s % 2}"])

                        load(0)
                        load(1)
                        K.op("pool", lambda e: e.memset(Qpad[:], 0.0), writes=["Qpad"])
                        for hh in range(2):
                            h = 2 * hp + hh
                            K.op("pool", lambda e, hh=hh, h=h: e.tensor_copy(
                                out=Qpad[:, hh, :].rearrange("p (s c) -> p s c", c=144)[:, :, 0:8],
                                in_=qinT[:, h, t0:t0 + 128].rearrange("p (s i) -> p s i", i=8)),
                                 reads=["qinT"], writes=["Qpad"])
                        prep(0)
                        for hh in range(2):
                            K.op("pe", lambda e, hh=hh: e.matmul(P[4 + hh][:, 0:192], lhsT=sTb_all[hp][:, ci, hh, :],
                                                                rhs=vb_all[hp][:, ci, hh * 192:(hh + 1) * 192], start=True, stop=False),
                                 reads=[sk, vk], writes=[PK[4 + hh]], signal=False)
                        yield
                        T1f = T[1][:].bitcast(F32)
                        for s in range(16):
                            fk, bk_, pk = f"S0f{s % 4}", f"S0b{s % 4}", f"Vpad{s % 2}"
                            UB, UBk = (P[3], PK[3])
                            if s + 2 < 16:
                                load(s + 2)
                            for hh in range(2):
                                K.op("pe", lambda e, hh=hh, s=s: e.matmul(
                                    P[4 + hh][:, 0:192], lhsT=Qpad[:, hh, 136 * s:136 * s + 128], rhs=S0b[s % 4][:, hh, :],
                                    start=False, stop=(s == 15)),
                                     reads=["Qpad", bk_], writes=[PK[4 + hh]], signal=True)
                            for hh in range(2):
                                K.op("pe", lambda e, hh=hh, s=s: e.matmul(
                                    UB[0:96, hh * 192:(hh + 1) * 192], lhsT=kob_all[hp][:, ci, hh * 96:(hh + 1) * 96],
                                    rhs=Vpad[s % 2][:, hh * 192:(hh + 1) * 192], start=True, stop=True),
                                     reads=[kk, pk], writes=[UBk], signal=(hh == 1))
                            if s + 1 < 16:
                                prep(s + 1)
                            for hh in range(2):
                                h = 2 * hp + hh
                                K.op("dve", lambda e, hh=hh, h=h, s=s: e.scalar_tensor_tensor(
                                    out=S0f[s % 4][:, hh, :], in0=S0f[s % 4][:, hh, :], scalar=dec[:, h, 8 + s:9 + s],
                                    in1=UB[0:96, hh * 192:(hh + 1) * 192], op0=ALU.mult, op1=ALU.add),
                                     reads=[fk, "dec", UBk], writes=[fk])
                            K.dma("sp", dr["ss"][s, 2 * hp:2 * hp + 2].rearrange("h d e -> d h e"), S0f[s % 4][:], reads=[fk])
                            yield
                        onorm_pre(hp, ci, [(P[4], PK[4], 0), (P[5], PK[5], 0)])
                        yield
                        onorm_post(hp, ci, t0)
                        yield

                    def chain(*gens):
                        for g in gens:
                            yield from g

                    def run(*specs):
                        specs = [[g, n] for (g, n) in specs]
                        while specs:
                            for sp_ in list(specs):
                                g, n = sp_
                                for _ in range(n):
                                    try:
                                        next(g)
                                    except StopIteration:
                                        specs.remove(sp_)
                                        break

                    pblocks = [(b0, nb) for (b0, nb, kind) in blocks if kind == "p"]
                    sblocks = [(b0, nb) for (b0, nb, kind) in blocks if kind == "s"]
                    if not with_out:
                        for hp in range(2):
                            wslot, wk = wget()
                            wprefetch()
                            if side is not None:
                                run((stage1(hp, wslot, wk, True), 2), (side, 1))
                            else:
                                run((stage1(hp, wslot, wk, True), 1))
                    else:
                        for hp in range(2):
                            wslot, wk = wget()
                            wprefetch()
                            g1_ = stage1(hp, wslot, wk, False)
                            next(g1_)
                            run((g1_, 1), (stage_s(hp), 2))

                        def gate_with(hp, slot, k_):
                            prog = set()
                            s2 = stage2(hp, prog)
                            for (b0, nb) in sblocks + pblocks:
                                need = {t0 for (kd, t0) in tiles if kd == "p" and b0 <= t0 < b0 + nb}
                                while not need <= prog:
                                    next(s2)
                                for _ in gate_feat(hT, hTk, [(b0, nb)], slot, k_, 3, 6 + 3 * hp, sg, "silu"):
                                    try:
                                        next(s2)
                                    except StopIteration:
                                        pass
                            for _ in s2:
                                pass

                        for hp in range(2):
                            gslot, gk_ = wget()
                            wprefetch()
                            gate_with(hp, gslot, gk_)
                    K.barrier()

        with ExitStack() as g1:
            hT = sb(g1, "hT", [128, 16, NMAIN], BF16)

            def norm_scope(src_list, wcol, wcolk, dstT, dstk):
                with ExitStack() as ns:
                    xsl = [sb(ns, f"xsl{i}", [128, D], F32) for i in range(4)]
                    xsb_ = [sb(ns, f"xsb{i}", [128, D], BF16) for i in range(3)]
                    jnk = sb(ns, "jnk", [128, D], BF16)
                    for _ in norm_gen((xsl, xsb_, jnk), src_list, dstT, dstk, wcol, wcolk):
                        pass
                    K.barrier()

            wissue_upto(1)
            with ExitStack() as cs:
                Wf = sb(cs, "Wf", [128, 4, 128], F32)
                Wsf = sb(cs, "Wsf", [128, 4, 128], F32)
                Wmb = sb(cs, "Wmb", [128, 4, 128], BF16)
                Wsmb = sb(cs, "Wsmb", [128, 4, 128], BF16)
                K.op("pool", lambda e: e.memset(Wsf[:], 0.0), writes=["Wsf"])
                ck = dict(semkey="const")
                K.dma("sp", normw_col[:], dr["norm_w"].rearrange("o (k p) -> p (o k)", p=128), writes=["normw_col"],
                      allow_slow_non_contiguous=True, **ck)
                K.dma("sp", memw_col[:], dr["mem_norm_w"].rearrange("o (k p) -> p (o k)", p=128), writes=["memw_col"],
                      allow_slow_non_contiguous=True, **ck)
                if stop == 'c2':
                    K.barrier()
                    return nc
                K.dma("sp", wo_b[:], dr["b_onorm_w"].partition_broadcast(128), writes=["wo_b"], **ck)
                K.dma("sp", abT[:], dr["a_bs"].rearrange("h i -> i h"), writes=["abT"], allow_slow_non_contiguous=True, **ck)
                K.dma("sp", bwa[0:16, :], dr["b_wa"], writes=["bwa"], **ck)
                K.dma("sp", bwa[16:17, :], dr["b_ba"], writes=["bwa"], part=True, **ck)
                if stop == 'c3':
                    K.barrier()
                    return nc
                K.dma("sp", Wf[:], dr["a_ws"].rearrange("h i j -> i h j"), writes=["Wf"], **ck)
                for s in range(16):
                    K.dma(["sp", "act"][s % 2], abTs[8 * s:8 * s + 8, :], dr["a_bs"][:, 0:8].rearrange("h i -> i h"), writes=["abTs"],
                          part=(s > 0), allow_slow_non_contiguous=True, **ck)
                    K.dma(["act", "sp"][s % 2], Wsf[8 * s:8 * s + 8, :, 8 * s:8 * s + 8], dr["a_ws"][:, 0:8, 0:8].rearrange("h i j -> i h j"),
                          writes=["Wsf"], part=(s > 0), **ck)
                if stop == 'c4':
                    K.barrier()
                    return nc
                K.seal("const")
                for src, srck, dst, dstk in ((Wf, "Wf", Wmb, "Wmb"), (Wsf, "Wsf", Wsmb, "Wsmb")):
                    K.op("pool", lambda e, src=src: e.affine_select(out=src[:], in_=src[:], pattern=[[0, 4], [-1, 128]],
                                                                    compare_op=ALU.is_ge, fill=0.0, base=0, channel_multiplier=1),
                         reads=[srck], writes=[srck])
                    K.op("pool", lambda e, src=src, dst=dst: e.tensor_copy(out=dst[:], in_=src[:]), reads=[srck], writes=[dstk])
                if stop == 'c5':
                    K.barrier()
                    return nc
                for src, srck, dst, dstk, tb in ((Wmb, "Wmb", WT, "WT", 0), (Wsmb, "Wsmb", WTs, "WTs", 1)):
                    for h in range(4):
                        K.op("pe", lambda e, src=src, h=h, tb=tb: e.transpose(out=T[tb][:, h * 128:(h + 1) * 128], in_=src[:, h, :],
                                                                             identity=ident[:]),
                             reads=[srck, "ident"], writes=[TK[tb]], signal=(h == 3))
                    K.op("dve", lambda e, dst=dst, tb=tb: e.tensor_copy(out=dst[:].rearrange("p h i -> p (h i)"), in_=T[tb][:, 0:512]),
                         reads=[TK[tb]], writes=[dstk])
                norm_scope([(dr["xpre"], 8, 0)], normw_col, "normw_col", branchT, "hTp")

            if stop == 'prenorm':
                K.barrier()
                return nc
            with ExitStack() as sn:
                xsl_s = [sb(sn, f"xsl{i}", [128, D], F32) for i in range(3)]
                xsb_s = [sb(sn, f"xsb{i}", [128, D], BF16) for i in range(2)]
                side = norm_gen((xsl_s, xsb_s, None), [(dr["xp"], 8, 0), (dr["xs"], 1, 1024)], hT, "hT", normw_col, "normw_col")
                gla_phase(branchT, "hTp", NPRE, [("p", t * 128) for t in range(8)], with_out=False, side=side)
                for _ in side:
                    pass
                K.barrier()

            if stop == 'pregla':
                K.barrier()
                return nc

            with ExitStack() as ms:
                hmT = sb(ms, "hmT", [128, 16, 256], BF16)
                stg = [sb(ms, f"stg{i}", [128, 256], F32) for i in range(2)]
                norm_scope([(dr["mem"], 2, 0)], memw_col, "memw_col", hmT, "hmT")
                if stop == 'm1':
                    K.barrier()
                    return nc
                si = 0
                for j in range(4):
                    wslot, wk = wget()
                    wprefetch()
                    isK = j < 2
                    jj = j % 2
                    for t in range(2):
                        pi = (2 * j + t) % 2
                        for k in range(16):
                            K.op("pe", lambda e, k=k, t=t, pi=pi: e.matmul(P[pi][:, 0:256], lhsT=hmT[:, k, t * 128:(t + 1) * 128],
                                                                          rhs=wslot[:, k, 0:256], start=(k == 0), stop=(k == 15)),
                                 reads=["hmT", wk], writes=[PK[pi]], signal=(k == 15))
                        sgi = si % 2
                        si += 1
                        K.op("act", lambda e, pi=pi, sgi=sgi: e.copy(out=stg[sgi][:], in_=P[pi][:, 0:256]), reads=[PK[pi]],
                             writes=[f"stg{sgi}"])
                        if True:
                            K.dma("sp", dr["mk" if isK else "mv"][t * 128:(t + 1) * 128, jj * 256:(jj + 1) * 256], stg[sgi][:],
                                  reads=[f"stg{sgi}"])
                        if not isK:
                            K.op("dve", lambda e, pi=pi, t=t, jj=jj: e.tensor_copy(out=Vb[:, t, jj * 256:(jj + 1) * 256], in_=P[pi][:, 0:256]),
                                 reads=[PK[pi]], writes=["Vb"])
                    if stop == 'm2':
                        K.barrier()
                        return nc
                    if isK:
                        for hh in range(2):
                            pi = 2 + hh
                            proj_feat(hmT, "hmT", 0, 256, wslot, wk, hh * 128, 128, P[pi], PK[pi])
                            K.op("dve", lambda e, pi=pi, hh=hh, jj=jj: e.tensor_copy(out=KT[:, 2 * jj + hh, :], in_=P[pi][:, 0:256]),
                                 reads=[PK[pi]], writes=["KT"])
                    if stop == f"mj{j}":
                        K.barrier()
                        return nc
                K.barrier()

            if stop == 'memkv':
                K.barrier()
                return nc


            if stop == 'mainnorm':
                K.barrier()
                return nc
            main_tiles = [("s", 1024)] + [("p", t * 128) for t in range(8)]
            gla_phase(hT, "hT", NMAIN, main_tiles, with_out=True)

            if stop == 'maingla':
                K.barrier()
                return nc

            mblocks = [(0, 512), (512, 512), (1024, 128)]
            late = g1.enter_context(ExitStack())
            wout = sb(late, "wout", [128, 16, D], BF16)

            with ExitStack() as as_:
                wv_b = sb(as_, "wv_b", [128, 768], F32)
                vnb = [sb(as_, f"vnb{i}", [128, 768], BF16) for i in range(2)]
                vnf = sb(as_, "vnf", [128, 768], F32)
                mixb = [sb(as_, f"mixb{i}", [128, 768], BF16) for i in range(2)]
                sgA = [sb(as_, f"sg{i}", [128, 512], BF16) for i in range(4)]
                K.dma("sp", wv_b[:], dr["a_vnorm_w"].partition_broadcast(128), writes=["wv_b"])
                (w0, w0k), (w1, w1k) = wget(2)
                for q in range(4):
                    K.dma("pool", wout[:, :, q * 512:(q + 1) * 512], wout_v[:, :, q * 512:(q + 1) * 512], writes=[f"wout{q}"])
                NTm = len(main_tiles)

                def a_proj_pe(ci):
                    kind, t0 = main_tiles[ci]
                    pv = [0, 1] if ci % 2 == 0 else [4, 5]
                    proj_tok(hT, "hT", t0, w0, w0k, 384, P[pv[0]], PK[pv[0]])
                    proj_tok(hT, "hT", t0, w1, w1k, 384, P[pv[1]], PK[pv[1]])

                def a_proj_post(ci):
                    kind, t0 = main_tiles[ci]
                    pv = [0, 1] if ci % 2 == 0 else [4, 5]
                    vi = ci % 2
                    sti, stk = st[ci % 2], f"st{ci % 2}"
                    for half in range(2):
                        K.op("act", lambda e, half=half: e.activation(
                            out=sgA[vi][:, 0:384], in_=P[pv[half]][:, 0:384], func=AF.Square, accum_out=sti[:, half:half + 1]),
                             reads=[PK[pv[half]]], writes=[f"sg{vi}", stk])
                    rstd_from_ss(sti, stk, 2, 768.0)
                    for half in range(2):
                        if kind == "s":
                            K.op("dve", lambda e, half=half: e.scalar_tensor_tensor(
                                out=vnf[:, half * 384:(half + 1) * 384], in0=P[pv[half]][:, 0:384], scalar=sti[:, 6:7],
                                in1=wv_b[:, half * 384:(half + 1) * 384], op0=ALU.mult, op1=ALU.mult),
                                 reads=[PK[pv[half]], stk, "wv_b"], writes=["vnf"])
                            K.op("act", lambda e, half=half: e.copy(out=vnb[vi][:, half * 384:(half + 1) * 384],
                                                                    in_=vnf[:, half * 384:(half + 1) * 384]),
                                 reads=["vnf"], writes=[f"vnb{vi}"])
                        else:
                            K.op("dve", lambda e, half=half: e.scalar_tensor_tensor(
                                out=vnb[vi][:, half * 384:(half + 1) * 384], in0=P[pv[half]][:, 0:384], scalar=sti[:, 6:7],
                                in1=wv_b[:, half * 384:(half + 1) * 384], op0=ALU.mult, op1=ALU.mult),
                                 reads=[PK[pv[half]], stk, "wv_b"], writes=[f"vnb{vi}"])
                    if kind == "s":
                        K.dma("sp", dr["cvs"][:, :], vnf[:], reads=["vnf"])

                def a_mix_pe(ci):
                    kind, t0 = main_tiles[ci]
                    vi = ci % 2
                    Wm, Wmk = (WT, "WT") if kind == "p" else (WTs, "WTs")
                    for h in range(4):
                        pb = 2 + h // 2
                        K.op("pe", lambda e, h=h, pb=pb: e.matmul(P[pb][:, (h % 2) * 192:(h % 2 + 1) * 192], lhsT=Wm[:, h, :],
                                                                 rhs=vnb[vi][:, h * 192:(h + 1) * 192], start=True, stop=True),
                             reads=[Wmk, f"vnb{vi}"], writes=[PK[pb]], signal=(h % 2 == 1))

                def a_mix_post(ci):
                    kind, t0 = main_tiles[ci]
                    vi = ci % 2
                    ab = abT if kind == "p" else abTs
                    for h in range(4):
                        pb = 2 + h // 2
                        K.op("dve", lambda e, h=h, pb=pb: e.tensor_scalar(
                            out=mixb[vi][:, h * 192:(h + 1) * 192], in0=P[pb][:, (h % 2) * 192:(h % 2 + 1) * 192],
                            scalar1=ab[:, h:h + 1], scalar2=None, op0=ALU.add),
                             reads=[PK[pb], "abT", "abTs"], writes=[f"mixb{vi}"])

                def a_tr(ci):
                    kind, t0 = main_tiles[ci]
                    transpose_to_branch(mixb[ci % 2], f"mixb{ci % 2}", 6, 0, t0, eng="act", tb=ci % 2)

                for step in range(NTm + 2):
                    if step < NTm:
                        a_proj_pe(step)
                        a_proj_post(step)
                    if 0 <= step - 1 < NTm:
                        a_mix_pe(step - 1)
                        a_mix_post(step - 1)
                    if 0 <= step - 2 < NTm:
                        a_tr(step - 2)
                wdone()
                wprefetch()
                for j in range(2):
                    wslot, wk = wget()
                    wprefetch()
                    for _ in gate_feat(hT, "hT", mblocks, wslot, wk, 3, 3 * j, sgA, "mul", nbank=4):
                        pass
                for j in range(2):
                    wslot, wk = wget()
                    wprefetch()
                    for _ in gate_feat(hT, "hT", mblocks, wslot, wk, 3, 3 * j, sgA, "silu", nbank=4):
                        pass
                K.barrier()

            with ExitStack() as xs_:
                qxT = sb(xs_, "qxT", [128, 4, NMAIN], BF16)
                pT = [sb(xs_, f"pT{i}", [128, 2, 512], BF16) for i in range(2)]
                rinv = sb(xs_, "rinv", [128, 512], F32)
                rprod = sb(xs_, "rprod", [128, 512], BF16)
                sgX = [sb(xs_, f"sg{i}", [128, 512], BF16) for i in range(2)]
                ckb = [sb(xs_, f"ckb{i}", [128, 2, 512], BF16) for i in range(2)]
                cvb = [sb(xs_, f"cvb{i}", [128, 2, 512], BF16) for i in range(3)]
                KTs = [sb(xs_, f"KTs{i}", [128, 4, 256], BF16) for i in range(2)]
                pTs = pT[0]
                scale = 128.0 ** -0.5
                t0 = 1024
                qslots = wget(2)
                qcnt = [0]

                def q_step(j, b0, nb, hh):
                    h = 2 * j + hh
                    pi = qcnt[0] % 2
                    qcnt[0] += 1
                    proj_feat(hT, "hT", b0, nb, qslots[j][0], qslots[j][1], hh * 128, 128, P[pi], PK[pi])
                    K.op("act", lambda e: e.copy(out=qxT[:, h, b0:b0 + nb], in_=P[pi][:, 0:nb]),
                         reads=[PK[pi]], writes=[f"qxT{h}_{b0}"])

                def q_prompt_gen():
                    for j in range(2):
                        for (b0, nb) in mblocks[:2]:
                            for hh in range(2):
                                q_step(j, b0, nb, hh)
                                yield

                its = [(b0, nb, h) for (b0, nb) in mblocks[:2] for h in range(4)]

                def x_scores(i):
                    b0, nb, h = its[i]
                    for c in range(2):
                        K.op("pe", lambda e, c=c: e.matmul(P[4 + c][:, 0:nb], lhsT=KT[:, h, c * 128:(c + 1) * 128],
                                                          rhs=qxT[:, h, b0:b0 + nb], start=True, stop=True),
                             reads=["KT", f"qxT{h}_{b0}"], writes=[PK[4 + c]])

                def x_exp(i):
                    b0, nb, h = its[i]
                    pi = i % 2
                    for c in range(2):
                        K.op("act", lambda e, c=c: e.activation(out=pT[pi][:, c, 0:nb], in_=P[4 + c][:, 0:nb], func=AF.Exp, scale=scale),
                             reads=[PK[4 + c]], writes=[f"pT{pi}"])

                def x_pv(i):
                    b0, nb, h = its[i]
                    pi = i % 2
                    for c in range(2):
                        K.op("pe", lambda e, c=c: e.matmul(P[2][:, 0:nb], lhsT=Vb[:, c, h * 128:(h + 1) * 128],
                                                          rhs=pT[pi][:, c, 0:nb], start=(c == 0), stop=(c == 1)),
                             reads=["Vb", f"pT{pi}"], writes=[PK[2]], signal=(c == 1))
                    for c in range(2):
                        K.op("pe", lambda e, c=c: e.matmul(P[3][:, 0:nb], lhsT=ones_b[:], rhs=pT[pi][:, c, 0:nb],
                                                          start=(c == 0), stop=(c == 1)),
                             reads=["ones_b", f"pT{pi}"], writes=[PK[3]], signal=(c == 1))

                def x_fin(i):
                    b0, nb, h = its[i]
                    K.op("dve", lambda e: e.reciprocal(out=rinv[:, 0:nb], in_=P[3][:, 0:nb]), reads=[PK[3]], writes=["rinv"])
                    K.op("dve", lambda e: e.tensor_tensor(out=branchT[:, 12 + h, b0:b0 + nb], in0=P[2][:, 0:nb], in1=rinv[:, 0:nb],
                                                          op=ALU.mult),
                         reads=[PK[2], "rinv"], writes=bkeys(12 + h, 1, b0, nb))

                xp_prog = set()

                def xp_gen():
                    for step in range(len(its) + 1):
                        if step < len(its):
                            x_scores(step)
                        if step >= 1:
                            x_pv(step - 1)
                        if step < len(its):
                            x_exp(step)
                        if step >= 1:
                            x_fin(step - 1)
                            xp_prog.add((its[step - 1][0], its[step - 1][2]))
                        yield

                def xs_load(s):
                    K.dma("pool", ckb[s % 2][:], dr["ck"][s].rearrange("(c p) f -> p c f", p=128), writes=[f"ckb{s % 2}"])
                    K.dma("pool", cvb[s % 3][:], dr["cv"][s].rearrange("(c p) f -> p c f", p=128), writes=[f"cvb{s % 3}"])

                def xs_tr(s):
                    si_ = s % 2
                    for h in range(4):
                        for c in range(2):
                            K.op("pe", lambda e, h=h, c=c: e.transpose(out=T[si_][:, (h * 2 + c) * 128:(h * 2 + c + 1) * 128],
                                                                       in_=ckb[s % 2][:, c, h * 128:(h + 1) * 128], identity=ident[:]),
                                 reads=[f"ckb{s % 2}", "ident"], writes=[TK[si_]], signal=(h == 3 and c == 1))

                def xs_tr_post(s):
                    si_ = s % 2
                    K.op("dve", lambda e: e.tensor_copy(out=KTs[si_][:].rearrange("p h n -> p (h n)"), in_=T[si_][:, 0:1024]),
                         reads=[TK[si_]], writes=[f"KTs{si_}"])

                def xs_scores(s):
                    si_ = s % 2
                    for h in range(4):
                        for c in range(2):
                            K.op("pe", lambda e, h=h, c=c: e.matmul(
                                P[4 + si_][:, c * 32 + h * 8:c * 32 + h * 8 + 8], lhsT=KTs[si_][:, h, c * 128:(c + 1) * 128],
                                rhs=qxT[:, h, t0 + 8 * s:t0 + 8 * s + 8], start=True, stop=True),
                                 reads=[f"KTs{si_}", f"qxT{h}_{t0}"], writes=[PK[4 + si_]], signal=(h == 3 and c == 1))

                def xs_exp(s):
                    si_ = s % 2
                    K.op("act", lambda e: e.activation(out=pTs[:, :, s * 32:(s + 1) * 32],
                                                       in_=P[4 + si_][:, 0:64].rearrange("p (c x) -> p c x", c=2), func=AF.Exp, scale=scale),
                         reads=[PK[4 + si_]], writes=[f"pTs{s}", "pT0"])

                def xs_pv(s):
                    for h in range(4):
                        for c in range(2):
                            K.op("pe", lambda e, h=h, c=c: e.matmul(
                                P[2][:, s * 32 + h * 8:s * 32 + h * 8 + 8], lhsT=cvb[s % 3][:, c, h * 128:(h + 1) * 128],
                                rhs=pTs[:, c, s * 32 + h * 8:s * 32 + h * 8 + 8], start=(c == 0), stop=(c == 1)),
                                 reads=[f"cvb{s % 3}", f"pTs{s}"], writes=[PK[2]], signal=(h == 3 and c == 1))

                def xs_gen():
                    xs_load(0)
                    for step in range(16 + 2):
                        if step < 16:
                            xs_tr(step)
                        if 0 <= step - 1 < 16:
                            xs_scores(step - 1)
                        if 0 <= step - 2 < 16:
                            xs_pv(step - 2)
                        if step + 1 < 16:
                            xs_load(step + 1)
                        if step < 16:
                            xs_tr_post(step)
                        if 0 <= step - 1 < 16:
                            xs_exp(step - 1)
                        yield
                    for c in range(2):
                        K.op("pe", lambda e, c=c: e.matmul(P[3][:, 0:512], lhsT=ones_b[:], rhs=pTs[:, c, :], start=(c == 0), stop=(c == 1)),
                             reads=["ones_b"] + [f"pTs{s}" for s in range(16)], writes=[PK[3]], signal=(c == 1))
                    K.op("dve", lambda e: e.reciprocal(out=rinv[:], in_=P[3][:, 0:512]), reads=[PK[3]], writes=["rinv"])
                    K.op("dve", lambda e: e.tensor_tensor(out=rprod[:], in0=P[2][:, 0:512], in1=rinv[:], op=ALU.mult),
                         reads=[PK[2], "rinv"], writes=["rprod"])
                    for h in range(4):
                        K.op("dve", lambda e, h=h: e.tensor_copy(
                            out=branchT[:, 12 + h, t0:t0 + 128].rearrange("p (s i) -> p s i", i=8),
                            in_=rprod[:].rearrange("p (s h i) -> p s h i", s=16, h=4)[:, :, h, :]),
                             reads=["rprod"], writes=bkeys(12 + h, 1, t0, 128))
                    yield

                def runx(*specs):
                    specs = [[g, n] for (g, n) in specs]
                    while specs:
                        for sp_ in list(specs):
                            g, n = sp_
                            for _ in range(n):
                                try:
                                    next(g)
                                except StopIteration:
                                    specs.remove(sp_)
                                    break

                for j in range(2):
                    for hh in range(2):
                        q_step(j, t0, 128, hh)
                runx((q_prompt_gen(), 1), (xs_gen(), 2))
                wdone()

                xp = xp_gen()

                def gate_with_xp(j, slot, k_):
                    for (b0, nb) in [mblocks[2]] + mblocks[:2]:
                        need = {(b0, 2 * j + hh) for hh in range(2)} if b0 < 1024 else set()
                        while not need <= xp_prog:
                            next(xp)
                        for _ in gate_feat(hT, "hT", [(b0, nb)], slot, k_, 2, 12 + 2 * j, sgX, "silu"):
                            try:
                                next(xp)
                            except StopIteration:
                                pass

                for j in range(2):
                    wslot, wk = wget()
                    wprefetch()
                    gate_with_xp(j, wslot, wk)
                for _ in xp:
                    pass
                K.barrier()

            with ExitStack() as os_:
                wf_b = sb(os_, "wf_b", [128, D], F32)
                xsl = [sb(os_, "xsl0", [128, D], F32)] * 2
                ysl = [sb(os_, f"ysl{i}", [128, D], F32) for i in range(2)]
                K.dma("sp", wf_b[:], dr["final_norm_w"].partition_broadcast(128), writes=["wf_b"])
                for t in range(9):
                    s = t % 2
                    src = dr["xp"][t * 128:(t + 1) * 128, :] if t < 8 else dr["xs"][:, :]
                    dst = dr["yp"][t * 128:(t + 1) * 128, :] if t < 8 else dr["ys"][:, :]
                    xk, yk, sk = "xsl0", f"ysl{s}", f"st{s}"
                    K.dma("sp", xsl[s][:], src, writes=[xk])
                    for q in range(4):
                        pi = (t * 4 + q) % 6
                        for k in range(16):
                            K.op("pe", lambda e, k=k, q=q, pi=pi: e.matmul(P[pi][:, 0:512], lhsT=branchT[:, k, t * 128:(t + 1) * 128],
                                                                          rhs=wout[:, k, q * 512:(q + 1) * 512], start=(k == 0), stop=(k == 15)),
                                 reads=[f"bT{k}_{t}", f"wout{q}"], writes=[PK[pi]], signal=(k == 15))
                        K.op("dve", lambda e, q=q, pi=pi, s=s: e.tensor_tensor(out=ysl[s][:, q * 512:(q + 1) * 512], in0=P[pi][:, 0:512],
                                                                              in1=xsl[s][:, q * 512:(q + 1) * 512], op=ALU.add),
                             reads=[PK[pi], xk], writes=[yk])
                    K.op("act", lambda e, s=s: e.activation(out=xsl[s][:], in_=ysl[s][:], func=AF.Square, accum_out=st[s][:, 0:1]),
                         reads=[yk], writes=[xk, sk])
                    rstd_from_ss(st[s], sk, 1, float(D))
                    K.op("dve", lambda e, s=s: e.scalar_tensor_tensor(out=ysl[s][:], in0=ysl[s][:], scalar=st[s][:, 6:7], in1=wf_b[:],
                                                                      op0=ALU.mult, op1=ALU.mult),
                         reads=[yk, sk, "wf_b"], writes=[yk])
                    K.dma("sp", dst, ysl[s][:], reads=[yk])
                K.barrier()
    return nc


_NC_CACHE = {}


def kernel(x_prompt, x_sample, mem_prompt, state_gla, cache_mem_k, cache_mem_v,
           norm_w, w_in, a_vnorm_w, a_ws, a_bs, b_wa, b_ba, b_onorm_w,
           mem_norm_w, w_mem_kv, w_out, final_norm_w):
    f = lambda a: np.ascontiguousarray(np.asarray(a, dtype=np.float32))
    x_prompt, x_sample, mem_prompt = f(x_prompt), f(x_sample), f(mem_prompt)
    state_gla, cache_mem_k, cache_mem_v = f(state_gla), f(cache_mem_k), f(cache_mem_v)
    shared = {
        "norm_w": f(norm_w).reshape(1, D), "w_in": f(w_in).reshape(D, DIN), "a_vnorm_w": f(a_vnorm_w).reshape(1, 768),
        "a_ws": f(a_ws).reshape(4, 128, 128), "a_bs": f(a_bs).reshape(4, 128), "b_wa": f(b_wa).reshape(16, 384),
        "b_ba": f(b_ba).reshape(1, 384), "b_onorm_w": f(b_onorm_w).reshape(1, 192), "mem_norm_w": f(mem_norm_w).reshape(1, D),
        "w_mem_kv": f(w_mem_kv).reshape(D, 1024), "w_out": f(w_out).reshape(D, D), "final_norm_w": f(final_norm_w).reshape(1, D),
    }
    zeros_pre = np.zeros((1024, D), np.float32)
    in_maps = []
    for c in range(8):
        b, half = c // 2, c % 2
        m = dict(shared)
        m["xp"] = np.ascontiguousarray(x_prompt[b, half * 1024:(half + 1) * 1024])
        m["xpre"] = np.ascontiguousarray(x_prompt[b, 0:1024]) if half == 1 else zeros_pre
        m["xs"] = np.ascontiguousarray(x_sample[16 * c:16 * c + 16].reshape(128, D))
        m["mem"] = np.ascontiguousarray(mem_prompt[b])
        m["s0"] = np.ascontiguousarray(state_gla[0, 16 * c:16 * c + 16])
        m["ck"] = np.ascontiguousarray(cache_mem_k[0, 16 * c:16 * c + 16].reshape(16, 256, 512))
        m["cv"] = np.ascontiguousarray(cache_mem_v[0, 16 * c:16 * c + 16].reshape(16, 256, 512))
        in_maps.append(m)
    if "nc" not in _NC_CACHE:
        _NC_CACHE["nc"] = build_program()
    res = run_bass_kernel_spmd(_NC_CACHE["nc"], in_maps, core_ids=list(range(8)))
    R = res.results
    y_prompt = np.zeros((4, 2048, D), np.float32)
    y_sample = np.zeros((128, 8, D), np.float32)
    mem_k = np.zeros((1, 4, 256, 4, 128), np.float32)
    mem_v = np.zeros((1, 4, 256, 4, 128), np.float32)
    st_p = np.zeros((1, 4, 4, 96, 192), np.float32)
    st_s = np.zeros((1, 128, 4, 96, 192), np.float32)
    cv_s = np.zeros((1, 128, 8, 768), np.float32)
    for c in range(8):
        b, half = c // 2, c % 2
        r = R[c]
        y_prompt[b, half * 1024:(half + 1) * 1024] = r["yp"]
        y_sample[16 * c:16 * c + 16] = r["ys"].reshape(16, 8, D)
        if half == 0:
            mem_k[0, b] = r["mk"].reshape(256, 4, 128)
            mem_v[0, b] = r["mv"].reshape(256, 4, 128)
        else:
            st_p[0, b] = r["sp"]
        st_s[0, 16 * c:16 * c + 16] = r["ss"]
        cv_s[0, 16 * c:16 * c + 16] = r["cvs"].reshape(16, 8, 768)
    return (y_prompt, y_sample, mem_k, mem_v, st_p, st_s, cv_s)
```

```python
import os
import numpy as np
from contextlib import ExitStack
import concourse.bass as bass
import concourse.mybir as mybir
from concourse.bass_utils import run_bass_kernel_spmd

F32 = mybir.dt.float32
BF16 = mybir.dt.bfloat16
AF = mybir.ActivationFunctionType
ALU = mybir.AluOpType

D = 2048
DIN = 5648
EPS = 1e-6
NPRE = 1024
NMAIN = 1152
C_AU, C_AV, C_AG = 0, 768, 1536
C_BQ, C_BK, C_BV, C_BR, C_BG = 2304, 2688, 3072, 3840, 3856
C_XQ, C_XG = 4624, 5136
WS = 384
NSLOT = 2


class KB:
    def __init__(self, nc, es):
        self.nc = nc
        self.es = es
        self.eng = {"pe": nc.tensor, "act": nc.scalar, "dve": nc.vector, "pool": nc.gpsimd, "sp": nc.sync}
        self.sems = {}
        self.cnt = {}
        for n in self.eng:
            self.sems["e_" + n] = es.enter_context(nc.semaphore("e_" + n))
            self.cnt["e_" + n] = 0
        self.seen = {n: {} for n in self.eng}
        self.lastw = {}
        self.readers = {}
        self.semkeys = {}
        self.groupbase = {}

    def _wait(self, E, ev):
        if ev is None:
            return
        sn, val = ev
        if sn == "e_" + E and E == "pe":
            return
        if sn.startswith("d_"):
            val = max(val, self.cnt[sn])
        if self.seen[E].get(sn, 0) >= val:
            return
        self.eng[E].wait_ge(self.sems[sn], val)
        self.seen[E][sn] = val

    def _deps(self, E, reads, writes, skip_waw_sem=None, nodrain=False):
        for r in reads:
            self._wait(E, self.lastw.get(r))
            if len(r) == 2 and r[0] in "PT" and r[1].isdigit():
                for ev in self.readers.get(r, ()):
                    if ev[0] != "e_" + E:
                        self._wait(E, ev)
        for w in writes:
            lw = self.lastw.get(w)
            if lw is not None and skip_waw_sem is not None and lw[0] == skip_waw_sem:
                self._wait(E, self.groupbase.get(w))
            else:
                if not (nodrain and lw is not None and lw[0] == "e_" + E):
                    self._wait(E, lw)
                if skip_waw_sem is None and E in ("sp", "pool", "act"):
                    self.groupbase[w] = lw
            for ev in self.readers.get(w, ()):
                self._wait(E, ev)

    def _record(self, ev, reads, writes):
        for w in writes:
            self.lastw[w] = ev
            self.readers[w] = []
        for r in reads:
            self.readers.setdefault(r, []).append(ev)

    def op(self, E, fn, reads=(), writes=(), signal=True, nodrain=False):
        self._deps(E, reads, writes, nodrain=nodrain)
        ins = fn(self.eng[E])
        sn = "e_" + E
        if signal:
            self.cnt[sn] += 1
            ins.then_inc(self.sems[sn], 1)
            ev = (sn, self.cnt[sn])
        else:
            ev = (sn, self.cnt[sn] + 1)
        self._record(ev, reads, writes)
        return ins

    def dma(self, Q, out, in_, reads=(), writes=(), semkey=None, part=False, **kw):
        key = semkey or (writes[0] if writes else reads[0])
        sn = "d_" + str(key)
        if sn not in self.sems:
            self.sems[sn] = self.es.enter_context(self.nc.semaphore(sn.replace("(", "_").replace(")", "_").replace(",", "_").replace(" ", "").replace("'", "")))
            self.cnt[sn] = 0
        self._deps(Q, reads, writes, skip_waw_sem=sn if part else None)
        ins = self.eng[Q].dma_start(out=out, in_=in_, **kw)
        self.cnt[sn] += 16
        ins.then_inc(self.sems[sn], 16)
        ev = (sn, self.cnt[sn])
        self._record(ev, reads, writes)
        self.semkeys.setdefault(sn, set()).update(writes)
        return ins

    def seal(self, semkey):
        sn = "d_" + str(semkey)
        for k in self.semkeys.get(sn, ()):
            if self.lastw.get(k, (None,))[0] == sn:
                self.lastw[k] = (sn, self.cnt[sn])

    def barrier(self):
        for E in self.eng:
            for sn, c in self.cnt.items():
                if c > 0:
                    self._wait(E, (sn, c))


def build_program(stop=None):
    nc = bass.Bass("TRN2", target_bir_lowering=False)
    dr = {}

    def din(name, shape):
        dr[name] = nc.dram_tensor(name, list(shape), F32, kind="ExternalInput").ap()

    def dout(name, shape):
        dr[name] = nc.dram_tensor(name, list(shape), F32, kind="ExternalOutput").ap()

    din("xp", [1024, D]); din("xpre", [1024, D]); din("xs", [128, D]); din("mem", [256, D])
    din("s0", [16, 4, 96, 192]); din("ck", [16, 256, 512]); din("cv", [16, 256, 512])
    din("norm_w", [1, D]); din("w_in", [D, DIN]); din("a_vnorm_w", [1, 768]); din("a_ws", [4, 128, 128])
    din("a_bs", [4, 128]); din("b_wa", [16, 384]); din("b_ba", [1, 384]); din("b_onorm_w", [1, 192])
    din("mem_norm_w", [1, D]); din("w_mem_kv", [D, 1024]); din("w_out", [D, D]); din("final_norm_w", [1, D])
    dout("yp", [1024, D]); dout("ys", [128, D]); dout("mk", [256, 512]); dout("mv", [256, 512])
    dout("sp", [4, 96, 192]); dout("ss", [16, 4, 96, 192]); dout("cvs", [128, 768])

    win_v = dr["w_in"].rearrange("(k p) c -> p k c", p=128)
    wkv_v = dr["w_mem_kv"].rearrange("(k p) c -> p k c", p=128)
    wout_v = dr["w_out"].rearrange("(k p) c -> p k c", p=128)

    with ExitStack() as es:
        K = KB(nc, es)

        uid = [0]

        def sb(stack, name, shape, dt):
            uid[0] += 1
            return stack.enter_context(nc.sbuf_tensor(f"{name}_{uid[0]}", list(shape), dt))

        P = [es.enter_context(nc.psum_tensor(f"P{i}", [128, 512], F32)) for i in range(6)]
        T = [es.enter_context(nc.psum_tensor(f"T{i}", [128, 1024], BF16)) for i in range(2)]
        PK = [f"P{i}" for i in range(6)]
        TK = [f"T{i}" for i in range(2)]

        ones_f = sb(es, "ones_f", [128, 512], F32)
        ident_f = sb(es, "ident_f", [128, 128], F32)
        ident = sb(es, "ident", [128, 128], BF16)
        maskT = sb(es, "maskT", [128, 128], F32)
        maskS = sb(es, "maskS", [128, 128], F32)
        ones_b = sb(es, "ones_b", [128, 128], BF16)
        blockmask = sb(es, "blockmask", [128, 16], F32)
        normw_col = sb(es, "normw_col", [128, 16], F32)
        memw_col = sb(es, "memw_col", [128, 16], F32)
        wo_b = sb(es, "wo_b", [128, 192], F32)
        abT = sb(es, "abT", [128, 4], F32)
        abTs = sb(es, "abTs", [128, 4], F32)
        WT = sb(es, "WT", [128, 4, 128], BF16)
        WTs = sb(es, "WTs", [128, 4, 128], BF16)
        bwa = sb(es, "bwa", [17, 384], F32)
        Sst = sb(es, "Sst", [96, 4, 192], F32)
        KT = sb(es, "KT", [128, 4, 256], BF16)
        Vb = sb(es, "Vb", [128, 2, 512], BF16)
        branchT = sb(es, "branchT", [128, 16, NMAIN], BF16)
        wsl = [sb(es, f"wsl{i}", [128, 16, WS], BF16) for i in range(NSLOT)]
        st = [sb(es, f"st{i}", [128, 8], F32) for i in range(4)]
        mhalf = sb(es, "mhalf", [128, 8], F32)

        K.op("pool", lambda e: e.memset(ones_f[:], 1.0), writes=["ones_f"])
        K.op("pool", lambda e: e.memset(ones_b[:], 1.0), writes=["ones_b"])
        K.op("pool", lambda e: e.memset(mhalf[:], -0.5), writes=["mhalf"])
        K.op("pool", lambda e: e.memset(Sst[:], 0.0), writes=["Sst0", "Sst1", "Sst2", "Sst3"])
        K.op("pool", lambda e: e.affine_select(out=ident_f[:], in_=ones_f[:, 0:128], pattern=[[-1, 128]],
                                               compare_op=ALU.is_equal, fill=0.0, base=0, channel_multiplier=1),
             reads=["ones_f"], writes=["ident_f"])
        K.op("pool", lambda e: e.tensor_copy(out=ident[:], in_=ident_f[:]), reads=["ident_f"], writes=["ident"])
        K.op("pool", lambda e: e.affine_select(out=maskT[:], in_=ones_f[:, 0:128], pattern=[[1, 128]],
                                               compare_op=ALU.is_ge, fill=0.0, base=0, channel_multiplier=-1),
             reads=["ones_f"], writes=["maskT"])
        K.op("pool", lambda e: e.affine_select(out=maskS[:], in_=maskT[:], pattern=[[-8, 16], [0, 8]],
                                               compare_op=ALU.is_ge, fill=0.0, base=0, channel_multiplier=1),
             reads=["maskT"], writes=["maskS"])
        K.op("pool", lambda e: e.affine_select(out=maskS[:], in_=maskS[:], pattern=[[8, 16], [0, 8]],
                                               compare_op=ALU.is_ge, fill=0.0, base=7, channel_multiplier=-1),
             reads=["maskS"], writes=["maskS"])
        K.op("pool", lambda e: e.affine_select(out=blockmask[:], in_=ones_f[:, 0:16], pattern=[[-8, 16]],
                                               compare_op=ALU.is_ge, fill=0.0, base=0, channel_multiplier=1),
             reads=["ones_f"], writes=["blockmask"])
        K.op("pool", lambda e: e.affine_select(out=blockmask[:], in_=blockmask[:], pattern=[[8, 16]],
                                               compare_op=ALU.is_ge, fill=0.0, base=7, channel_multiplier=-1),
             reads=["blockmask"], writes=["blockmask"])

        if stop == 'c1':
            K.barrier()
            return nc
        jobs = []
        wstate = {"issued": 0, "taken": 0, "done": 0}

        def wissue_upto(j):
            while wstate["issued"] <= min(j, len(jobs) - 1):
                i = wstate["issued"]
                view, c0, ncols = jobs[i]
                s = i % NSLOT
                K.dma("pool", wsl[s][:, :, 0:ncols], view[:, :, c0:c0 + ncols], writes=[f"wsl{s}"])
                wstate["issued"] += 1

        def wget(n=1, keep=0):
            j = wstate["taken"]
            wstate["done"] = j - keep
            wissue_upto(j + n - 1)
            wstate["taken"] += n
            res = [(wsl[(j + i) % NSLOT], f"wsl{(j + i) % NSLOT}") for i in range(n)]
            return res if n > 1 else res[0]

        def wdone():
            wstate["done"] = wstate["taken"]

        def wprefetch():
            wissue_upto(wstate["done"] + NSLOT - 1)

        def J(view, c0, n):
            jobs.append((view, c0, n))

        J(win_v, C_BR, 16); J(win_v, C_BK, 384); J(win_v, C_BV, 384); J(win_v, C_BV + 384, 384)
        J(wkv_v, 0, 256); J(wkv_v, 256, 256); J(wkv_v, 512, 256); J(wkv_v, 768, 256)
        J(win_v, C_BR, 16); J(win_v, C_BQ, 384); J(win_v, C_BK, 384); J(win_v, C_BV, 384); J(win_v, C_BV + 384, 384)
        J(win_v, C_BG, 384); J(win_v, C_BG + 384, 384)
        J(win_v, C_AV, 384); J(win_v, C_AV + 384, 384); J(win_v, C_AU, 384); J(win_v, C_AU + 384, 384)
        J(win_v, C_AG, 384); J(win_v, C_AG + 384, 384)
        J(win_v, C_XQ, 256); J(win_v, C_XQ + 256, 256); J(win_v, C_XG, 256); J(win_v, C_XG + 256, 256)

        def proj_tok(hT, hTk, tok0, wslot, wk, ncols, ps, psk):
            for k in range(16):
                K.op("pe", lambda e, k=k: e.matmul(ps[:, 0:ncols], lhsT=hT[:, k, tok0:tok0 + 128], rhs=wslot[:, k, 0:ncols],
                                                   start=(k == 0), stop=(k == 15)),
                     reads=[hTk, wk], writes=[psk], signal=(k == 15))

        def proj_feat(hT, hTk, tok0, ntok, wslot, wk, c0, M, ps, psk):
            for k in range(16):
                K.op("pe", lambda e, k=k: e.matmul(ps[0:M, 0:ntok], lhsT=wslot[:, k, c0:c0 + M], rhs=hT[:, k, tok0:tok0 + ntok],
                                                   start=(k == 0), stop=(k == 15)),
                     reads=[hTk, wk], writes=[psk], signal=(k == 15))

        def rstd_from_ss(stt, stk, ncol_in, n, eng_sum=True):
            if ncol_in == 2:
                K.op("dve", lambda e: e.tensor_tensor(out=stt[:, 4:5], in0=stt[:, 0:1], in1=stt[:, 1:2], op=ALU.add),
                     reads=[stk], writes=[stk])
                src = stt[:, 4:5]
            else:
                src = stt[:, 0:1]
            K.op("act", lambda e: e.activation(out=stt[:, 5:6], in_=src, func=AF.Sqrt, scale=1.0 / n, bias=EPS),
                 reads=[stk], writes=[stk])
            K.op("dve", lambda e: e.reciprocal(out=stt[:, 6:7], in_=stt[:, 5:6]), reads=[stk], writes=[stk])

        def norm_gen(stack_bufs, src_list, dstT, dstk, wcol, wcolk):
            xsl, xsb_, jnk = stack_bufs
            nx, nbf = len(xsl), len(xsb_)
            tl = [(src, t, col0 + t * 128) for (src, ntiles, col0) in src_list for t in range(ntiles)]
            n = len(tl)

            def load(i):
                src, t, c0 = tl[i]
                K.dma("sp", xsl[i % nx][:], src[t * 128:(t + 1) * 128, :], writes=[f"xsl{i % nx}"])

            def front(i):
                s, sx, sq_ = i % 2, i % nx, i % nbf
                xk, bk, sk = f"xsl{sx}", f"xsb{sq_}", f"st{s}"
                jo, jk = (jnk[:], "jnk") if jnk is not None else (xsb_[sq_][:], bk)
                K.op("act", lambda e: e.activation(out=jo, in_=xsl[sx][:], func=AF.Square, accum_out=st[s][:, 0:1]),
                     reads=[xk], writes=[jk, sk])
                rstd_from_ss(st[s], sk, 1, float(D))
                K.op("pool", lambda e: e.tensor_scalar(out=xsb_[sq_][:], in0=xsl[sx][:], scalar1=st[s][:, 6:7], scalar2=1.0,
                                                       op0=ALU.mult, op1=ALU.mult),
                     reads=[xk, sk], writes=[bk])

            def transposes(i):
                sq_ = i % nbf
                for k in range(16):
                    tb = k // 8
                    K.op("pe", lambda e, k=k, tb=tb: e.transpose(out=T[tb][:, (k % 8) * 128:(k % 8 + 1) * 128],
                                                                 in_=xsb_[sq_][:, k * 128:(k + 1) * 128], identity=ident[:]),
                         reads=[f"xsb{sq_}", "ident"], writes=[TK[tb]], signal=(k % 8 == 7))

            def back(i):
                src, t, c0 = tl[i]
                for tb in range(2):
                    K.op("dve", lambda e, tb=tb: e.tensor_tensor(
                        out=dstT[:, tb * 8:(tb + 1) * 8, c0:c0 + 128], in0=T[tb][:].rearrange("p (k i) -> p k i", k=8),
                        in1=wcol[:, tb * 8:(tb + 1) * 8].unsqueeze(2).broadcast_to([128, 8, 128]), op=ALU.mult),
                         reads=[TK[tb], wcolk], writes=[dstk], nodrain=(tb == 1))

            for i in range(min(nx - 1, n)):
                load(i)
            front(0)
            yield
            transposes(0)
            yield
            for i in range(n):
                if i + 1 < n:
                    front(i + 1)
                back(i)
                if i + nx - 1 < n:
                    load(i + nx - 1)
                yield
                if i + 1 < n:
                    transposes(i + 1)
                    yield

        def transpose_to_branch(srcb, srck, nchunks, kbase, col0, eng="dve", tb=0):
            for kk in range(nchunks):
                K.op("pe", lambda e, kk=kk: e.transpose(out=T[tb][:, kk * 128:(kk + 1) * 128], in_=srcb[:, kk * 128:(kk + 1) * 128],
                                                        identity=ident[:]),
                     reads=[srck, "ident"], writes=[TK[tb]], signal=(kk == nchunks - 1))
            K.op(eng, lambda e: (e.tensor_copy if eng != "act" else e.copy)(
                out=branchT[:, kbase:kbase + nchunks, col0:col0 + 128],
                in_=T[tb][:, 0:nchunks * 128].rearrange("p (k i) -> p k i", k=nchunks)),
                 reads=[TK[tb]], writes=[f"bT{k}_{col0 // 128}" for k in range(kbase, kbase + nchunks)])

        def bkeys(k0, nk, tok0, ntok):
            return [f"bT{k}_{t}" for k in range(k0, k0 + nk) for t in range(tok0 // 128, (tok0 + ntok + 127) // 128)]

        def gate_feat(hT, hTk, blocks, wslot, wk, nchunks, kbase, sg, mode, nbank=2):
            i = 0
            for (b0, nb) in blocks:
                for kk in range(nchunks):
                    pi = i % nbank
                    proj_feat(hT, hTk, b0, nb, wslot, wk, kk * 128, 128, P[pi], PK[pi])
                    dst = branchT[:, kbase + kk, b0:b0 + nb]
                    bk = bkeys(kbase + kk, 1, b0, nb)
                    if mode == "silu":
                        K.op("act", lambda e, pi=pi: e.activation(out=sg[pi][:, 0:nb], in_=P[pi][:, 0:nb], func=AF.Silu),
                             reads=[PK[pi]], writes=[f"sg{pi}"])
                        K.op("dve", lambda e, pi=pi, dst=dst: e.tensor_tensor(out=dst, in0=dst, in1=sg[pi][:, 0:nb], op=ALU.mult),
                             reads=[f"sg{pi}"] + bk, writes=bk)
                    else:
                        K.op("dve", lambda e, pi=pi, dst=dst: e.tensor_tensor(out=dst, in0=P[pi][:, 0:nb], in1=dst, op=ALU.mult),
                             reads=[PK[pi]] + bk, writes=bk)
                    i += 1
                    yield

        def gla_phase(hT, hTk, NT, tiles, with_out, side=None):
            blocks = []
            ptoks = sum(128 for k, _ in tiles if k == "p")
            for b0 in range(0, ptoks, 512):
                blocks.append((b0, min(512, ptoks - b0), "p"))
            for k, t0 in tiles:
                if k == "s":
                    blocks.append((t0, 128, "s"))
            with ExitStack() as gs:
                dec = sb(gs, "dec", [96, 4, 24], F32)
                dtmp = sb(gs, "dtmp", [96, 4, 16], F32)
                kinT = sb(gs, "kinT", [96, 4, NT], BF16)
                koutT = sb(gs, "koutT", [96, 4, NT], BF16)
                qinT = sb(gs, "qinT", [96, 4, NT], BF16) if with_out else None
                has_s = any(k == "s" for k, _ in tiles)
                with ExitStack() as gs1:
                    G = sb(gs1, "G", [96, 4, NT + 1], F32)
                    rT = sb(gs1, "rT", [17, NT], F32)
                    etmp = [sb(gs1, f"etmp{i}", [96, 512], F32) for i in range(2)]
                    ltmp = [sb(gs1, f"ltmp{i}", [96, 512], F32) for i in range(2)]
                    cwt, cet = etmp, ltmp
                    ex1 = [sb(gs1, f"ex1_{i}", [96, 512], F32) for i in range(2)]
                    ex2 = [sb(gs1, f"ex2_{i}", [96, 512], F32) for i in range(2)]
                    gte = [sb(gs1, f"gte{i}", [96, 512], F32) for i in range(2)]
                    gtl = [sb(gs1, f"gtl{i}", [96, 512], F32) for i in range(2)]
                    K.op("pool", lambda e: e.memset(rT[:], 1.0), writes=["rTm"])
                    K.op("pool", lambda e: e.memset(G[:, :, 0:1], 0.0), writes=["Gi"])
                    NPT_ = sum(1 for k, _ in tiles if k == "p")
                    gkeys = ["Gi"] + [f"G{bi}" for bi in range(len(blocks))]

                    def gates_gen(rslot, rk):
                        i = 0
                        for bi, (b0, nb, kind) in enumerate(blocks):
                            pr = 4 + bi % 2
                            proj_feat(hT, hTk, b0, nb, rslot, rk, 0, 16, P[pr], PK[pr])
                            K.op("act", lambda e, pr=pr, b0=b0, nb=nb: e.copy(out=rT[0:16, b0:b0 + nb], in_=P[pr][0:16, 0:nb]),
                                 reads=[PK[pr], "rTm"], writes=[f"rT{bi}"])
                        for bi, (b0, nb, kind) in enumerate(blocks):
                            for h in range(4):
                                pi = 2 + i % 2
                                ti = i % 2
                                K.op("pe", lambda e, pi=pi, h=h, b0=b0, nb=nb: e.matmul(P[pi][0:96, 0:nb], lhsT=bwa[0:17, h * 96:(h + 1) * 96],
                                                                                       rhs=rT[0:17, b0:b0 + nb], start=True, stop=True),
                                     reads=["bwa", "rTm", f"rT{bi}"], writes=[PK[pi]])
                                K.op("act", lambda e, pi=pi, ti=ti, nb=nb: e.activation(out=gte[ti][:, 0:nb], in_=P[pi][0:96, 0:nb], func=AF.Exp,
                                                                                       scale=-1.0),
                                     reads=[PK[pi]], writes=[f"gte{ti}"])
                                K.op("act", lambda e, ti=ti, nb=nb: e.activation(out=gtl[ti][:, 0:nb], in_=gte[ti][:, 0:nb], func=AF.Ln, bias=1.0),
                                     reads=[f"gte{ti}"], writes=[f"gtl{ti}"])
                                K.op("dve", lambda e, ti=ti, h=h, b0=b0, nb=nb: e.tensor_tensor_scan(
                                    out=G[:, h, 1 + b0:1 + b0 + nb], data0=ones_f[0:96, 0:nb], data1=gtl[ti][:, 0:nb],
                                    initial=G[:, h, b0:b0 + 1], op0=ALU.mult, op1=ALU.subtract),
                                     reads=[f"gtl{ti}", "ones_f", gkeys[bi]], writes=[gkeys[bi + 1]], nodrain=(h > 0))
                                i += 1
                            yield
                        if NPT_:
                            K.op("dve", lambda e: e.tensor_tensor(out=dtmp[:, :, 0:NPT_], in0=G[:, :, 128:128 * NPT_ + 1:128],
                                                                  in1=G[:, :, 0:128 * (NPT_ - 1) + 1:128], op=ALU.subtract),
                                 reads=gkeys, writes=["dtmp"])
                            K.op("act", lambda e: e.activation(out=dec[:, :, 0:NPT_], in_=dtmp[:, :, 0:NPT_], func=AF.Exp, scale=1.0 / 16),
                                 reads=["dtmp"], writes=["dec"])
                        for ci, (kind, t0) in enumerate(tiles):
                            if kind == "s":
                                K.op("dve", lambda e, t0=t0: e.tensor_tensor(out=dtmp[:, :, 0:16], in0=G[:, :, t0 + 8:t0 + 129:8],
                                                                             in1=G[:, :, t0:t0 + 121:8], op=ALU.subtract),
                                     reads=gkeys, writes=["dtmp"])
                                K.op("act", lambda e: e.activation(out=dec[:, :, 8:24], in_=dtmp[:, :, 0:16], func=AF.Exp, scale=1.0 / 16),
                                     reads=["dtmp"], writes=["dec"])
                        yield

                    def rel_exps(h, bi, b0, nb, kind, ti, want):
                        gk = [gkeys[bi], gkeys[bi + 1]]
                        if kind == "p":
                            m = nb // 128
                            gin = G[:, h, 1 + b0:1 + b0 + nb].rearrange("p (m i) -> p m i", m=m)
                            gst = G[:, h, b0:b0 + nb].rearrange("p (m i) -> p m i", m=m)[:, :, 0:1].broadcast_to([96, m, 128])
                            gen = G[:, h, b0 + 128:b0 + nb + 1:128].unsqueeze(2).broadcast_to([96, m, 128])
                            shp = "p (m i) -> p m i"
                            kw = dict(m=m)
                        else:
                            gin = G[:, h, 1 + b0:1 + b0 + 128].rearrange("p (m i) -> p m i", m=16)
                            gst = G[:, h, b0:b0 + 128].rearrange("p (m i) -> p m i", m=16)[:, :, 0:1].broadcast_to([96, 16, 8])
                            gen = G[:, h, b0 + 8:b0 + 129:8].unsqueeze(2).broadcast_to([96, 16, 8])
                            shp = "p (m i) -> p m i"
                            kw = dict(m=16)
                        K.op("dve", lambda e: e.tensor_tensor(out=cwt[ti][:, 0:nb].rearrange(shp, **kw), in0=gin, in1=gst, op=ALU.subtract),
                             reads=gk, writes=[f"etmp{ti}"])
                        if want == "q":
                            K.op("act", lambda e: e.activation(out=ex1[ti][:, 0:nb], in_=cwt[ti][:, 0:nb], func=AF.Exp, scale=1.0 / 16),
                                 reads=[f"etmp{ti}"], writes=[f"ex1_{ti}"])
                        else:
                            K.op("dve", lambda e: e.tensor_tensor(out=cet[ti][:, 0:nb].rearrange(shp, **kw), in0=gen, in1=gin, op=ALU.subtract),
                                 reads=gk, writes=[f"ltmp{ti}"])
                            K.op("act", lambda e: e.activation(out=ex1[ti][:, 0:nb], in_=cwt[ti][:, 0:nb], func=AF.Exp, scale=-1.0 / 16),
                                 reads=[f"etmp{ti}"], writes=[f"ex1_{ti}"])
                            K.op("act", lambda e: e.activation(out=ex2[ti][:, 0:nb], in_=cet[ti][:, 0:nb], func=AF.Exp, scale=1.0 / 16),
                                 reads=[f"ltmp{ti}"], writes=[f"ex2_{ti}"])

                    def qk_gen(which, wslot, wk):
                        i = 0
                        for bi, (b0, nb, kind) in enumerate(blocks):
                            for h in range(4):
                                pi = i % 2
                                ti = i % 2
                                proj_feat(hT, hTk, b0, nb, wslot, wk, h * 96, 96, P[pi], PK[pi])
                                rel_exps(h, bi, b0, nb, kind, ti, which)
                                if which == "q":
                                    K.op("dve", lambda e, pi=pi, ti=ti, h=h, b0=b0, nb=nb: e.scalar_tensor_tensor(
                                        out=qinT[:, h, b0:b0 + nb], in0=P[pi][0:96, 0:nb], scalar=96.0 ** -0.5, in1=ex1[ti][:, 0:nb],
                                        op0=ALU.mult, op1=ALU.mult),
                                         reads=[PK[pi], f"ex1_{ti}"], writes=["qinT"], nodrain=True)
                                else:
                                    K.op("dve", lambda e, pi=pi, ti=ti, h=h, b0=b0, nb=nb: e.tensor_tensor(
                                        out=kinT[:, h, b0:b0 + nb], in0=P[pi][0:96, 0:nb], in1=ex1[ti][:, 0:nb], op=ALU.mult),
                                         reads=[PK[pi], f"ex1_{ti}"], writes=["kinT"], nodrain=True)
                                    K.op("dve", lambda e, pi=pi, ti=ti, h=h, b0=b0, nb=nb: e.tensor_tensor(
                                        out=koutT[:, h, b0:b0 + nb], in0=P[pi][0:96, 0:nb], in1=ex2[ti][:, 0:nb], op=ALU.mult),
                                         reads=[PK[pi], f"ex2_{ti}"], writes=["koutT"], nodrain=True)
                                i += 1
                                yield

                    def run1(*specs):
                        specs = [[g, n] for (g, n) in specs]
                        while specs:
                            for sp_ in list(specs):
                                g, n = sp_
                                for _ in range(n):
                                    try:
                                        next(g)
                                    except StopIteration:
                                        specs.remove(sp_)
                                        break

                    rslot, rk = wget()
                    wprefetch()
                    gg = gates_gen(rslot, rk)
                    next(gg)
                    w1slot, w1k = wget(keep=1)
                    if side is not None:
                        run1((qk_gen("q" if with_out else "k", w1slot, w1k), 4), (gg, 1), (side, 1))
                    else:
                        run1((qk_gen("q" if with_out else "k", w1slot, w1k), 4), (gg, 1))
                    wdone()
                    wprefetch()
                    if with_out:
                        w2slot, w2k = wget()
                        wprefetch()
                        run1((qk_gen("k", w2slot, w2k), 1))

                    K.barrier()

                with ExitStack() as gs2:
                    NTL = len(tiles)
                    NPT = sum(1 for k, _ in tiles if k == "p")
                    pidx = {}
                    for ci_, (k_, _) in enumerate(tiles):
                        if k_ == "p":
                            pidx[ci_] = len(pidx)
                    vb_all = [sb(gs2, f"vball{i}", [128, NTL, 384], BF16) for i in range(2)]
                    kob_all = [sb(gs2, f"koball{i}", [128, NTL, 192], BF16) for i in range(2)]
                    Sb_all = [sb(gs2, f"Sball{i}", [96, NPT, 2, 192], BF16) for i in range(2)]
                    if with_out:
                        sTb_all = [sb(gs2, f"sTball{i}", [128, NTL, 2, 128], BF16) for i in range(2)]
                        ob = [sb(gs2, f"ob{i}", [128, 384], BF16) for i in range(2)]
                        sg = [sb(gs2, f"sg{i}", [128, 512], BF16) for i in range(2)]
                    if has_s:
                        Qpad = sb(gs2, "Qpad", [96, 2, 2304], BF16)
                        S0f = [sb(gs2, f"S0f{i}", [96, 2, 192], F32) for i in range(4)]
                        S0b = [sb(gs2, f"S0b{i}", [96, 2, 192], BF16) for i in range(4)]
                        Vpad = [sb(gs2, f"Vpad{i}", [128, 384], BF16) for i in range(2)]

                    def state_step(hp, ci, pbank, pbk):
                        vk, kk = f"vb{hp}_{ci}", f"kob{hp}_{ci}"
                        pk_ = pidx[ci]
                        for hh in range(2):
                            K.op("pe", lambda e, hh=hh: e.matmul(pbank[0:96, hh * 192:(hh + 1) * 192],
                                                                lhsT=kob_all[hp][:, ci, hh * 96:(hh + 1) * 96],
                                                                rhs=vb_all[hp][:, ci, hh * 192:(hh + 1) * 192], start=True, stop=True),
                                 reads=[kk, vk], writes=[pbk], signal=(hh == 1))

                        def post():
                            for hh in range(2):
                                h = 2 * hp + hh
                                K.op("dve", lambda e, hh=hh, h=h: e.scalar_tensor_tensor(
                                    out=Sst[:, h, :], in0=Sst[:, h, :], scalar=dec[:, h, pk_:pk_ + 1], in1=pbank[0:96, hh * 192:(hh + 1) * 192],
                                    op0=ALU.mult, op1=ALU.add),
                                     reads=[f"Sst{h}", "dec", pbk], writes=[f"Sst{h}"])
                            if pk_ + 1 < NPT:
                                K.op("act", lambda e: e.copy(out=Sb_all[hp][:, pk_ + 1], in_=Sst[:, 2 * hp:2 * hp + 2, :]),
                                     reads=[f"Sst{2 * hp}", f"Sst{2 * hp + 1}"], writes=[f"Sb{hp}_{pk_ + 1}"])
                        return post

                    def stage1(hp, wslot, wk, do_state):
                        K.op("act", lambda e: e.copy(out=Sb_all[hp][:, 0], in_=Sst[:, 2 * hp:2 * hp + 2, :]),
                             reads=[f"Sst{2 * hp}", f"Sst{2 * hp + 1}"], writes=[f"Sb{hp}_0"])
                        for ci, (kind, t0) in enumerate(tiles):
                            pb = ci % 2
                            vk, kk, sk = f"vb{hp}_{ci}", f"kob{hp}_{ci}", f"sTb{hp}_{ci}"
                            proj_tok(hT, hTk, t0, wslot, wk, 384, P[pb], PK[pb])
                            for hh in range(2):
                                h = 2 * hp + hh
                                K.op("pe", lambda e, hh=hh, h=h, t0=t0: e.transpose(out=T[1][:, hh * 96:(hh + 1) * 96],
                                                                                    in_=koutT[:, h, t0:t0 + 128], identity=ident[0:96, 0:96]),
                                     reads=["koutT", "ident"], writes=[TK[1]], signal=(hh == 1))
                            if with_out:
                                for hh in range(2):
                                    h = 2 * hp + hh
                                    K.op("pe", lambda e, hh=hh, h=h, t0=t0: e.matmul(P[2][:, hh * 128:(hh + 1) * 128],
                                                                                    lhsT=kinT[:, h, t0:t0 + 128], rhs=qinT[:, h, t0:t0 + 128],
                                                                                    start=True, stop=True),
                                         reads=["kinT", "qinT"], writes=[PK[2]], signal=(hh == 1))
                            post = None
                            if do_state and ci >= 1 and tiles[ci - 1][0] == "p":
                                post = state_step(hp, ci - 1, P[3], PK[3])
                            K.op("act", lambda e, ci=ci, pb=pb: e.copy(out=vb_all[hp][:, ci, :], in_=P[pb][:, 0:384]), reads=[PK[pb]], writes=[vk])
                            K.op("dve", lambda e, ci=ci: e.tensor_copy(out=kob_all[hp][:, ci, :], in_=T[1][:, 0:192]), reads=[TK[1]], writes=[kk])
                            if with_out:
                                msk = maskT if kind == "p" else maskS
                                K.op("dve", lambda e, ci=ci, msk=msk: e.tensor_tensor(
                                    out=sTb_all[hp][:, ci], in0=P[2][:, 0:256].rearrange("p (h i) -> p h i", h=2),
                                    in1=msk[:].unsqueeze(1).broadcast_to([128, 2, 128]), op=ALU.mult),
                                     reads=[PK[2], "maskT", "maskS"], writes=[sk])
                            if post:
                                post()
                            yield
                        if do_state and tiles[NTL - 1][0] == "p":
                            state_step(hp, NTL - 1, P[3], PK[3])()
                        yield

                    def onorm_pre(hp, ci, ops_banks):
                        sti, stk = st[2 + ci % 2], f"st{2 + ci % 2}"
                        oi = ci % 2
                        for hh, (pb, pbk, off) in enumerate(ops_banks):
                            K.op("act", lambda e, pb=pb, off=off, hh=hh: e.activation(
                                out=ob[oi][:, hh * 192:(hh + 1) * 192], in_=pb[:, off:off + 192], func=AF.Square,
                                accum_out=sti[:, hh:hh + 1]),
                                 reads=[pbk], writes=[f"ob{oi}", stk])
                        K.op("pool", lambda e: e.tensor_scalar(out=sti[:, 2:4], in0=sti[:, 0:2], scalar1=1.0 / 192, scalar2=EPS,
                                                               op0=ALU.mult, op1=ALU.add),
                             reads=[stk], writes=[stk])
                        K.op("pool", lambda e: e.tensor_tensor(out=sti[:, 4:6], in0=sti[:, 2:4], in1=mhalf[:, 0:2], op=ALU.pow),
                             reads=[stk, "mhalf"], writes=[stk])
                        for hh, (pb, pbk, off) in enumerate(ops_banks):
                            K.op("dve", lambda e, pb=pb, off=off, hh=hh: e.scalar_tensor_tensor(
                                out=ob[oi][:, hh * 192:(hh + 1) * 192], in0=pb[:, off:off + 192], scalar=sti[:, 4 + hh:5 + hh],
                                in1=wo_b[:], op0=ALU.mult, op1=ALU.mult),
                                 reads=[pbk, stk, "wo_b"], writes=[f"ob{oi}"], nodrain=(hh == 1))

                    def onorm_post(hp, ci, t0):
                        transpose_to_branch(ob[ci % 2], f"ob{ci % 2}", 3, 6 + 3 * hp, t0, eng="act", tb=0)

                    def stage2(hp, prog):
                        prev = None
                        for ci, (kind, t0) in enumerate(tiles):
                            if kind != "p":
                                continue
                            pb = 4 + ci % 2
                            vk, sk = f"vb{hp}_{ci}", f"sTb{hp}_{ci}"
                            for hh in range(2):
                                h = 2 * hp + hh
                                K.op("pe", lambda e, hh=hh, ci=ci, pb=pb: e.matmul(P[pb][:, hh * 192:(hh + 1) * 192], lhsT=sTb_all[hp][:, ci, hh, :],
                                                                                  rhs=vb_all[hp][:, ci, hh * 192:(hh + 1) * 192], start=True, stop=False),
                                     reads=[sk, vk], writes=[PK[pb]], signal=False)
                                K.op("pe", lambda e, hh=hh, h=h, t0=t0, ci=ci, pb=pb: e.matmul(P[pb][:, hh * 192:(hh + 1) * 192],
                                                                                              lhsT=qinT[:, h, t0:t0 + 128], rhs=Sb_all[hp][:, pidx[ci], hh, :],
                                                                                              start=False, stop=True),
                                     reads=["qinT", f"Sb{hp}_{pidx[ci]}"], writes=[PK[pb]], signal=True)
                            post = state_step(hp, ci, P[3], PK[3])
                            if prev is not None:
                                onorm_post(hp, *prev)
                                prog.add(prev[1])
                            post()
                            onorm_pre(hp, ci, [(P[pb], PK[pb], 0), (P[pb], PK[pb], 192)])
                            prev = (ci, t0)
                            yield
                        onorm_post(hp, *prev)
                        prog.add(prev[1])
                        K.dma("sp", dr["sp"][2 * hp:2 * hp + 2].rearrange("h d e -> d h e"), Sst[:, 2 * hp:2 * hp + 2, :],
                              reads=[f"Sst{2 * hp}", f"Sst{2 * hp + 1}"], semkey="Sst")
                        yield

                    def stage_s(hp):
                        ci = [i for i, (k, _) in enumerate(tiles) if k == "s"][0]
                        t0 = tiles[ci][1]
                        vk, kk, sk = f"vb{hp}_{ci}", f"kob{hp}_{ci}", f"sTb{hp}_{ci}"

                        def load(s):
                            K.dma("sp", S0f[s % 4][:], dr["s0"][s, 2 * hp:2 * hp + 2].rearrange("h d e -> d h e"), writes=[f"S0f{s % 4}"])

                        def prep(s):
                            K.op("act", lambda e: e.copy(out=S0b[s % 4][:], in_=S0f[s % 4][:]), reads=[f"S0f{s % 4}"], writes=[f"S0b{s % 4}"])
                            K.op("dve", lambda e: e.tensor_scalar(out=Vpad[s % 2][:], in0=vb_all[hp][:, ci, :], scalar1=blockmask[:, s:s + 1],
                                                                  scalar2=None, op0=ALU.mult),
                                 reads=[vk, "blockmask"], writes=[f"Vpad{s % 2}"])

                        load(0)
                        load(1)
                        K.op("pool", lambda e: e.memset(Qpad[:], 0.0), writes=["Qpad"])
                        for hh in range(2):
                            h = 2 * hp + hh
                            K.op("pool", lambda e, hh=hh, h=h: e.tensor_copy(
                                out=Qpad[:, hh, :].rearrange("p (s c) -> p s c", c=144)[:, :, 0:8],
                                in_=qinT[:, h, t0:t0 + 128].rearrange("p (s i) -> p s i", i=8)),
                                 reads=["qinT"], writes=["Qpad"])
                        prep(0)
                        for hh in range(2):
                            K.op("pe", lambda e, hh=hh: e.matmul(P[4 + hh][:, 0:192], lhsT=sTb_all[hp][:, ci, hh, :],
                                                                rhs=vb_all[hp][:, ci, hh * 192:(hh + 1) * 192], start=True, stop=False),
                                 reads=[sk, vk], writes=[PK[4 + hh]], signal=False)
                        yield
                        T1f = T[1][:].bitcast(F32)
                        for s in range(16):
                            fk, bk_, pk = f"S0f{s % 4}", f"S0b{s % 4}", f"Vpad{s % 2}"
                            UB, UBk = (P[3], PK[3])
                            if s + 2 < 16:
                                load(s + 2)
                            for hh in range(2):
                                K.op("pe", lambda e, hh=hh, s=s: e.matmul(
                                    P[4 + hh][:, 0:192], lhsT=Qpad[:, hh, 136 * s:136 * s + 128], rhs=S0b[s % 4][:, hh, :],
                                    start=False, stop=(s == 15)),
                                     reads=["Qpad", bk_], writes=[PK[4 + hh]], signal=True)
                            for hh in range(2):
                                K.op("pe", lambda e, hh=hh, s=s: e.matmul(
                                    UB[0:96, hh * 192:(hh + 1) * 192], lhsT=kob_all[hp][:, ci, hh * 96:(hh + 1) * 96],
                                    rhs=Vpad[s % 2][:, hh * 192:(hh + 1) * 192], start=True, stop=True),
                                     reads=[kk, pk], writes=[UBk], signal=(hh == 1))
                            if s + 1 < 16:
                                prep(s + 1)
                            for hh in range(2):
                                h = 2 * hp + hh
                                K.op("dve", lambda e, hh=hh, h=h, s=s: e.scalar_tensor_tensor(
                                    out=S0f[s % 4][:, hh, :], in0=S0f[s % 4][:, hh, :], scalar=dec[:, h, 8 + s:9 + s],
                                    in1=UB[0:96, hh * 192:(hh + 1) * 192], op0=ALU.mult, op1=ALU.add),
                                     reads=[fk, "dec", UBk], writes=[fk])
                            K.dma("sp", dr["ss"][s, 2 * hp:2 * hp + 2].rearrange("h d e -> d h e"), S0f[s % 4][:], reads=[fk])
                            yield
                        onorm_pre(hp, ci, [(P[4], PK[4], 0), (P[5], PK[5], 0)])
                        yield
                        onorm_post(hp, ci, t0)
                        yield

                    def chain(*gens):
                        for g in gens:
                            yield from g

                    def run(*specs):
                        specs = [[g, n] for (g, n) in specs]
                        while specs:
                            for sp_ in list(specs):
                                g, n = sp_
                                for _ in range(n):
                                    try:
                                        next(g)
                                    except StopIteration:
                                        specs.remove(sp_)
                                        break

                    pblocks = [(b0, nb) for (b0, nb, kind) in blocks if kind == "p"]
                    sblocks = [(b0, nb) for (b0, nb, kind) in blocks if kind == "s"]
                    if not with_out:
                        for hp in range(2):
                            wslot, wk = wget()
                            wprefetch()
                            if side is not None:
                                run((stage1(hp, wslot, wk, True), 2), (side, 1))
                            else:
                                run((stage1(hp, wslot, wk, True), 1))
                    else:
                        for hp in range(2):
                            wslot, wk = wget()
                            wprefetch()
                            g1_ = stage1(hp, wslot, wk, False)
                            next(g1_)
                            run((g1_, 1), (stage_s(hp), 2))

                        def gate_with(hp, slot, k_):
                            prog = set()
                            s2 = stage2(hp, prog)
                            for (b0, nb) in sblocks + pblocks:
                                need = {t0 for (kd, t0) in tiles if kd == "p" and b0 <= t0 < b0 + nb}
                                while not need <= prog:
                                    next(s2)
                                for _ in gate_feat(hT, hTk, [(b0, nb)], slot, k_, 3, 6 + 3 * hp, sg, "silu"):
                                    try:
                                        next(s2)
                                    except StopIteration:
                                        pass
                            for _ in s2:
                                pass

                        for hp in range(2):
                            gslot, gk_ = wget()
                            wprefetch()
                            gate_with(hp, gslot, gk_)
                    K.barrier()

        with ExitStack() as g1:
            hT = sb(g1, "hT", [128, 16, NMAIN], BF16)

            def norm_scope(src_list, wcol, wcolk, dstT, dstk):
                with ExitStack() as ns:
                    xsl = [sb(ns, f"xsl{i}", [128, D], F32) for i in range(4)]
                    xsb_ = [sb(ns, f"xsb{i}", [128, D], BF16) for i in range(3)]
                    jnk = sb(ns, "jnk", [128, D], BF16)
                    for _ in norm_gen((xsl, xsb_, jnk), src_list, dstT, dstk, wcol, wcolk):
                        pass
                    K.barrier()

            wissue_upto(1)
            with ExitStack() as cs:
                Wf = sb(cs, "Wf", [128, 4, 128], F32)
                Wsf = sb(cs, "Wsf", [128, 4, 128], F32)
                Wmb = sb(cs, "Wmb", [128, 4, 128], BF16)
                Wsmb = sb(cs, "Wsmb", [128, 4, 128], BF16)
                K.op("pool", lambda e: e.memset(Wsf[:], 0.0), writes=["Wsf"])
                ck = dict(semkey="const")
                K.dma("sp", normw_col[:], dr["norm_w"].rearrange("o (k p) -> p (o k)", p=128), writes=["normw_col"],
                      allow_slow_non_contiguous=True, **ck)
                K.dma("sp", memw_col[:], dr["mem_norm_w"].rearrange("o (k p) -> p (o k)", p=128), writes=["memw_col"],
                      allow_slow_non_contiguous=True, **ck)
                if stop == 'c2':
                    K.barrier()
                    return nc
                K.dma("sp", wo_b[:], dr["b_onorm_w"].partition_broadcast(128), writes=["wo_b"], **ck)
                K.dma("sp", abT[:], dr["a_bs"].rearrange("h i -> i h"), writes=["abT"], allow_slow_non_contiguous=True, **ck)
                K.dma("sp", bwa[0:16, :], dr["b_wa"], writes=["bwa"], **ck)
                K.dma("sp", bwa[16:17, :], dr["b_ba"], writes=["bwa"], part=True, **ck)
                if stop == 'c3':
                    K.barrier()
                    return nc
                K.dma("sp", Wf[:], dr["a_ws"].rearrange("h i j -> i h j"), writes=["Wf"], **ck)
                for s in range(16):
                    K.dma(["sp", "act"][s % 2], abTs[8 * s:8 * s + 8, :], dr["a_bs"][:, 0:8].rearrange("h i -> i h"), writes=["abTs"],
                          part=(s > 0), allow_slow_non_contiguous=True, **ck)
                    K.dma(["act", "sp"][s % 2], Wsf[8 * s:8 * s + 8, :, 8 * s:8 * s + 8], dr["a_ws"][:, 0:8, 0:8].rearrange("h i j -> i h j"),
                          writes=["Wsf"], part=(s > 0), **ck)
                if stop == 'c4':
                    K.barrier()
                    return nc
                K.seal("const")
                for src, srck, dst, dstk in ((Wf, "Wf", Wmb, "Wmb"), (Wsf, "Wsf", Wsmb, "Wsmb")):
                    K.op("pool", lambda e, src=src: e.affine_select(out=src[:], in_=src[:], pattern=[[0, 4], [-1, 128]],
                                                                    compare_op=ALU.is_ge, fill=0.0, base=0, channel_multiplier=1),
                         reads=[srck], writes=[srck])
                    K.op("pool", lambda e, src=src, dst=dst: e.tensor_copy(out=dst[:], in_=src[:]), reads=[srck], writes=[dstk])
                if stop == 'c5':
                    K.barrier()
                    return nc
                for src, srck, dst, dstk, tb in ((Wmb, "Wmb", WT, "WT", 0), (Wsmb, "Wsmb", WTs, "WTs", 1)):
                    for h in range(4):
                        K.op("pe", lambda e, src=src, h=h, tb=tb: e.transpose(out=T[tb][:, h * 128:(h + 1) * 128], in_=src[:, h, :],
                                                                             identity=ident[:]),
                             reads=[srck, "ident"], writes=[TK[tb]], signal=(h == 3))
                    K.op("dve", lambda e, dst=dst, tb=tb: e.tensor_copy(out=dst[:].rearrange("p h i -> p (h i)"), in_=T[tb][:, 0:512]),
                         reads=[TK[tb]], writes=[dstk])
                norm_scope([(dr["xpre"], 8, 0)], normw_col, "normw_col", branchT, "hTp")

            if stop == 'prenorm':
                K.barrier()
                return nc
            with ExitStack() as sn:
                xsl_s = [sb(sn, f"xsl{i}", [128, D], F32) for i in range(3)]
                xsb_s = [sb(sn, f"xsb{i}", [128, D], BF16) for i in range(2)]
                side = norm_gen((xsl_s, xsb_s, None), [(dr["xp"], 8, 0), (dr["xs"], 1, 1024)], hT, "hT", normw_col, "normw_col")
                gla_phase(branchT, "hTp", NPRE, [("p", t * 128) for t in range(8)], with_out=False, side=side)
                for _ in side:
                    pass
                K.barrier()

            if stop == 'pregla':
                K.barrier()
                return nc

            with ExitStack() as ms:
                hmT = sb(ms, "hmT", [128, 16, 256], BF16)
                stg = [sb(ms, f"stg{i}", [128, 256], F32) for i in range(2)]
                norm_scope([(dr["mem"], 2, 0)], memw_col, "memw_col", hmT, "hmT")
                if stop == 'm1':
                    K.barrier()
                    return nc
                si = 0
                for j in range(4):
                    wslot, wk = wget()
                    wprefetch()
                    isK = j < 2
                    jj = j % 2
                    for t in range(2):
                        pi = (2 * j + t) % 2
                        for k in range(16):
                            K.op("pe", lambda e, k=k, t=t, pi=pi: e.matmul(P[pi][:, 0:256], lhsT=hmT[:, k, t * 128:(t + 1) * 128],
                                                                          rhs=wslot[:, k, 0:256], start=(k == 0), stop=(k == 15)),
                                 reads=["hmT", wk], writes=[PK[pi]], signal=(k == 15))
                        sgi = si % 2
                        si += 1
                        K.op("act", lambda e, pi=pi, sgi=sgi: e.copy(out=stg[sgi][:], in_=P[pi][:, 0:256]), reads=[PK[pi]],
                             writes=[f"stg{sgi}"])
                        if True:
                            K.dma("sp", dr["mk" if isK else "mv"][t * 128:(t + 1) * 128, jj * 256:(jj + 1) * 256], stg[sgi][:],
                                  reads=[f"stg{sgi}"])
                        if not isK:
                            K.op("dve", lambda e, pi=pi, t=t, jj=jj: e.tensor_copy(out=Vb[:, t, jj * 256:(jj + 1) * 256], in_=P[pi][:, 0:256]),
                                 reads=[PK[pi]], writes=["Vb"])
                    if stop == 'm2':
                        K.barrier()
                        return nc
                    if isK:
                        for hh in range(2):
                            pi = 2 + hh
                            proj_feat(hmT, "hmT", 0, 256, wslot, wk, hh * 128, 128, P[pi], PK[pi])
                            K.op("dve", lambda e, pi=pi, hh=hh, jj=jj: e.tensor_copy(out=KT[:, 2 * jj + hh, :], in_=P[pi][:, 0:256]),
                                 reads=[PK[pi]], writes=["KT"])
                    if stop == f"mj{j}":
                        K.barrier()
                        return nc
                K.barrier()

            if stop == 'memkv':
                K.barrier()
                return nc


            if stop == 'mainnorm':
                K.barrier()
                return nc
            main_tiles = [("s", 1024)] + [("p", t * 128) for t in range(8)]
            gla_phase(hT, "hT", NMAIN, main_tiles, with_out=True)

            if stop == 'maingla':
                K.barrier()
                return nc

            mblocks = [(0, 512), (512, 512), (1024, 128)]
            late = g1.enter_context(ExitStack())
            wout = sb(late, "wout", [128, 16, D], BF16)

            with ExitStack() as as_:
                wv_b = sb(as_, "wv_b", [128, 768], F32)
                vnb = [sb(as_, f"vnb{i}", [128, 768], BF16) for i in range(2)]
                vnf = sb(as_, "vnf", [128, 768], F32)
                mixb = [sb(as_, f"mixb{i}", [128, 768], BF16) for i in range(2)]
                sgA = [sb(as_, f"sg{i}", [128, 512], BF16) for i in range(4)]
                K.dma("sp", wv_b[:], dr["a_vnorm_w"].partition_broadcast(128), writes=["wv_b"])
                (w0, w0k), (w1, w1k) = wget(2)
                for q in range(4):
                    K.dma("pool", wout[:, :, q * 512:(q + 1) * 512], wout_v[:, :, q * 512:(q + 1) * 512], writes=[f"wout{q}"])
                NTm = len(main_tiles)

                def a_proj_pe(ci):
                    kind, t0 = main_tiles[ci]
                    pv = [0, 1] if ci % 2 == 0 else [4, 5]
                    proj_tok(hT, "hT", t0, w0, w0k, 384, P[pv[0]], PK[pv[0]])
                    proj_tok(hT, "hT", t0, w1, w1k, 384, P[pv[1]], PK[pv[1]])

                def a_proj_post(ci):
                    kind, t0 = main_tiles[ci]
                    pv = [0, 1] if ci % 2 == 0 else [4, 5]
                    vi = ci % 2
                    sti, stk = st[ci % 2], f"st{ci % 2}"
                    for half in range(2):
                        K.op("act", lambda e, half=half: e.activation(
                            out=sgA[vi][:, 0:384], in_=P[pv[half]][:, 0:384], func=AF.Square, accum_out=sti[:, half:half + 1]),
                             reads=[PK[pv[half]]], writes=[f"sg{vi}", stk])
                    rstd_from_ss(sti, stk, 2, 768.0)
                    for half in range(2):
                        if kind == "s":
                            K.op("dve", lambda e, half=half: e.scalar_tensor_tensor(
                                out=vnf[:, half * 384:(half + 1) * 384], in0=P[pv[half]][:, 0:384], scalar=sti[:, 6:7],
                                in1=wv_b[:, half * 384:(half + 1) * 384], op0=ALU.mult, op1=ALU.mult),
                                 reads=[PK[pv[half]], stk, "wv_b"], writes=["vnf"])
                            K.op("act", lambda e, half=half: e.copy(out=vnb[vi][:, half * 384:(half + 1) * 384],
                                                                    in_=vnf[:, half * 384:(half + 1) * 384]),
                                 reads=["vnf"], writes=[f"vnb{vi}"])
                        else:
                            K.op("dve", lambda e, half=half: e.scalar_tensor_tensor(
                                out=vnb[vi][:, half * 384:(half + 1) * 384], in0=P[pv[half]][:, 0:384], scalar=sti[:, 6:7],
                                in1=wv_b[:, half * 384:(half + 1) * 384], op0=ALU.mult, op1=ALU.mult),
                                 reads=[PK[pv[half]], stk, "wv_b"], writes=[f"vnb{vi}"], nodrain=(half == 1))
                    if kind == "s":
                        K.dma("sp", dr["cvs"][:, :], vnf[:], reads=["vnf"])

                def a_mix_pe(ci):
                    kind, t0 = main_tiles[ci]
                    vi = ci % 2
                    Wm, Wmk = (WT, "WT") if kind == "p" else (WTs, "WTs")
                    for h in range(4):
                        pb = 2 + h // 2
                        K.op("pe", lambda e, h=h, pb=pb: e.matmul(P[pb][:, (h % 2) * 192:(h % 2 + 1) * 192], lhsT=Wm[:, h, :],
                                                                 rhs=vnb[vi][:, h * 192:(h + 1) * 192], start=True, stop=True),
                             reads=[Wmk, f"vnb{vi}"], writes=[PK[pb]], signal=(h % 2 == 1))

                def a_mix_post(ci):
                    kind, t0 = main_tiles[ci]
                    vi = ci % 2
                    ab = abT if kind == "p" else abTs
                    for h in range(4):
                        pb = 2 + h // 2
                        K.op("dve", lambda e, h=h, pb=pb: e.tensor_scalar(
                            out=mixb[vi][:, h * 192:(h + 1) * 192], in0=P[pb][:, (h % 2) * 192:(h % 2 + 1) * 192],
                            scalar1=ab[:, h:h + 1], scalar2=None, op0=ALU.add),
                             reads=[PK[pb], "abT", "abTs"], writes=[f"mixb{vi}"], nodrain=(h > 0))

                def a_tr(ci):
                    kind, t0 = main_tiles[ci]
                    transpose_to_branch(mixb[ci % 2], f"mixb{ci % 2}", 6, 0, t0, eng="act", tb=ci % 2)

                for step in range(NTm + 2):
                    if step < NTm:
                        a_proj_pe(step)
                        a_proj_post(step)
                    if 0 <= step - 1 < NTm:
                        a_mix_pe(step - 1)
                        a_mix_post(step - 1)
                    if 0 <= step - 2 < NTm:
                        a_tr(step - 2)
                wdone()
                wprefetch()
                for j in range(2):
                    wslot, wk = wget()
                    wprefetch()
                    for _ in gate_feat(hT, "hT", mblocks, wslot, wk, 3, 3 * j, sgA, "mul", nbank=4):
                        pass
                for j in range(2):
                    wslot, wk = wget()
                    wprefetch()
                    for _ in gate_feat(hT, "hT", mblocks, wslot, wk, 3, 3 * j, sgA, "silu", nbank=4):
                        pass
                K.barrier()

            with ExitStack() as xs_:
                qxT = sb(xs_, "qxT", [128, 4, NMAIN], BF16)
                pT = [sb(xs_, f"pT{i}", [128, 2, 512], BF16) for i in range(2)]
                rinv = sb(xs_, "rinv", [128, 512], F32)
                rprod = sb(xs_, "rprod", [128, 512], BF16)
                sgX = [sb(xs_, f"sg{i}", [128, 512], BF16) for i in range(2)]
                ckb = [sb(xs_, f"ckb{i}", [128, 2, 512], BF16) for i in range(2)]
                cvb = [sb(xs_, f"cvb{i}", [128, 2, 512], BF16) for i in range(3)]
                KTs = [sb(xs_, f"KTs{i}", [128, 4, 256], BF16) for i in range(2)]
                pTs = pT[0]
                scale = 128.0 ** -0.5
                t0 = 1024
                qslots = wget(2)
                qcnt = [0]

                def q_step(j, b0, nb, hh):
                    h = 2 * j + hh
                    pi = qcnt[0] % 2
                    qcnt[0] += 1
                    proj_feat(hT, "hT", b0, nb, qslots[j][0], qslots[j][1], hh * 128, 128, P[pi], PK[pi])
                    K.op("act", lambda e: e.copy(out=qxT[:, h, b0:b0 + nb], in_=P[pi][:, 0:nb]),
                         reads=[PK[pi]], writes=[f"qxT{h}_{b0}"])

                def q_prompt_gen():
                    for j in range(2):
                        for (b0, nb) in mblocks[:2]:
                            for hh in range(2):
                                q_step(j, b0, nb, hh)
                                yield

                its = [(b0, nb, h) for (b0, nb) in mblocks[:2] for h in range(4)]

                def x_scores(i):
                    b0, nb, h = its[i]
                    for c in range(2):
                        K.op("pe", lambda e, c=c: e.matmul(P[4 + c][:, 0:nb], lhsT=KT[:, h, c * 128:(c + 1) * 128],
                                                          rhs=qxT[:, h, b0:b0 + nb], start=True, stop=True),
                             reads=["KT", f"qxT{h}_{b0}"], writes=[PK[4 + c]])

                def x_exp(i):
                    b0, nb, h = its[i]
                    pi = i % 2
                    for c in range(2):
                        K.op("act", lambda e, c=c: e.activation(out=pT[pi][:, c, 0:nb], in_=P[4 + c][:, 0:nb], func=AF.Exp, scale=scale),
                             reads=[PK[4 + c]], writes=[f"pT{pi}"])

                def x_pv(i):
                    b0, nb, h = its[i]
                    pi = i % 2
                    for c in range(2):
                        K.op("pe", lambda e, c=c: e.matmul(P[2][:, 0:nb], lhsT=Vb[:, c, h * 128:(h + 1) * 128],
                                                          rhs=pT[pi][:, c, 0:nb], start=(c == 0), stop=(c == 1)),
                             reads=["Vb", f"pT{pi}"], writes=[PK[2]], signal=(c == 1))
                    for c in range(2):
                        K.op("pe", lambda e, c=c: e.matmul(P[3][:, 0:nb], lhsT=ones_b[:], rhs=pT[pi][:, c, 0:nb],
                                                          start=(c == 0), stop=(c == 1)),
                             reads=["ones_b", f"pT{pi}"], writes=[PK[3]], signal=(c == 1))

                def x_fin(i):
                    b0, nb, h = its[i]
                    K.op("dve", lambda e: e.reciprocal(out=rinv[:, 0:nb], in_=P[3][:, 0:nb]), reads=[PK[3]], writes=["rinv"])
                    K.op("dve", lambda e: e.tensor_tensor(out=branchT[:, 12 + h, b0:b0 + nb], in0=P[2][:, 0:nb], in1=rinv[:, 0:nb],
                                                          op=ALU.mult),
                         reads=[PK[2], "rinv"], writes=bkeys(12 + h, 1, b0, nb))

                xp_prog = set()

                def xp_gen():
                    for step in range(len(its) + 1):
                        if step < len(its):
                            x_scores(step)
                        if step >= 1:
                            x_pv(step - 1)
                        if step < len(its):
                            x_exp(step)
                        if step >= 1:
                            x_fin(step - 1)
                            xp_prog.add((its[step - 1][0], its[step - 1][2]))
                        yield

                def xs_load(s):
                    K.dma("pool", ckb[s % 2][:], dr["ck"][s].rearrange("(c p) f -> p c f", p=128), writes=[f"ckb{s % 2}"])
                    K.dma("pool", cvb[s % 3][:], dr["cv"][s].rearrange("(c p) f -> p c f", p=128), writes=[f"cvb{s % 3}"])

                def xs_tr(s):
                    si_ = s % 2
                    for h in range(4):
                        for c in range(2):
                            K.op("pe", lambda e, h=h, c=c: e.transpose(out=T[si_][:, (h * 2 + c) * 128:(h * 2 + c + 1) * 128],
                                                                       in_=ckb[s % 2][:, c, h * 128:(h + 1) * 128], identity=ident[:]),
                                 reads=[f"ckb{s % 2}", "ident"], writes=[TK[si_]], signal=(h == 3 and c == 1))

                def xs_tr_post(s):
                    si_ = s % 2
                    K.op("dve", lambda e: e.tensor_copy(out=KTs[si_][:].rearrange("p h n -> p (h n)"), in_=T[si_][:, 0:1024]),
                         reads=[TK[si_]], writes=[f"KTs{si_}"])

                def xs_scores(s):
                    si_ = s % 2
                    for h in range(4):
                        for c in range(2):
                            K.op("pe", lambda e, h=h, c=c: e.matmul(
                                P[4 + si_][:, c * 32 + h * 8:c * 32 + h * 8 + 8], lhsT=KTs[si_][:, h, c * 128:(c + 1) * 128],
                                rhs=qxT[:, h, t0 + 8 * s:t0 + 8 * s + 8], start=True, stop=True),
                                 reads=[f"KTs{si_}", f"qxT{h}_{t0}"], writes=[PK[4 + si_]], signal=(h == 3 and c == 1))

                def xs_exp(s):
                    si_ = s % 2
                    K.op("act", lambda e: e.activation(out=pTs[:, :, s * 32:(s + 1) * 32],
                                                       in_=P[4 + si_][:, 0:64].rearrange("p (c x) -> p c x", c=2), func=AF.Exp, scale=scale),
                         reads=[PK[4 + si_]], writes=[f"pTs{s}", "pT0"])

                def xs_pv(s):
                    for h in range(4):
                        for c in range(2):
                            K.op("pe", lambda e, h=h, c=c: e.matmul(
                                P[2][:, s * 32 + h * 8:s * 32 + h * 8 + 8], lhsT=cvb[s % 3][:, c, h * 128:(h + 1) * 128],
                                rhs=pTs[:, c, s * 32 + h * 8:s * 32 + h * 8 + 8], start=(c == 0), stop=(c == 1)),
                                 reads=[f"cvb{s % 3}", f"pTs{s}"], writes=[PK[2]], signal=(h == 3 and c == 1))

                def xs_gen():
                    xs_load(0)
                    for step in range(16 + 2):
                        if step < 16:
                            xs_tr(step)
                        if 0 <= step - 1 < 16:
                            xs_scores(step - 1)
                        if 0 <= step - 2 < 16:
                            xs_pv(step - 2)
                        if step + 1 < 16:
                            xs_load(step + 1)
                        if step < 16:
                            xs_tr_post(step)
                        if 0 <= step - 1 < 16:
                            xs_exp(step - 1)
                        yield
                    for c in range(2):
                        K.op("pe", lambda e, c=c: e.matmul(P[3][:, 0:512], lhsT=ones_b[:], rhs=pTs[:, c, :], start=(c == 0), stop=(c == 1)),
                             reads=["ones_b"] + [f"pTs{s}" for s in range(16)], writes=[PK[3]], signal=(c == 1))
                    K.op("dve", lambda e: e.reciprocal(out=rinv[:], in_=P[3][:, 0:512]), reads=[PK[3]], writes=["rinv"])
                    K.op("dve", lambda e: e.tensor_tensor(out=rprod[:], in0=P[2][:, 0:512], in1=rinv[:], op=ALU.mult),
                         reads=[PK[2], "rinv"], writes=["rprod"])
                    for h in range(4):
                        K.op("dve", lambda e, h=h: e.tensor_copy(
                            out=branchT[:, 12 + h, t0:t0 + 128].rearrange("p (s i) -> p s i", i=8),
                            in_=rprod[:].rearrange("p (s h i) -> p s h i", s=16, h=4)[:, :, h, :]),
                             reads=["rprod"], writes=bkeys(12 + h, 1, t0, 128))
                    yield

                def runx(*specs):
                    specs = [[g, n] for (g, n) in specs]
                    while specs:
                        for sp_ in list(specs):
                            g, n = sp_
                            for _ in range(n):
                                try:
                                    next(g)
                                except StopIteration:
                                    specs.remove(sp_)
                                    break

                for j in range(2):
                    for hh in range(2):
                        q_step(j, t0, 128, hh)
                runx((q_prompt_gen(), 1), (xs_gen(), 2))
                wdone()

                xp = xp_gen()

                def gate_with_xp(j, slot, k_):
                    for (b0, nb) in [mblocks[2]] + mblocks[:2]:
                        need = {(b0, 2 * j + hh) for hh in range(2)} if b0 < 1024 else set()
                        while not need <= xp_prog:
                            next(xp)
                        for _ in gate_feat(hT, "hT", [(b0, nb)], slot, k_, 2, 12 + 2 * j, sgX, "silu"):
                            try:
                                next(xp)
                            except StopIteration:
                                pass

                for j in range(2):
                    wslot, wk = wget()
                    wprefetch()
                    gate_with_xp(j, wslot, wk)
                for _ in xp:
                    pass
                K.barrier()

            with ExitStack() as os_:
                wf_b = sb(os_, "wf_b", [128, D], F32)
                xsl = [sb(os_, "xsl0", [128, D], F32)] * 2
                ysl = [sb(os_, f"ysl{i}", [128, D], F32) for i in range(2)]
                K.dma("sp", wf_b[:], dr["final_norm_w"].partition_broadcast(128), writes=["wf_b"])
                for t in range(9):
                    s = t % 2
                    src = dr["xp"][t * 128:(t + 1) * 128, :] if t < 8 else dr["xs"][:, :]
                    dst = dr["yp"][t * 128:(t + 1) * 128, :] if t < 8 else dr["ys"][:, :]
                    xk, yk, sk = "xsl0", f"ysl{s}", f"st{s}"
                    K.dma("sp", xsl[s][:], src, writes=[xk])
                    for q in range(4):
                        pi = (t * 4 + q) % 6
                        for k in range(16):
                            K.op("pe", lambda e, k=k, q=q, pi=pi: e.matmul(P[pi][:, 0:512], lhsT=branchT[:, k, t * 128:(t + 1) * 128],
                                                                          rhs=wout[:, k, q * 512:(q + 1) * 512], start=(k == 0), stop=(k == 15)),
                                 reads=[f"bT{k}_{t}", f"wout{q}"], writes=[PK[pi]], signal=(k == 15))
                        K.op("dve", lambda e, q=q, pi=pi, s=s: e.tensor_tensor(out=ysl[s][:, q * 512:(q + 1) * 512], in0=P[pi][:, 0:512],
                                                                              in1=xsl[s][:, q * 512:(q + 1) * 512], op=ALU.add),
                             reads=[PK[pi], xk], writes=[yk])
                    K.op("act", lambda e, s=s: e.activation(out=xsl[s][:], in_=ysl[s][:], func=AF.Square, accum_out=st[s][:, 0:1]),
                         reads=[yk], writes=[xk, sk])
                    rstd_from_ss(st[s], sk, 1, float(D))
                    K.op("dve", lambda e, s=s: e.scalar_tensor_tensor(out=ysl[s][:], in0=ysl[s][:], scalar=st[s][:, 6:7], in1=wf_b[:],
                                                                      op0=ALU.mult, op1=ALU.mult),
                         reads=[yk, sk, "wf_b"], writes=[yk])
                    K.dma("sp", dst, ysl[s][:], reads=[yk])
                K.barrier()
    return nc


_NC_CACHE = {}


def kernel(x_prompt, x_sample, mem_prompt, state_gla, cache_mem_k, cache_mem_v,
           norm_w, w_in, a_vnorm_w, a_ws, a_bs, b_wa, b_ba, b_onorm_w,
           mem_norm_w, w_mem_kv, w_out, final_norm_w):
    f = lambda a: np.ascontiguousarray(np.asarray(a, dtype=np.float32))
    x_prompt, x_sample, mem_prompt = f(x_prompt), f(x_sample), f(mem_prompt)
    state_gla, cache_mem_k, cache_mem_v = f(state_gla), f(cache_mem_k), f(cache_mem_v)
    shared = {
        "norm_w": f(norm_w).reshape(1, D), "w_in": f(w_in).reshape(D, DIN), "a_vnorm_w": f(a_vnorm_w).reshape(1, 768),
        "a_ws": f(a_ws).reshape(4, 128, 128), "a_bs": f(a_bs).reshape(4, 128), "b_wa": f(b_wa).reshape(16, 384),
        "b_ba": f(b_ba).reshape(1, 384), "b_onorm_w": f(b_onorm_w).reshape(1, 192), "mem_norm_w": f(mem_norm_w).reshape(1, D),
        "w_mem_kv": f(w_mem_kv).reshape(D, 1024), "w_out": f(w_out).reshape(D, D), "final_norm_w": f(final_norm_w).reshape(1, D),
    }
    zeros_pre = np.zeros((1024, D), np.float32)
    in_maps = []
    for c in range(8):
        b, half = c // 2, c % 2
        m = dict(shared)
        m["xp"] = np.ascontiguousarray(x_prompt[b, half * 1024:(half + 1) * 1024])
        m["xpre"] = np.ascontiguousarray(x_prompt[b, 0:1024]) if half == 1 else zeros_pre
        m["xs"] = np.ascontiguousarray(x_sample[16 * c:16 * c + 16].reshape(128, D))
        m["mem"] = np.ascontiguousarray(mem_prompt[b])
        m["s0"] = np.ascontiguousarray(state_gla[0, 16 * c:16 * c + 16])
        m["ck"] = np.ascontiguousarray(cache_mem_k[0, 16 * c:16 * c + 16].reshape(16, 256, 512))
        m["cv"] = np.ascontiguousarray(cache_mem_v[0, 16 * c:16 * c + 16].reshape(16, 256, 512))
        in_maps.append(m)
    if "nc" not in _NC_CACHE:
        _NC_CACHE["nc"] = build_program()
    res = run_bass_kernel_spmd(_NC_CACHE["nc"], in_maps, core_ids=list(range(8)))
    R = res.results
    y_prompt = np.zeros((4, 2048, D), np.float32)
    y_sample = np.zeros((128, 8, D), np.float32)
    mem_k = np.zeros((1, 4, 256, 4, 128), np.float32)
    mem_v = np.zeros((1, 4, 256, 4, 128), np.float32)
    st_p = np.zeros((1, 4, 4, 96, 192), np.float32)
    st_s = np.zeros((1, 128, 4, 96, 192), np.float32)
    cv_s = np.zeros((1, 128, 8, 768), np.float32)
    for c in range(8):
        b, half = c // 2, c % 2
        r = R[c]
        y_prompt[b, half * 1024:(half + 1) * 1024] = r["yp"]
        y_sample[16 * c:16 * c + 16] = r["ys"].reshape(16, 8, D)
        if half == 0:
            mem_k[0, b] = r["mk"].reshape(256, 4, 128)
            mem_v[0, b] = r["mv"].reshape(256, 4, 128)
        else:
            st_p[0, b] = r["sp"]
        st_s[0, 16 * c:16 * c + 16] = r["ss"]
        cv_s[0, 16 * c:16 * c + 16] = r["cvs"].reshape(16, 8, 768)
    return (y_prompt, y_sample, mem_k, mem_v, st_p, st_s, cv_s)
```
